# Optimizing a Trainium2 kernel written in Bass

```python
import jax, jax.numpy as jnp
from jax import lax
import numpy as np

D_MODEL = 4096
BATCH = 4
SEQ = 2048
DEPTH = 1

GRID_W = 64

NA_HEADS = 16
NA_HEAD_DIM = 128
NA_KH_MAX = 8
NA_KW = 16
NA_QBLOCK_W = 16
NA_KBLOCK_W = 32

MLA_HEADS = 16
MLA_Q_RANK = 1024
MLA_KV_RANK = 512
MLA_NOPE_DIM = 128
MLA_ROPE_DIM = 64
MLA_V_DIM = 128
ROPE_THETA = 10000.0
Q_BLOCK = 128

MEM_LEN = 256
MEM_HEADS = 4
MEM_HEAD_DIM = 128

D_FF = 256 * ((8 * D_MODEL // 3 + 255) // 256)
FFN_RES_WEIGHT = 0.5

NORM_EPS = 1e-6
NEG_INF = -1e30

NA_WIDTH = NA_HEADS * NA_HEAD_DIM
MLA_WIDTH = MLA_HEADS * MLA_V_DIM
MIX_WIDTH = NA_WIDTH + MLA_WIDTH
IN_PROJ_WIDTH = 3 * NA_WIDTH + MLA_Q_RANK + MLA_KV_RANK + MLA_ROPE_DIM
MEM_WIDTH = MEM_HEADS * MEM_HEAD_DIM

kernel_name = "hybrid_natten_mla_macaron_layer"


def rms_norm(x, g):
    xf = x.astype(jnp.float32)
    y = xf * lax.rsqrt(jnp.mean(xf * xf, axis=-1, keepdims=True) + NORM_EPS)
    return (y * g.astype(jnp.float32)).astype(x.dtype)


def swiglu(x, w_gate, w_up, w_down):
    return (jax.nn.silu(x @ w_gate) * (x @ w_up)) @ w_down


def rope(x, positions):
    half = MLA_ROPE_DIM // 2
    inv_freq = 1.0 / (ROPE_THETA ** (jnp.arange(half, dtype=jnp.float32) / half))
    ang = positions.astype(jnp.float32)[..., None] * inv_freq
    ang = ang.reshape(ang.shape[:2] + (1,) * (x.ndim - 3) + (half,))
    cos, sin = jnp.cos(ang), jnp.sin(ang)
    xf = x.astype(jnp.float32)
    x1, x2 = xf[..., :half], xf[..., half:]
    out = jnp.concatenate([x1 * cos - x2 * sin, x2 * cos + x1 * sin], axis=-1)
    return out.astype(x.dtype)


def neighborhood_attention(q, k, v, rpb):
    B, S, H, Dh = q.shape
    rows = S // GRID_W
    kh = min(NA_KH_MAX, rows)
    n_cb = GRID_W // NA_QBLOCK_W
    scale = Dh ** -0.5

    qg = jnp.moveaxis(q.reshape(B, rows, n_cb, NA_QBLOCK_W, H, Dh), 1, 0)
    kg = k.reshape(B, rows, GRID_W, H, Dh)
    vg = v.reshape(B, rows, GRID_W, H, Dh)

    qcol = np.arange(GRID_W).reshape(n_cb, NA_QBLOCK_W)
    cstart = np.clip(qcol - NA_KW // 2, 0, GRID_W - NA_KW)
    kb_start = np.clip(np.arange(n_cb) * NA_QBLOCK_W - NA_KW // 2, 0, GRID_W - NA_KBLOCK_W)
    kcol = kb_start[:, None] + np.arange(NA_KBLOCK_W)
    col_in = (kcol[:, None, :] >= cstart[:, :, None]) & (kcol[:, None, :] < cstart[:, :, None] + NA_KW)
    col_idx = np.clip(kcol[:, None, :] - qcol[:, :, None] + NA_KW - 1, 0, 2 * NA_KW - 2)
    mask = np.broadcast_to(col_in[:, :, None, :], (n_cb, NA_QBLOCK_W, kh, NA_KBLOCK_W))
    mask = jnp.asarray(mask.reshape(n_cb, NA_QBLOCK_W, kh * NA_KBLOCK_W))

    def row_block(args):
        r, q_r = args
        rs = jnp.clip(r - kh // 2, 0, rows - kh)
        k_rows = lax.dynamic_slice_in_dim(kg, rs, kh, axis=1)
        v_rows = lax.dynamic_slice_in_dim(vg, rs, kh, axis=1)
        k_blk = jnp.transpose(k_rows[:, :, kcol], (0, 2, 1, 3, 4, 5)).reshape(B, n_cb, kh * NA_KBLOCK_W, H, Dh)
        v_blk = jnp.transpose(v_rows[:, :, kcol], (0, 2, 1, 3, 4, 5)).reshape(B, n_cb, kh * NA_KBLOCK_W, H, Dh)
        s = jnp.einsum('bjqhd,bjkhd->bhjqk', q_r, k_blk).astype(jnp.float32) * scale
        row_idx = rs + jnp.arange(kh) - r + NA_KH_MAX - 1
        bias = rpb[:, row_idx][:, :, col_idx]
        bias = jnp.transpose(bias, (0, 2, 3, 1, 4)).reshape(H, n_cb, NA_QBLOCK_W, kh * NA_KBLOCK_W)
        s = jnp.where(mask, s + bias.astype(jnp.float32), NEG_INF)
        p = jax.nn.softmax(s, axis=-1).astype(v.dtype)
        o = jnp.einsum('bhjqk,bjkhd->bjqhd', p, v_blk)
        return o.reshape(B, GRID_W, H, Dh)

    out = lax.map(row_block, (jnp.arange(rows), qg))
    return jnp.moveaxis(out, 0, 1).reshape(B, S, H * Dh)


def latent_attention(c_q, c_kv, k_rope_in, positions, g_q_a, w_q_b, g_kv_a, w_kv_b):
    B, S, _ = c_q.shape
    H = MLA_HEADS
    q = (rms_norm(c_q, g_q_a) @ w_q_b).reshape(B, S, H, MLA_NOPE_DIM + MLA_ROPE_DIM)
    q_nope, q_pe = q[..., :MLA_NOPE_DIM], rope(q[..., MLA_NOPE_DIM:], positions)
    kv = (rms_norm(c_kv, g_kv_a) @ w_kv_b).reshape(B, S, H, MLA_NOPE_DIM + MLA_V_DIM)
    k_nope, v = kv[..., :MLA_NOPE_DIM], kv[..., MLA_NOPE_DIM:]
    k_pe = rope(k_rope_in, positions)
    scale = (MLA_NOPE_DIM + MLA_ROPE_DIM) ** -0.5
    nb = S // Q_BLOCK
    qn_b = jnp.moveaxis(q_nope.reshape(B, nb, Q_BLOCK, H, MLA_NOPE_DIM), 1, 0)
    qp_b = jnp.moveaxis(q_pe.reshape(B, nb, Q_BLOCK, H, MLA_ROPE_DIM), 1, 0)

    def q_block(args):
        qn, qp = args
        s = (jnp.einsum('bqhd,bkhd->bhqk', qn, k_nope)
             + jnp.einsum('bqhr,bkr->bhqk', qp, k_pe)).astype(jnp.float32) * scale
        p = jax.nn.softmax(s, axis=-1).astype(v.dtype)
        return jnp.einsum('bhqk,bkhd->bqhd', p, v)

    out = lax.map(q_block, (qn_b, qp_b))
    return jnp.moveaxis(out, 0, 1).reshape(B, S, MLA_WIDTH)


def memory_cross_attention(h, mem_n, w_q, w_kv, w_o):
    B, S, _ = h.shape
    M = mem_n.shape[1]
    q = (h @ w_q).reshape(B, S, MEM_HEADS, MEM_HEAD_DIM)
    kv = (mem_n @ w_kv).reshape(B, M, MEM_HEADS, 2 * MEM_HEAD_DIM)
    k, v = kv[..., :MEM_HEAD_DIM], kv[..., MEM_HEAD_DIM:]
    s = jnp.einsum('bqhd,bkhd->bhqk', q, k).astype(jnp.float32) * (MEM_HEAD_DIM ** -0.5)
    p = jax.nn.softmax(s, axis=-1).astype(v.dtype)
    o = jnp.einsum('bhqk,bkhd->bqhd', p, v).reshape(B, S, MEM_WIDTH)
    return o @ w_o


def setup_inputs(seed: int = 0) -> dict:
    key = jax.random.key(seed)
    ks = jax.random.split(key, 32)

    def w(k, shape, fan_in):
        return jax.random.normal(k, shape, jnp.float32) * (fan_in ** -0.5)

    def gain(k, shape):
        return 1.0 + 0.01 * jax.random.normal(k, shape, jnp.float32)

    L, D = DEPTH, D_MODEL
    return {
        "x": jax.random.normal(ks[0], (BATCH, SEQ, D), jnp.float32),
        "mem": jax.random.normal(ks[1], (BATCH, MEM_LEN, D), jnp.float32),
        "positions": jnp.broadcast_to(jnp.arange(SEQ, dtype=jnp.int32), (BATCH, SEQ)),
        "ffn1_w_gate": w(ks[2], (L, D, D_FF), D),
        "ffn1_w_up": w(ks[3], (L, D, D_FF), D),
        "ffn1_w_down": w(ks[4], (L, D_FF, D), D_FF),
        "g_ffn1": gain(ks[5], (L, 2, D)),
        "w_in": w(ks[6], (L, D, IN_PROJ_WIDTH), D),
        "g_q_a": gain(ks[7], (L, MLA_Q_RANK)),
        "w_q_b": w(ks[8], (L, MLA_Q_RANK, MLA_HEADS * (MLA_NOPE_DIM + MLA_ROPE_DIM)), MLA_Q_RANK),
        "g_kv_a": gain(ks[9], (L, MLA_KV_RANK)),
        "w_kv_b": w(ks[10], (L, MLA_KV_RANK, MLA_HEADS * (MLA_NOPE_DIM + MLA_V_DIM)), MLA_KV_RANK),
        "na_rpb": 0.02 * jax.random.normal(ks[11], (L, NA_HEADS, 2 * NA_KH_MAX - 1, 2 * NA_KW - 1), jnp.float32),
        "w_out": w(ks[12], (L, MIX_WIDTH, D), MIX_WIDTH),
        "g_mix": gain(ks[13], (L, 2, D)),
        "g_mem_in": gain(ks[14], (L, D)),
        "w_mem_q": w(ks[15], (L, D, MEM_WIDTH), D),
        "w_mem_kv": w(ks[16], (L, D, 2 * MEM_WIDTH), D),
        "w_mem_o": w(ks[17], (L, MEM_WIDTH, D), MEM_WIDTH),
        "g_mem_attn": gain(ks[18], (L, 2, D)),
        "ffn2_w_gate": w(ks[19], (L, D, D_FF), D),
        "ffn2_w_up": w(ks[20], (L, D, D_FF), D),
        "ffn2_w_down": w(ks[21], (L, D_FF, D), D_FF),
        "g_ffn2": gain(ks[22], (L, 2, D)),
        "g_final": gain(ks[23], (L, D)),
    }


def reference(x, mem, positions,
              ffn1_w_gate, ffn1_w_up, ffn1_w_down, g_ffn1,
              w_in, g_q_a, w_q_b, g_kv_a, w_kv_b, na_rpb, w_out, g_mix,
              g_mem_in, w_mem_q, w_mem_kv, w_mem_o, g_mem_attn,
              ffn2_w_gate, ffn2_w_up, ffn2_w_down, g_ffn2,
              g_final):
    B, S, _ = x.shape
    splits = list(np.cumsum([NA_WIDTH, NA_WIDTH, NA_WIDTH, MLA_Q_RANK, MLA_KV_RANK]))
    h = x
    for l in range(DEPTH):
        f = swiglu(rms_norm(h, g_ffn1[l, 0]), ffn1_w_gate[l], ffn1_w_up[l], ffn1_w_down[l])
        h = h + FFN_RES_WEIGHT * rms_norm(f, g_ffn1[l, 1])

        u = rms_norm(h, g_mix[l, 0])
        q_na, k_na, v_na, c_q, c_kv, k_rope_in = jnp.split(u @ w_in[l], splits, axis=-1)
        hd = (B, S, NA_HEADS, NA_HEAD_DIM)
        o_na = neighborhood_attention(q_na.reshape(hd), k_na.reshape(hd), v_na.reshape(hd), na_rpb[l])
        o_mla = latent_attention(c_q, c_kv, k_rope_in, positions, g_q_a[l], w_q_b[l], g_kv_a[l], w_kv_b[l])
        o = jnp.concatenate([o_na, o_mla], axis=-1) @ w_out[l]
        h = h + rms_norm(o, g_mix[l, 1])

        a = memory_cross_attention(rms_norm(h, g_mem_attn[l, 0]), rms_norm(mem, g_mem_in[l]),
                                   w_mem_q[l], w_mem_kv[l], w_mem_o[l])
        h = h + rms_norm(a, g_mem_attn[l, 1])

        f = swiglu(rms_norm(h, g_ffn2[l, 0]), ffn2_w_gate[l], ffn2_w_up[l], ffn2_w_down[l])
        h = h + FFN_RES_WEIGHT * rms_norm(f, g_ffn2[l, 1])

        h = rms_norm(h, g_final[l])
    return h
```

```python
import contextlib
import numpy as np
import concourse.bass as bass
import concourse.mybir as mybir
from concourse.bass_utils import run_bass_kernel_spmd

F32 = mybir.dt.float32
BF16 = mybir.dt.bfloat16
I32 = mybir.dt.int32
AF = mybir.ActivationFunctionType
ALU = mybir.AluOpType
AX = mybir.AxisListType

D = 4096
DC = 32
T = 1024
NT = 8
S = 2048
FF = 11008
FC = 86
EPS = 1e-6
FPARTS = [15, 15, 14, 14, 14, 14]
NEG = -30000.0
SAME_ENGINE_SYNC = True
DEBUG = False
STOP = None
SCOPES = False

ENGS = ("pe", "act", "dve", "pool", "sp")


class Sem:
    def __init__(self, h):
        self.h = h
        self.n = 0


class Prog:
    def __init__(self, nc, sem_handles):
        self.nc = nc
        self.free_sems = list(sem_handles)
        self.q = {e: [] for e in ENGS}
        self.esem = {e: Sem(self.free_sems.pop()) for e in ENGS}
        self.bank_free = [None] * 8
        self.dma_sems = []
        self.sem_pool = []
        self.alloc_log = []
        self.bsc = None
        self.stage = "init"

    def newsem(self):
        if self.sem_pool:
            s_ = self.sem_pool.pop()
        else:
            s_ = Sem(self.free_sems.pop())
            self.dma_sems.append(s_)
        self.alloc_log.append(s_)
        return s_

    def mark(self):
        return len(self.alloc_log)

    def reset_to(self, m):
        while len(self.alloc_log) > m:
            self.sem_pool.append(self.alloc_log.pop())

    @staticmethod
    def flat(waits):
        out = []
        for w in waits:
            if w is None:
                continue
            if isinstance(w, list):
                out.extend(Prog.flat(w))
            else:
                out.append(w)
        return tuple(out)

    def barrier(self, extra=()):
        ws = [(self.esem[e], self.esem[e].n) for e in ENGS if self.esem[e].n > 0]
        ws += [(s_, s_.n) for s_ in self.dma_sems if s_.n > 0]
        ws += list(extra)
        bs = self.bsc
        return self.op("dve", lambda e: e.tensor_copy(out=bs, in_=bs), ws)

    def op(self, eng, fn, waits=(), signal=True):
        ws = Prog.flat(waits)
        if signal:
            sem = self.esem[eng]
            sem.n += 1
            self.q[eng].append((fn, ws, (sem, 1), self.stage))
            return (sem, sem.n)
        self.q[eng].append((fn, ws, None, self.stage))
        return None

    def dma(self, eng, out, in_, sem, waits=(), **kw):
        ws = Prog.flat(waits)
        sem.n += 16
        self.q[eng].append((lambda e: e.dma_start(out=out, in_=in_, **kw), ws, (sem, 16), self.stage))
        return (sem, sem.n)

    def replay(self, block):
        nc = self.nc
        engobj = {"pe": "tensor", "act": "scalar", "dve": "vector", "pool": "gpsimd", "sp": "sync"}

        def mk(ename):
            lst = self.q[ename]
            own = self.esem[ename]

            def body(e):
                seen = {}
                cur = [None, None]
                for fn, waits, sig, stg in lst:
                    if SCOPES and stg != cur[0]:
                        if cur[1] is not None:
                            cur[1].__exit__(None, None, None)
                        cur[0] = stg
                        cur[1] = nc.named_scope(stg)
                        cur[1].__enter__()
                    for (sem, v) in waits:
                        if sem is own and not SAME_ENGINE_SYNC:
                            continue
                        if seen.get(id(sem), 0) < v:
                            e.wait_ge(sem.h, v)
                            seen[id(sem)] = v
                    if fn is None:
                        continue
                    ins = fn(e)
                    if sig is not None:
                        ins.then_inc(sig[0].h, sig[1])
                if cur[1] is not None:
                    cur[1].__exit__(None, None, None)
            return body

        for ename in ENGS:
            getattr(block, engobj[ename])(mk(ename))


class Ring:
    def __init__(self, P, aps, init=None):
        self.aps = aps
        self.sems = [P.newsem() for _ in aps]
        self.free = [init] * len(aps)
        self.i = 0

    def done(self):
        return [(s_, s_.n) for s_ in self.sems if s_.n > 0]

    def next(self):
        i = self.i
        self.i = (self.i + 1) % len(self.aps)
        return i


def build_program():
    nc = bass.Bass("TRN2", target_bir_lowering=False)

    def din(name, shape, dt=F32):
        return nc.dram_tensor(name, list(shape), dt, kind="ExternalInput").ap()

    def dscr(name, shape, dt=F32):
        kind = "ExternalOutput" if DEBUG else "Internal"
        return nc.dram_tensor(name, list(shape), dt, kind=kind).ap()

    x_d = din("x", [T, D])
    xo_d = din("xo", [T, D])
    mem_d = din("mem", [256, D])
    posall_d = din("pos_all", [1, S], I32)
    gains_d = din("gains", [10, D])
    w1gu_d = din("w1gu", [FC, 128, 2 * DC * 128])
    w1d_d = din("w1d", [8, 128, FC * 512])
    w2gu_d = din("w2gu", [FC, 128, 2 * DC * 128])
    w2d_d = din("w2d", [8, 128, FC * 512])
    wq_d = din("w_q", [16, 128, DC * 128])
    wk_d = din("w_k", [16, 128, DC * 128])
    wv_d = din("w_v", [4, 128, DC * 512])
    wcq_d = din("w_cq", [8, 128, DC * 128])
    wckr_d = din("w_ckr", [5, 128, DC * 128])
    gqa_d = din("g_q_a", [128, 8])
    gkva_d = din("g_kv_a", [128, 4])
    wqbn_d = din("w_qb_nope", [16, 128, 8 * 128])
    wqbp_d = din("w_qb_pe", [16, 128, 8 * 128])
    wkvbk_d = din("w_kvb_k", [16, 128, 4 * 128])
    wkvbv_d = din("w_kvb_v", [512, 2048])
    nabias_d = din("na_bias", [16, 16, 64, 768])
    wout_d = din("w_out", [8, 128, DC * 512])
    wmq_d = din("w_mem_q", [4, 128, DC * 128])
    wmk_d = din("w_mem_k", [4, 128, DC * 128])
    wmv_d = din("w_mem_v", [D, 512])
    wmo_d = din("w_mem_o", [512, D])
    ropec_d = din("rope_c", [64, 2])
    out_d = nc.dram_tensor("out", [T, D], F32, kind="ExternalOutput").ap()

    FO = dscr("FO", [T, D])
    H = dscr("H", [T, D])
    QN = dscr("QN", [2048, T], BF16)
    KE = dscr("KE", [2048, 1536], BF16)
    VE = dscr("VE", [1536, 2048], BF16)
    CQ = dscr("CQ", [1024, T])
    CKV = dscr("CKV", [512, S])
    KR = dscr("KR", [2, 64, S])
    QNP = dscr("QNP", [2048, T], BF16)
    QPE = dscr("QPE", [16, 64, T], BF16)
    KNP = dscr("KNP", [2048, S], BF16)
    VM = dscr("VM", [S, 2048], BF16)

    ARENA = 211968
    es = contextlib.ExitStack()
    with es:
        arena = es.enter_context(nc.sbuf_tensor("arena", [128, ARENA // 2], BF16))
        psum = es.enter_context(nc.psum_tensor("psum", [128, 8, 512], F32))
        sems = [es.enter_context(nc.semaphore("s%d" % i)) for i in range(96)]
        P = Prog(nc, sems)

        def V(off, n, dt=BF16):
            if dt == BF16:
                assert off % 2 == 0
                return arena[:, off // 2: off // 2 + n]
            assert off % 4 == 0
            return arena[:, off // 2: off // 2 + 2 * n].bitcast(dt)

        AT = V(0, DC * T).rearrange("p (c t) -> p c t", c=DC)
        CB = 65536
        ones_bf = V(CB, 128)
        ident_bf = V(CB + 256, 128)
        ones_f = V(CB + 512, 128, F32)
        sc = V(CB + 1024, 64, F32)
        P.bsc = sc[:, 62:63]
        ropec = V(CB + 1288, 2, F32)
        gqa_sb = V(CB + 1296, 8, F32)
        gkva_sb = V(CB + 1328, 4, F32)
        ident_f = V(CB + 1536, 128, F32)
        R0 = CB + 4096

        misc_sem = P.newsem()

        def c_ones(e):
            return e.memset(ones_bf, 1.0)
        t_ob = P.op("pool", c_ones)
        t_of = P.op("pool", lambda e: e.memset(ones_f, 1.0))
        t_i1 = P.op("pool", lambda e: e.iota(ident_f.bitcast(I32), [[1, 128]], base=0, channel_multiplier=-1))
        t_i2 = P.op("dve", lambda e: e.tensor_copy(out=ident_f, in_=ident_f.bitcast(I32)), [t_i1])
        t_id = P.op("dve", lambda e: e.tensor_scalar(out=ident_bf, in0=ident_f, scalar1=0.0, scalar2=None,
                                                     op0=ALU.is_equal), [t_i2])
        t_c2 = P.dma("sp", ropec[0:64, :], ropec_d[:, :], misc_sem)
        t_c3 = P.dma("sp", gqa_sb, gqa_d[:, :], misc_sem)
        t_c4 = P.dma("sp", gkva_sb, gkva_d[:, :], misc_sem)
        const_tok = t_c4

        rr = {"ev": 0}

        def evac_engine():
            rr["ev"] += 1
            return "act" if rr["ev"] % 2 else "dve"

        def copy_op(eng, out, in_):
            if eng == "act":
                return lambda e: e.activation(out=out, in_=in_, func=AF.Copy)
            return lambda e: e.tensor_copy(out=out, in_=in_)

        def gemm_fm(rhs_fn, nK, ntb, n_fo, w_src_fn, slot_ring, evac_fn, sub=1, rhs_ready=None,
                    bank_sets=((0, 1), (2, 3), (4, 5), (6, 7)), mcols=None, ntok=512):
            u = gemm_fm.counter
            last_pe = None
            for fo in range(n_fo):
                si = slot_ring.next()
                slot = slot_ring.aps[si]
                wsrc = w_src_fn(fo)
                ld = P.dma("pool", slot[:, 0:wsrc.shape[1]], wsrc, slot_ring.sems[si], [slot_ring.free[si]])
                for tb in range(ntb):
                    bs = bank_sets[u % len(bank_sets)]
                    u += 1
                    for s_ in range(sub):
                        b = bs[s_]
                        lo, hi = (0, 128) if mcols is None else mcols[s_]
                        wsub = s_ if mcols is None else 0
                        for k in range(nK):
                            first, last = (k == 0), (k == nK - 1)
                            ws = []
                            if first:
                                ws = [ld, P.bank_free[b], rhs_ready]
                            base = (wsub * nK + k) * 128

                            def mm(e, b=b, base=base, lo=lo, hi=hi, k=k, tb=tb, first=first, last=last, slot=slot):
                                return e.matmul(psum[0:hi - lo, b, 0:ntok], lhsT=slot[:, base + lo: base + hi],
                                                rhs=rhs_fn(k, tb), start=first, stop=last)
                            tok = P.op("pe", mm, ws, signal=(last and s_ == sub - 1))
                    last_pe = tok
                    rel = evac_fn(fo, tb, bs[:sub], tok)
                    for b in bs[:sub]:
                        P.bank_free[b] = rel
                slot_ring.free[si] = last_pe
            gemm_fm.counter = u
            return last_pe
        gemm_fm.counter = 0

        def gemm_tm(lhs_fn, nK, nt, n_cb, w_src_fn, slot_ring, evac_fn, lhs_ready=None, ncol=512, tgs=2):
            g = gemm_tm.counter
            last_pe = None
            for cb in range(n_cb):
                si = slot_ring.next()
                slot = slot_ring.aps[si]
                wsrc = w_src_fn(cb)
                wdst = slot[:, 0:nK * ncol]
                if len(wsrc.shape) == 3:
                    wdst = wdst.rearrange("p (k n) -> p k n", k=nK)
                ld = P.dma("pool", wdst, wsrc, slot_ring.sems[si], [slot_ring.free[si]])
                for tg in range(nt // tgs):
                    bs = [((g % (8 // tgs)) * tgs + j) for j in range(tgs)]
                    g += 1
                    for k in range(nK):
                        first, last = (k == 0), (k == nK - 1)
                        for j in range(tgs):
                            b = bs[j]
                            t = tg * tgs + j
                            ws = [ld, P.bank_free[b], lhs_ready] if first else []

                            def mm(e, b=b, k=k, t=t, first=first, last=last, slot=slot):
                                return e.matmul(psum[:, b, 0:ncol], lhsT=lhs_fn(k, t),
                                                rhs=slot[:, k * ncol:(k + 1) * ncol], start=first, stop=last)
                            tok = P.op("pe", mm, ws, signal=(last and j == tgs - 1))
                    last_pe = tok
                    rel = evac_fn(cb, tg, bs, tok)
                    for b in bs:
                        P.bank_free[b] = rel
                slot_ring.free[si] = last_pe
            gemm_tm.counter = g
            return last_pe
        gemm_tm.counter = 0

        def fm_src(wd, nK):
            return lambda fo: wd[fo]

        class TMEvac:
            def __init__(self, dst, stage_ring, in_ring=None, tgs=2, ncol=512, dt=F32):
                self.dst, self.ring, self.in_ring, self.tgs, self.ncol, self.dt = dst, stage_ring, in_ring, tgs, ncol, dt
                self.store_tok = {}

            def __call__(self, cb, tg, bs, pe_tok, accumulate=False):
                tgs, ncol = self.tgs, self.ncol
                si = self.ring.next()
                st = self.ring.aps[si]
                dview = self.dst[tg * tgs * 128:(tg + 1) * tgs * 128, cb * ncol:(cb + 1) * ncol].rearrange(
                    "(t p) n -> p t n", p=128)
                src = psum[:, bs[0]:bs[0] + tgs, 0:ncol]
                stv = st.rearrange("p (t n) -> p t n", t=tgs)
                if accumulate:
                    ii = self.in_ring.next()
                    sin = self.in_ring.aps[ii].rearrange("p (t n) -> p t n", t=tgs)
                    ldt = P.dma("act", sin, dview, self.in_ring.sems[ii],
                                [self.in_ring.free[ii], self.store_tok.get((cb, tg))])
                    ct = P.op("dve", lambda e: e.tensor_tensor(out=stv, in0=src, in1=sin, op=ALU.add),
                              [pe_tok, ldt, self.ring.free[si]])
                    self.in_ring.free[ii] = ct
                else:
                    eng = evac_engine()
                    ct = P.op(eng, copy_op(eng, stv, src), [pe_tok, self.ring.free[si]])
                stt = P.dma("sp", dview, stv, self.ring.sems[si], [ct])
                self.ring.free[si] = stt
                self.store_tok[(cb, tg)] = stt
                self.last_store = stt
                return ct

        class FMEvac:
            def __init__(self, dst_fn, stage_ring, ntok=512):
                self.dst_fn, self.ring, self.ntok = dst_fn, stage_ring, ntok
                self.last_store = None
                self.stores = []

            def __call__(self, fo, tb, bs, pe_tok):
                si = self.ring.next()
                st = self.ring.aps[si]
                eng = evac_engine()
                ct = P.op(eng, copy_op(eng, st[:, 0:self.ntok], psum[:, bs[0], 0:self.ntok]), [pe_tok, self.ring.free[si]])
                stt = P.dma("sp", self.dst_fn(fo, tb), st[:, 0:self.ntok], self.ring.sems[si], [ct])
                self.ring.free[si] = stt
                self.last_store = stt
                self.stores.append(stt)
                return ct

        def carve(base, sizes):
            offs = []
            o = base
            for s_ in sizes:
                offs.append(o)
                o += s_
            assert o <= ARENA, (o, ARENA)
            return offs

        def load_gain(dst, idx, sem, waits=()):
            return P.dma("sp", dst, gains_d[idx:idx + 1, :].partition_broadcast(128), sem, waits)

        def rstd_from_ss(ss, tmp, rstd, n, waits, npart=128):
            a = P.op("dve", lambda e: e.tensor_scalar(out=tmp, in0=ss, scalar1=1.0 / n, scalar2=EPS,
                                                      op0=ALU.mult, op1=ALU.add), waits)
            b = P.op("act", lambda e: e.activation(out=tmp, in_=tmp, func=AF.Sqrt), [a])
            c = P.op("dve", lambda e: e.reciprocal(out=rstd, in_=tmp), [b])
            return c

        def transpose_into_AT(ubf, t, u_tok, at_free):
            last = None
            for grp in range(8):
                b = grp % 8
                pb = psum[:, b, :].bitcast(BF16)
                for j in range(4):
                    c = grp * 4 + j
                    ws = [u_tok, P.bank_free[b]] if j == 0 else []

                    def tr(e, c=c, j=j, pb=pb):
                        return e.transpose(pb[:, j * 128:(j + 1) * 128], ubf[:, c * 128:(c + 1) * 128], ident_bf)
                    tok = P.op("pe", tr, ws, signal=(j == 3))
                eng = evac_engine()
                dst = AT[:, grp * 4:(grp + 1) * 4, t * 128:(t + 1) * 128]
                srcv = pb[:, 0:512].rearrange("p (c n) -> p c n", c=4)
                ct = P.op(eng, copy_op(eng, dst, srcv), [tok, at_free])
                P.bank_free[b] = ct
                last = ct
            return last

        def epilogue(src, g_post, res_w, g_pre, final=False, at_free=None, src_ready=None, n_tiles=NT,
                     h_src=None, store_h=True):
            offs = carve(R0, [16384, 16384, 16384, 16384, 16384, 16384, 8192, 8192, 8192])
            fbuf = [V(offs[0], D, F32), V(offs[1], D, F32)]
            hbuf = [V(offs[2], D, F32), V(offs[3], D, F32)]
            gpost = V(offs[4], D, F32)
            gpre = V(offs[5], D, F32)
            ubf = [V(offs[6], D), V(offs[7], D)]
            junk = V(offs[8], D)
            gs = P.newsem()
            fs = [P.newsem(), P.newsem()]
            hs = [P.newsem(), P.newsem()]
            os_ = [P.newsem(), P.newsem()]
            tg1 = load_gain(gpost, g_post, gs, [at_free])
            tg2 = load_gain(gpre, g_pre, gs, [at_free])
            ffree = [at_free, at_free]
            hfree = [at_free, at_free]
            ufree = [at_free, at_free]
            hsrc = H if h_src is None else h_src
            stt = {}
            res = {}

            def p1(t):
                i = t % 2
                rows = slice(t * 128, (t + 1) * 128)
                lf = P.dma("sp", fbuf[i], src[rows, :], fs[i], [ffree[i], src_ready])
                lh = P.dma("sp", hbuf[i], hsrc[rows, :], hs[i], [hfree[i]])
                c0 = 8 * i
                ss1, tmp1, r1, ss2, tmp2, r2 = [sc[:, c0 + k:c0 + k + 1] for k in range(6)]
                a1 = P.op("act", lambda e, i=i, ss1=ss1: e.activation(out=junk, in_=fbuf[i], func=AF.Square,
                                                                      accum_out=ss1), [lf])
                rt1 = rstd_from_ss(ss1, tmp1, r1, D, [a1])
                y = P.op("dve", lambda e, i=i, r1=r1: e.scalar_tensor_tensor(
                    out=fbuf[i], in0=fbuf[i], scalar=r1, in1=gpost, op0=ALU.mult, op1=ALU.mult), [rt1, tg2])
                hh = P.op("dve", lambda e, i=i: e.scalar_tensor_tensor(
                    out=hbuf[i], in0=fbuf[i], scalar=float(res_w), in1=hbuf[i], op0=ALU.mult, op1=ALU.add), [y, lh])
                if not final:
                    ffree[i] = hh
                sth = None
                if not final and store_h:
                    sth = P.dma("sp", H[rows, :], hbuf[i], os_[i], [hh])
                a2 = P.op("act", lambda e, i=i, ss2=ss2: e.activation(out=junk, in_=hbuf[i], func=AF.Square,
                                                                      accum_out=ss2), [hh])
                stt[t] = (a2, sth, ss2, tmp2, r2)

            def p2(t):
                i = t % 2
                rows = slice(t * 128, (t + 1) * 128)
                a2, sth, ss2, tmp2, r2 = stt.pop(t)
                rt2 = rstd_from_ss(ss2, tmp2, r2, D, [a2])
                if final:
                    uo = P.op("dve", lambda e, i=i, r2=r2: e.scalar_tensor_tensor(
                        out=fbuf[i], in0=hbuf[i], scalar=r2, in1=gpre, op0=ALU.mult, op1=ALU.mult), [rt2])
                    sto = P.dma("sp", out_d[rows, :], fbuf[i], os_[i], [uo])
                    ffree[i] = sto
                    hfree[i] = uo
                    res["tok"] = sto
                else:
                    uo = P.op("dve", lambda e, i=i, r2=r2: e.scalar_tensor_tensor(
                        out=ubf[i], in0=hbuf[i], scalar=r2, in1=gpre, op0=ALU.mult, op1=ALU.mult), [rt2, ufree[i]])
                    hfree[i] = [sth, uo]
                    tr = transpose_into_AT(ubf[i], t, uo, at_free)
                    ufree[i] = tr
                    res["tok"] = tr

            for it in range(n_tiles + 1):
                if it < n_tiles:
                    p1(it)
                if it >= 1:
                    p2(it - 1)
            return res["tok"]

        def ffn(wgu_d, wd_d, at_ready):
            nPmax = max(FPARTS)
            offs = carve(R0, [nPmax * 2048, 16384, 16384, 16384, nPmax * 1024, nPmax * 1024,
                              2048, 2048, 4096, 4096, 4096, 4096, 4096, 4096])
            HT = V(offs[0], nPmax * T).rearrange("p (c t) -> p c t", c=nPmax)
            gu_ring = Ring(P, [V(offs[1], 8192), V(offs[2], 8192), V(offs[3], 8192)], at_ready)
            wd_ring = Ring(P, [V(offs[4], nPmax * 512), V(offs[5], nPmax * 512)], at_ready)
            sg = [V(offs[6], 512, F32), V(offs[7], 512, F32)]
            st_ring = Ring(P, [V(offs[8], 1024, F32), V(offs[9], 1024, F32)], at_ready)
            in_ring = Ring(P, [V(offs[10 + k], 1024, F32) for k in range(4)], at_ready)
            tmev = TMEvac(FO, st_ring, in_ring, tgs=2)
            sgfree = [None, None]
            c0 = 0
            ht_free = at_ready
            st = {"k": 0}
            last_b = None
            for pi, nP in enumerate(FPARTS):
                ht_toks = []

                def evacA(fo, tb, bs, pe_tok):
                    k = st["k"] % 2
                    st["k"] += 1
                    a = P.op("act", lambda e, k=k, bs=bs: e.activation(out=sg[k], in_=psum[:, bs[0], :], func=AF.Silu),
                             [pe_tok, sgfree[k]])
                    m = P.op("dve", lambda e, k=k, bs=bs, fo=fo, tb=tb: e.tensor_tensor(
                        out=HT[:, fo, tb * 512:(tb + 1) * 512], in0=sg[k], in1=psum[:, bs[1], :], op=ALU.mult),
                        [a, ht_free])
                    sgfree[k] = m
                    ht_toks.append(m)
                    return m

                base_stage = P.stage if pi == 0 else base_stage
                if SCOPES:
                    P.stage = base_stage + "_A%d" % pi
                gemm_fm(lambda k, tb: AT[:, k, tb * 512:(tb + 1) * 512], DC, 2, nP,
                        lambda fo, c0=c0: wgu_d[c0 + fo], gu_ring, evacA, sub=2, rhs_ready=at_ready)
                if SCOPES:
                    P.stage = base_stage + "_B%d" % pi
                ht_ready = ht_toks[-1]

                def evacB(cb, tg, bs, pe_tok, pi=pi):
                    return tmev(cb, tg, bs, pe_tok, accumulate=(pi > 0))

                last_b = gemm_tm(lambda k, t: HT[:, k, t * 128:(t + 1) * 128], nP, NT, 8,
                                 lambda cb, c0=c0, nP=nP: wd_d[cb][:, c0 * 512:(c0 + nP) * 512],
                                 wd_ring, evacB, lhs_ready=ht_ready, tgs=2)
                ht_free = last_b
                c0 += nP
            return last_b, tmev.last_store

        def prenorm_to_AT(src, gidx, bar, n_tiles=NT):
            offs = carve(R0, [16384, 16384, 16384, 8192, 8192, 8192])
            hb = [V(offs[0], D, F32), V(offs[1], D, F32)]
            gpre = V(offs[2], D, F32)
            ubf = [V(offs[3], D), V(offs[4], D)]
            junk = V(offs[5], D)
            gs = P.newsem()
            hs = [P.newsem(), P.newsem()]
            tg = load_gain(gpre, gidx, gs, [bar])
            hfree = [bar, bar]
            ufree = [bar, bar]
            for t in range(n_tiles):
                i = t % 2
                lh = P.dma("sp", hb[i], src[t * 128:(t + 1) * 128, :], hs[i], [hfree[i]])
                c0 = 8 * i
                ss, tmp, r = [sc[:, c0 + k:c0 + k + 1] for k in range(3)]
                a = P.op("act", lambda e, i=i, ss=ss: e.activation(out=junk, in_=hb[i], func=AF.Square, accum_out=ss),
                         [lh, bar])
                rt = rstd_from_ss(ss, tmp, r, D, [a])
                uo = P.op("dve", lambda e, i=i, r=r: e.scalar_tensor_tensor(
                    out=ubf[i], in0=hb[i], scalar=r, in1=gpre, op0=ALU.mult, op1=ALU.mult), [rt, tg, ufree[i]])
                hfree[i] = uo
                tr = transpose_into_AT(ubf[i], t, uo, bar)
                ufree[i] = tr

        stage_ctr = {"n": 0}

        def stage(fn, *a, name=None, **kw):
            stage_ctr["n"] += 1
            P.stage = "s%02d_%s" % (stage_ctr["n"], name or getattr(fn, "__name__", "x"))
            if STOP is not None and stage_ctr["n"] > STOP:
                return None
            bar = P.barrier()
            m = P.mark()
            r = fn(bar, *a, **kw)
            P.reset_to(m)
            return r

        t_const = P.op("dve", lambda e: e.tensor_copy(out=sc[:, 60:61], in_=sc[:, 60:61]),
                       [t_ob, t_of, t_id, const_tok])

        def inproj(bar, own):
            offs = carve(R0, [8192, 8192, 8192, 32768, 32768, 2048, 2048, 2048, 2048, 4096, 4096])
            fm_ring = Ring(P, [V(offs[0], 4096), V(offs[1], 4096), V(offs[2], 4096)], bar)
            tm_ring = Ring(P, [V(offs[3], 16384), V(offs[4], 16384)], bar)
            stf = [V(offs[5 + k], 512, F32) for k in range(4)]
            stb = [V(offs[5 + k], 512) for k in range(4)]
            fst_ring = Ring(P, stf, bar)
            bst_ring = Ring(P, stb, bar)
            bst_ring.sems = fst_ring.sems
            bst_ring.free = fst_ring.free

            class SharedRing:
                pass
            tst_ring = Ring(P, [V(offs[9], 1024), V(offs[10], 1024)], bar)

            def rhs_all(k, tb):
                return AT[:, k, tb * 512:(tb + 1) * 512]

            def fm(n_fo, wd, ring, dst_fn, ntb=2, **kw):
                ev = FMEvac(dst_fn, ring)
                gemm_fm(rhs_all, DC, ntb, n_fo, lambda fo: wd[fo], fm_ring, ev, rhs_ready=bar, **kw)
                fst_ring.i = bst_ring.i = ring.i

            coff = 0 if own else T
            fm(4, wckr_d, fst_ring, lambda fo, tb: CKV[fo * 128:(fo + 1) * 128, coff + tb * 512: coff + (tb + 1) * 512])

            class KREvac:
                def __call__(self, fo, tb, bs, pe_tok):
                    last = None
                    for s_ in range(2):
                        si = fst_ring.next()
                        bst_ring.i = fst_ring.i
                        st = fst_ring.aps[si]
                        eng = evac_engine()
                        ct = P.op(eng, copy_op(eng, st[0:64, :], psum[0:64, bs[s_], :]), [pe_tok, fst_ring.free[si]])
                        stt = P.dma("sp", KR[s_, :, coff + tb * 512: coff + (tb + 1) * 512], st[0:64, :],
                                    fst_ring.sems[si], [ct])
                        fst_ring.free[si] = stt
                        last = ct
                    return last
            gemm_fm(rhs_all, DC, 2, 1, lambda fo: wckr_d[4], fm_ring, KREvac(), sub=2, rhs_ready=bar,
                    mcols=[(0, 64), (64, 128)])
            if own:
                fm(8, wcq_d, fst_ring, lambda fo, tb: CQ[fo * 128:(fo + 1) * 128, tb * 512:(tb + 1) * 512])
                fm(16, wq_d, bst_ring, lambda fo, tb: QN[fo * 128:(fo + 1) * 128, tb * 512:(tb + 1) * 512])
                fm(16, wk_d, bst_ring, lambda fo, tb: KE[fo * 128:(fo + 1) * 128, 256 + tb * 512: 256 + (tb + 1) * 512])
                tmev = TMEvac(VE[256:1280, :], tst_ring, tgs=2, dt=BF16)
                gemm_tm(lambda k, t: AT[:, k, t * 128:(t + 1) * 128], DC, NT, 4,
                        lambda cb: wv_d[cb],
                        tm_ring, tmev, lhs_ready=bar, tgs=2)
            else:
                class KHEvac:
                    def __call__(self, fo, tb, bs, pe_tok):
                        si = bst_ring.next()
                        fst_ring.i = bst_ring.i
                        st = bst_ring.aps[si]
                        eng = evac_engine()
                        ct = P.op(eng, copy_op(eng, st, psum[:, bs[0], :]), [pe_tok, bst_ring.free[si]])
                        P.dma("sp", KE[fo * 128:(fo + 1) * 128, 1280:1536], st[:, 0:256], bst_ring.sems[si], [ct])
                        stt = P.dma("sp", KE[fo * 128:(fo + 1) * 128, 0:256], st[:, 256:512], bst_ring.sems[si], [ct])
                        bst_ring.free[si] = stt
                        return ct
                gemm_fm(rhs_all, DC, 1, 16, lambda fo: wk_d[fo], fm_ring, KHEvac(), rhs_ready=bar)

                class VHEvac:
                    def __call__(self, cb, tg, bs, pe_tok):
                        si = tst_ring.next()
                        st = tst_ring.aps[si].rearrange("p (t n) -> p t n", t=2)
                        eng = evac_engine()
                        ct = P.op(eng, copy_op(eng, st, psum[:, bs[0]:bs[0] + 2, :]), [pe_tok, tst_ring.free[si]])
                        r0 = 1280 if tg == 0 else 0
                        stt = P.dma("sp", VE[r0:r0 + 256, cb * 512:(cb + 1) * 512].rearrange("(t p) n -> p t n", p=128),
                                    st, tst_ring.sems[si], [ct])
                        tst_ring.free[si] = stt
                        return ct
                gemm_tm(lambda k, t: AT[:, k, t * 128:(t + 1) * 128], DC, 4, 4,
                        lambda cb: wv_d[cb],
                        tm_ring, VHEvac(), lhs_ready=bar, tgs=2)

        KPE_OFF = R0
        KPEs = V(KPE_OFF, S)
        ATT0 = R0 + 4096

        def mla_prep(bar):
            offs = carve(ATT0, [32768, 8192, 8192, 8192, 16384, 16384,
                                2048, 2048, 2048, 2048, 4096, 4096, 4096, 4096, 4096, 2048, 2048, 2048])
            big = V(offs[0], 8192, F32)
            cosT = V(offs[1], S, F32)
            ssinT = V(offs[2], S, F32)
            rbc = V(offs[3], S, F32)
            CQN = V(offs[4], 8 * T).rearrange("p (c t) -> p c t", c=8)
            CKVN = V(offs[5], 4 * S).rearrange("p (c t) -> p c t", c=4)
            stf = [V(offs[6 + k], 512, F32) for k in range(4)]
            stb = [V(offs[6 + k], 512) for k in range(4)]
            fst_ring = Ring(P, stf, bar)
            bst_ring = Ring(P, stb, bar)
            bst_ring.sems, bst_ring.free = fst_ring.sems, fst_ring.free
            fm_ring = Ring(P, [V(offs[10], 2048), V(offs[11], 2048), V(offs[12], 2048)], bar)
            tm_ring = Ring(P, [V(offs[13], 2048), V(offs[14], 2048)], bar)
            tst_ring = Ring(P, [V(offs[15], 1024), V(offs[16], 1024)], bar)
            sq = V(offs[17], 512, F32)
            ls = P.newsem()

            posi = big[0:64, 0:S].bitcast(I32)
            l0 = P.dma("sp", posi, posall_d[0:1, :].partition_broadcast(64), ls, [bar])
            ang = big[0:64, 2048:4096]
            tmpa = big[0:64, 4096:6144]
            TWO_PI = 2.0 * np.pi
            MAGIC = 12582912.0
            a0 = P.op("dve", lambda e: e.tensor_copy(out=ang, in_=posi), [l0])
            a1 = P.op("dve", lambda e: e.tensor_scalar(out=ang, in0=ang, scalar1=ropec[0:64, 0:1], scalar2=None,
                                                       op0=ALU.mult), [a0])

            def sin_table(dst, shift, prev):
                b0 = P.op("dve", lambda e: e.tensor_scalar(out=tmpa, in0=ang, scalar1=float(shift),
                                                           scalar2=float(1.0 / TWO_PI), op0=ALU.add, op1=ALU.mult), [prev])
                b1 = P.op("dve", lambda e: e.tensor_scalar(out=tmpa, in0=tmpa, scalar1=MAGIC, scalar2=None,
                                                           op0=ALU.add), [b0])
                b2 = P.op("dve", lambda e: e.tensor_scalar(out=tmpa, in0=tmpa, scalar1=MAGIC, scalar2=float(TWO_PI),
                                                           op0=ALU.subtract, op1=ALU.mult), [b1])
                b3 = P.op("dve", lambda e: e.scalar_tensor_tensor(out=tmpa, in0=ang, scalar=float(shift), in1=tmpa,
                                                                  op0=ALU.add, op1=ALU.subtract), [b2])
                b4 = P.op("dve", lambda e: e.tensor_scalar(out=tmpa, in0=tmpa, scalar1=3.1415925, scalar2=-3.1415925,
                                                           op0=ALU.min, op1=ALU.max), [b3])
                b5 = P.op("act", lambda e: e.activation(out=dst[0:64, :], in_=tmpa, func=AF.Sin), [b4])
                return b5
            s1 = sin_table(cosT, np.pi / 2.0, a1)
            s2 = sin_table(ssinT, 0.0, s1)
            s3 = P.op("dve", lambda e: e.tensor_scalar(out=ssinT[0:64, :], in0=ssinT[0:64, :], scalar1=ropec[0:64, 1:2],
                                                       scalar2=None, op0=ALU.mult), [s2])

            kr0 = big[0:64, 2048:4096]
            kr1 = big[0:64, 4096:6144]
            l1 = P.dma("sp", kr0, KR[0], ls, [s3])
            l2 = P.dma("sp", kr1, KR[1], ls, [s3])
            k0 = P.op("dve", lambda e: e.tensor_tensor(out=kr0, in0=kr0, in1=cosT[0:64, :], op=ALU.mult), [l1, l2])
            k1 = P.op("dve", lambda e: e.tensor_tensor(out=kr1, in0=kr1, in1=ssinT[0:64, :], op=ALU.mult), [k0])
            k2 = P.op("dve", lambda e: e.tensor_tensor(out=KPEs[0:64, :], in0=kr0, in1=kr1, op=ALU.add), [k1])
            k3 = P.dma("sp", KPEs[64:128, :], KPEs[0:64, :], ls, [k2])

            def fm_norm(src, nch, ntok, gcol, dst, prev):
                xin = big[:, 0:nch * ntok].rearrange("p (c t) -> p c t", c=nch)
                ld = P.dma("sp", xin, src.rearrange("(c p) t -> p c t", p=128), ls, [prev])
                last = ld
                for tb in range(ntok // 512):
                    b = tb % 2
                    for c in range(nch):
                        q = P.op("act", lambda e, c=c, tb=tb: e.activation(
                            out=sq, in_=xin[:, c, tb * 512:(tb + 1) * 512], func=AF.Square), [ld, last])
                        ws = [q, P.bank_free[b]] if c == 0 else [q]
                        last = P.op("pe", lambda e, b=b, c=c: e.matmul(psum[:, b, :], lhsT=ones_f, rhs=sq,
                                                                       start=(c == 0), stop=(c == nch - 1)), ws)
                    r0 = P.op("dve", lambda e, b=b, tb=tb: e.tensor_scalar(
                        out=rbc[:, tb * 512:(tb + 1) * 512], in0=psum[:, b, :], scalar1=1.0 / (nch * 128), scalar2=EPS,
                        op0=ALU.mult, op1=ALU.add), [last])
                    P.bank_free[b] = r0
                    r1 = P.op("act", lambda e, tb=tb: e.activation(out=rbc[:, tb * 512:(tb + 1) * 512],
                                                                   in_=rbc[:, tb * 512:(tb + 1) * 512], func=AF.Sqrt), [r0])
                    r2 = P.op("dve", lambda e, tb=tb: e.reciprocal(out=rbc[:, tb * 512:(tb + 1) * 512],
                                                                   in_=rbc[:, tb * 512:(tb + 1) * 512]), [r1])
                    last = r2
                for c in range(nch):
                    last = P.op("dve", lambda e, c=c: e.scalar_tensor_tensor(
                        out=dst[:, c, :], in0=xin[:, c, :], scalar=gcol[:, c:c + 1], in1=rbc[:, 0:ntok],
                        op0=ALU.mult, op1=ALU.mult), [last])
                return last
            n1 = fm_norm(CQ, 8, T, gqa_sb, CQN, k3)
            n2 = fm_norm(CKV, 4, S, gkva_sb, CKVN, n1)

            ev = FMEvac(lambda fo, tb: QNP[fo * 128:(fo + 1) * 128, tb * 512:(tb + 1) * 512], bst_ring)
            gemm_fm(lambda k, tb: CQN[:, k, tb * 512:(tb + 1) * 512], 8, 2, 16, lambda fo: wqbn_d[fo], fm_ring, ev,
                    rhs_ready=n2)
            fst_ring.i = bst_ring.i

            class QPEvac:
                def __call__(self, fo, tb, bs, pe_tok):
                    s0 = fst_ring.next()
                    s1_ = fst_ring.next()
                    bst_ring.i = fst_ring.i
                    ta, tb_ = fst_ring.aps[s0], fst_ring.aps[s1_]
                    cs = slice(tb * 512, (tb + 1) * 512)
                    o1 = P.op("dve", lambda e: e.tensor_tensor(out=ta[0:64, :], in0=psum[0:64, bs[0], :],
                                                               in1=cosT[0:64, cs], op=ALU.mult),
                              [pe_tok, fst_ring.free[s0]])
                    o2 = P.op("dve", lambda e: e.tensor_tensor(out=tb_[0:64, :], in0=psum[0:64, bs[1], :],
                                                               in1=ssinT[0:64, cs], op=ALU.mult),
                              [o1, fst_ring.free[s1_]])
                    ob = bst_ring.aps[s0]
                    o3 = P.op("dve", lambda e: e.tensor_tensor(out=ob[0:64, 0:512], in0=ta[0:64, :], in1=tb_[0:64, :],
                                                               op=ALU.add), [o2])
                    stt = P.dma("sp", QPE[fo, :, cs], ob[0:64, 0:512], fst_ring.sems[s0], [o3])
                    fst_ring.free[s0] = stt
                    fst_ring.free[s1_] = o3
                    return o2
            gemm_fm(lambda k, tb: CQN[:, k, tb * 512:(tb + 1) * 512], 8, 2, 16, lambda fo: wqbp_d[fo], fm_ring, QPEvac(),
                    sub=2, rhs_ready=n2, mcols=[(0, 64), (64, 128)])

            ev2 = FMEvac(lambda fo, tb: KNP[fo * 128:(fo + 1) * 128, tb * 512:(tb + 1) * 512], bst_ring)
            gemm_fm(lambda k, tb: CKVN[:, k, tb * 512:(tb + 1) * 512], 4, 4, 16, lambda fo: wkvbk_d[fo], fm_ring, ev2,
                    rhs_ready=n2)
            fst_ring.i = bst_ring.i
            tmev = TMEvac(VM, tst_ring, tgs=2, dt=BF16)
            gemm_tm(lambda k, t: CKVN[:, k, t * 128:(t + 1) * 128], 4, 16, 4,
                    lambda cb: wkvbv_d.rearrange("(k p) n -> p k n", p=128)[:, :, cb * 512:(cb + 1) * 512],
                    tm_ring, tmev, lhs_ready=n2, tgs=2)

        def attn_T(bar, heads, nkc, load_fn, kT_fn, q_fn, v_fn, out_fn, scale, bufs, pe_fn=None):
            pt_ring = Ring(P, bufs["pt"], bar)
            rec = bufs["rec"]
            st_banks = [0, 1, 2]
            unit = 0
            hfree = [bar, bar]
            recfree = bar
            ldts = {0: load_fn(0, 0, hfree[0])}
            for h in range(heads):
                hs_ = h % 2
                if h + 1 < heads:
                    ldts[h + 1] = load_fn(h + 1, (h + 1) % 2, hfree[(h + 1) % 2])
                ldt = ldts[h]
                last_pe = None
                for qb in range(2):
                    ob = 3 + 2 * (unit % 2)
                    sb_ = ob + 1
                    unit += 1
                    st_tok = {}

                    def emit_st(kc):
                        b = st_banks[kc % 3]
                        ws = [ldt, P.bank_free[b]]
                        kt_, qv_ = kT_fn(h, kc), q_fn(h, qb)
                        if pe_fn is None:
                            st_tok[kc] = P.op("pe", lambda e, b=b, kt_=kt_, qv_=qv_: e.matmul(
                                psum[:, b, :], lhsT=kt_, rhs=qv_, start=True, stop=True), ws)
                        else:
                            P.op("pe", lambda e, b=b, kt_=kt_, qv_=qv_: e.matmul(
                                psum[:, b, :], lhsT=kt_, rhs=qv_, start=True, stop=False), ws,
                                signal=False)
                            kp, qp = pe_fn(h, kc, qb)
                            st_tok[kc] = P.op("pe", lambda e, b=b, kp=kp, qp=qp: e.matmul(
                                psum[:, b, :], lhsT=kp, rhs=qp, start=False, stop=True))
                    for kc in range(min(3, nkc)):
                        emit_st(kc)
                    for kc in range(nkc):
                        b = st_banks[kc % 3]
                        pi = pt_ring.next()
                        pt = pt_ring.aps[pi]
                        ex = P.op("act", lambda e, b=b, pt=pt: e.activation(out=pt, in_=psum[:, b, :], func=AF.Exp,
                                                                            scale=float(scale)),
                                  [st_tok[kc], pt_ring.free[pi]])
                        P.bank_free[b] = ex
                        ws = [ex]
                        if kc == 0:
                            ws += [P.bank_free[ob], P.bank_free[sb_]]
                        vt_ = v_fn(h, kc)
                        P.op("pe", lambda e, kc=kc, pt=pt, ob=ob, vt_=vt_: e.matmul(
                            psum[:, ob, :], lhsT=vt_, rhs=pt, start=(kc == 0), stop=(kc == nkc - 1)), ws,
                            signal=False)
                        sm = P.op("pe", lambda e, kc=kc, pt=pt, sb_=sb_: e.matmul(
                            psum[:, sb_, :], lhsT=ones_bf, rhs=pt, start=(kc == 0), stop=(kc == nkc - 1)))
                        pt_ring.free[pi] = sm
                        last_pe = sm
                        if kc + 3 < nkc:
                            emit_st(kc + 3)
                    r0 = P.op("dve", lambda e, sb_=sb_: e.reciprocal(out=rec, in_=psum[:, sb_, :]), [last_pe, recfree])
                    o0 = P.op("dve", lambda e, ob=ob, h=h, qb=qb: e.tensor_tensor(
                        out=out_fn(h, qb), in0=psum[:, ob, :], in1=rec, op=ALU.mult), [r0])
                    recfree = o0
                    P.bank_free[ob] = o0
                    P.bank_free[sb_] = o0
                hfree[hs_] = last_pe

        def mla_attn(bar):
            offs = carve(ATT0, [12288, 12288, 1024, 1024, 1024, 1024, 2048])
            hb = [V(offs[0], 6144), V(offs[1], 6144)]
            bufs = {"pt": [V(offs[2 + k], 512) for k in range(4)], "rec": V(offs[6], 512, F32)}
            lsem = [P.newsem(), P.newsem()]

            def views(s_):
                b = hb[s_]
                return (b[:, 0:2048], b[:, 2048:4096].rearrange("p (c d) -> p c d", c=16), b[:, 4096:5120],
                        b[:, 5120:6144])

            def load(h, s_, free):
                knp, vv, qn, qp = views(s_)
                P.dma("sp", knp, KNP[h * 128:(h + 1) * 128, :], lsem[s_], [free])
                P.dma("sp", vv, VM[:, h * 128:(h + 1) * 128].rearrange("(c p) d -> p c d", p=128), lsem[s_], [free])
                P.dma("sp", qn, QNP[h * 128:(h + 1) * 128, :], lsem[s_], [free])
                return P.dma("sp", qp[0:64, :], QPE[h], lsem[s_], [free])

            attn_T(bar, 16, 16, load,
                   lambda h, kc: views(h % 2)[0][:, kc * 128:(kc + 1) * 128],
                   lambda h, qb: views(h % 2)[2][:, qb * 512:(qb + 1) * 512],
                   lambda h, kc: views(h % 2)[1][:, kc, :],
                   lambda h, qb: AT[:, 16 + h, qb * 512:(qb + 1) * 512],
                   192.0 ** -0.5, bufs,
                   pe_fn=lambda h, kc, qb: (KPEs[0:64, kc * 128:(kc + 1) * 128],
                                            views(h % 2)[3][0:64, qb * 512:(qb + 1) * 512]))

        def na_attn(bar):
            offs = carve(ATT0, [2048, 2048, 3072, 3072, 3072, 3072, 2816, 2816,
                                3072, 3072, 3072, 3072, 3072, 1536, 1536, 768, 768])
            qh = [V(offs[0], 1024), V(offs[1], 1024)]
            kh = [V(offs[2], 1536), V(offs[3], 1536)]
            va = [V(offs[4], 1536).rearrange("p (c d) -> p c d", c=12), V(offs[5], 1536).rearrange("p (c d) -> p c d", c=12)]
            vb = [V(offs[6], 1408).rearrange("p (c d) -> p c d", c=11), V(offs[7], 1408).rearrange("p (c d) -> p c d", c=11)]
            bias_ring = Ring(P, [V(offs[8 + k], 768, F32) for k in range(3)], bar)
            sbuf_ = [V(offs[11], 768, F32), V(offs[12], 768, F32)]
            pn = [V(offs[13], 768), V(offs[14], 768)]
            pts = [V(offs[15], 384), V(offs[16], 384)]
            lsem = [P.newsem(), P.newsem()]
            hfree = [bar, bar]
            sbfree = [bar, bar]
            pnfree = [bar, bar]
            ptfree = [bar, bar]
            scale = 128.0 ** -0.5
            units = [(hd, r) for hd in range(16) for r in range(16)]
            N = len(units)
            stt = {}
            ldts = {}

            def load_head(hd):
                s_ = hd % 2
                P.dma("sp", qh[s_], QN[hd * 128:(hd + 1) * 128, :], lsem[s_], [hfree[s_]])
                P.dma("sp", kh[s_], KE[hd * 128:(hd + 1) * 128, :], lsem[s_], [hfree[s_]])
                P.dma("sp", va[s_], VE[:, hd * 128:(hd + 1) * 128].rearrange("(c p) d -> p c d", p=128), lsem[s_],
                      [hfree[s_]])
                ldts[hd] = P.dma("sp", vb[s_], VE[64:64 + 1408, hd * 128:(hd + 1) * 128].rearrange(
                    "(c p) d -> p c d", p=128), lsem[s_], [hfree[s_]])

            def phase1(u):
                hd, r = units[u]
                s_ = hd % 2
                su = r if r < 4 else (r - 2 if r <= 12 else r - 4)
                k0 = su * 64
                i = u % 2
                ba = 2 * i
                bi = bias_ring.next()
                bt = bias_ring.aps[bi]
                lb = P.dma("sp", bt[0:64, :], nabias_d[hd, r], bias_ring.sems[bi], [bias_ring.free[bi]])
                qv_ = qh[s_][:, r * 64:(r + 1) * 64]
                k1_, k2_ = kh[s_][:, k0:k0 + 512], kh[s_][:, k0 + 512:k0 + 768]
                P.op("pe", lambda e, ba=ba, qv_=qv_, k1_=k1_: e.matmul(
                    psum[0:64, ba, :], lhsT=qv_, rhs=k1_, start=True, stop=True),
                    [ldts[hd], P.bank_free[ba], P.bank_free[ba + 1]], signal=False)
                s1 = P.op("pe", lambda e, ba=ba, qv_=qv_, k2_=k2_: e.matmul(
                    psum[0:64, ba + 1, 0:256], lhsT=qv_, rhs=k2_, start=True, stop=True))
                sflat = psum[0:64, ba:ba + 2, :].rearrange("p a n -> p (a n)")[:, 0:768]
                sbv = sbuf_[i][0:64, :]
                d0 = P.op("dve", lambda e, sflat=sflat, sbv=sbv, bt=bt: e.scalar_tensor_tensor(
                    out=sbv, in0=sflat, scalar=float(scale), in1=bt[0:64, :], op0=ALU.mult, op1=ALU.add),
                    [s1, lb, sbfree[i]])
                P.bank_free[ba] = d0
                P.bank_free[ba + 1] = d0
                bias_ring.free[bi] = d0
                c0 = 16 + 4 * i
                nmx, rs_, rinv = [sc[0:64, c0 + k:c0 + k + 1] for k in range(3)]
                d1 = P.op("dve", lambda e, sbv=sbv, nmx=nmx: e.tensor_reduce(out=nmx, in_=sbv, op=ALU.max, axis=AX.X,
                                                                           negate=True), [d0])
                a0 = P.op("act", lambda e, sbv=sbv, nmx=nmx, rs_=rs_: e.activation(
                    out=sbv, in_=sbv, func=AF.Exp, bias=nmx, scale=1.0, accum_out=rs_), [d1])
                stt[u] = dict(a0=a0, sbv=sbv, rs_=rs_, rinv=rinv, su=su, i=i)

            def phase2(u):
                hd, r = units[u]
                st_ = stt[u]
                i = st_["i"]
                sbv, rs_, rinv = st_["sbv"], st_["rs_"], st_["rinv"]
                pb_ = 4 + i
                d2 = P.op("dve", lambda e, rs_=rs_, rinv=rinv: e.reciprocal(out=rinv, in_=rs_), [st_["a0"]])
                pnv = pn[i][0:64, :]
                d3 = P.op("dve", lambda e, sbv=sbv, pnv=pnv, rinv=rinv: e.tensor_scalar(
                    out=pnv, in0=sbv, scalar1=rinv, scalar2=None, op0=ALU.mult), [d2, pnfree[i]])
                sbfree[i] = d3
                pbv = psum[:, pb_, :].bitcast(BF16)
                for j in range(6):
                    ws = [d3, P.bank_free[pb_]] if j == 0 else []
                    tr = P.op("pe", lambda e, j=j, pnv=pnv, pbv=pbv: e.transpose(
                        pbv[:, j * 64:(j + 1) * 64], pnv[:, j * 128:(j + 1) * 128], ident_bf[0:64, 0:64]), ws,
                        signal=(j == 5))
                pnfree[i] = tr
                ptv = pts[i]
                c1 = P.op("act", lambda e, ptv=ptv, pbv=pbv: e.activation(out=ptv, in_=pbv[:, 0:384], func=AF.Copy),
                          [tr, ptfree[i]])
                P.bank_free[pb_] = c1
                st_["c1"] = c1
                st_["ptv"] = ptv

            def phase3(u):
                hd, r = units[u]
                s_ = hd % 2
                st_ = stt.pop(u)
                c1, ptv, su, i = st_["c1"], st_["ptv"], st_["su"], st_["i"]
                ob = 6 + (r // 8)
                for j in range(6):
                    if su % 2 == 0:
                        vt = va[s_][:, su // 2 + j, :]
                    else:
                        vt = vb[s_][:, (su - 1) // 2 + j, :]
                    ws = [c1] if j else [c1, P.bank_free[ob]]
                    pv = P.op("pe", lambda e, vt=vt, ptv=ptv, j=j, ob=ob, r=r: e.matmul(
                        psum[:, ob, (r % 8) * 64:(r % 8 + 1) * 64], lhsT=vt, rhs=ptv[:, j * 64:(j + 1) * 64],
                        start=(j == 0), stop=(j == 5), skip_group_check=True), ws, signal=(j == 5))
                ptfree[i] = pv
                if r % 8 == 7:
                    eng = evac_engine()
                    ev = P.op(eng, copy_op(eng, AT[:, hd, (r // 8) * 512:(r // 8 + 1) * 512], psum[:, ob, :]), [pv])
                    P.bank_free[ob] = ev
                if r == 15:
                    hfree[s_] = pv

            load_head(0)
            load_head(1)
            for it in range(N + 2):
                if it < N:
                    hd, r = units[it]
                    if r == 2 and hd >= 1 and hd + 1 < 16:
                        load_head(hd + 1)
                    phase1(it)
                if 0 <= it - 1 < N:
                    phase2(it - 1)
                if 0 <= it - 2 < N:
                    phase3(it - 2)

        def out_proj(bar):
            offs = carve(R0, [32768, 32768, 4096, 4096])
            tm_ring = Ring(P, [V(offs[0], 16384), V(offs[1], 16384)], bar)
            st_ring = Ring(P, [V(offs[2], 1024, F32), V(offs[3], 1024, F32)], bar)
            tmev = TMEvac(FO, st_ring, tgs=2)
            gemm_tm(lambda k, t: AT[:, k, t * 128:(t + 1) * 128], DC, NT, 8,
                    lambda cb: wout_d[cb],
                    tm_ring, tmev, lhs_ready=bar, tgs=2)

        MEMT_OFF = R0
        KM_OFF = R0 + 16384
        VMM_OFF = KM_OFF + 2048
        QM_OFF = VMM_OFF + 2048
        OM_OFF = QM_OFF + 8192
        MEM_FREE = OM_OFF + 8192
        KMs = V(KM_OFF, 1024).rearrange("p (h k) -> p h k", h=4)
        VMMs = V(VMM_OFF, 1024).rearrange("p (c n) -> p c n", c=2)
        QMs = V(QM_OFF, 4096).rearrange("p (h t) -> p h t", h=4)
        OMs = V(OM_OFF, 4096).rearrange("p (h t) -> p h t", h=4)
        MEMT = V(MEMT_OFF, 8192).rearrange("p (c t) -> p c t", c=DC)

        def mem_prep(bar):
            offs = carve(MEM_FREE, [8192, 8192, 8192, 16384, 16384, 16384, 8192, 8192, 8192])
            fm_ring = Ring(P, [V(offs[0], 4096), V(offs[1], 4096), V(offs[2], 4096)], bar)
            hb = [V(offs[3], D, F32), V(offs[4], D, F32)]
            gpre = V(offs[5], D, F32)
            ubf = [V(offs[6], D), V(offs[7], D)]
            junk = V(offs[8], D)
            ls = P.newsem()
            gs = P.newsem()

            def qev(fo, tb, bs, pe_tok):
                eng = evac_engine()
                return P.op(eng, copy_op(eng, QMs[:, fo, tb * 512:(tb + 1) * 512], psum[:, bs[0], :]), [pe_tok, bar])
            gemm_fm(lambda k, tb: AT[:, k, tb * 512:(tb + 1) * 512], DC, 2, 4, lambda fo: wmq_d[fo], fm_ring, qev,
                    rhs_ready=bar)
            tg = load_gain(gpre, 9, gs, [bar])
            last = None
            for t in range(2):
                lh = P.dma("sp", hb[t], mem_d[t * 128:(t + 1) * 128, :], ls, [bar])
                ss, tmp, r = [sc[:, 8 * t + k:8 * t + k + 1] for k in range(3)]
                a = P.op("act", lambda e, t=t, ss=ss: e.activation(out=junk, in_=hb[t], func=AF.Square, accum_out=ss),
                         [lh])
                rt = rstd_from_ss(ss, tmp, r, D, [a])
                uo = P.op("dve", lambda e, t=t, r=r: e.scalar_tensor_tensor(
                    out=ubf[t], in0=hb[t], scalar=r, in1=gpre, op0=ALU.mult, op1=ALU.mult), [rt, tg])
                for grp in range(8):
                    b = grp % 8
                    pb = psum[:, b, :].bitcast(BF16)
                    for j in range(4):
                        c = grp * 4 + j
                        ws = [uo, P.bank_free[b]] if j == 0 else []
                        tok = P.op("pe", lambda e, c=c, j=j, pb=pb, t=t: e.transpose(
                            pb[:, j * 128:(j + 1) * 128], ubf[t][:, c * 128:(c + 1) * 128], ident_bf), ws,
                            signal=(j == 3))
                    eng = evac_engine()
                    ct = P.op(eng, copy_op(eng, MEMT[:, grp * 4:(grp + 1) * 4, t * 128:(t + 1) * 128],
                                           pb[:, 0:512].rearrange("p (c n) -> p c n", c=4)), [tok, bar])
                    P.bank_free[b] = ct
                    last = ct
            memt_ready = P.barrier()

            def kev(fo, tb, bs, pe_tok):
                eng = evac_engine()
                return P.op(eng, copy_op(eng, KMs[:, fo, :], psum[:, bs[0], 0:256]), [pe_tok])
            gemm_fm(lambda k, tb: MEMT[:, k, :], DC, 1, 4, lambda fo: wmk_d[fo], fm_ring, kev, rhs_ready=memt_ready,
                    ntok=256)
            wvs = V(offs[3], 16384)
            lw = P.dma("pool", wvs.rearrange("p (k n) -> p k n", k=DC), wmv_d.rearrange("(k p) n -> p k n", p=128), ls,
                       [memt_ready])
            for t in range(2):
                b = 4 + t
                for k in range(DC):
                    ws = [lw, P.bank_free[b]] if k == 0 else []
                    tok = P.op("pe", lambda e, k=k, t=t, b=b: e.matmul(
                        psum[:, b, :], lhsT=MEMT[:, k, t * 128:(t + 1) * 128], rhs=wvs[:, k * 512:(k + 1) * 512],
                        start=(k == 0), stop=(k == DC - 1)), ws, signal=(k == DC - 1))
                eng = evac_engine()
                ct = P.op(eng, copy_op(eng, VMMs[:, t, :], psum[:, b, :]), [tok])
                P.bank_free[b] = ct

        def mem_attn(bar):
            offs = carve(MEM_FREE, [1024, 1024, 1024, 1024, 2048])
            bufs = {"pt": [V(offs[k], 512) for k in range(4)], "rec": V(offs[4], 512, F32)}
            attn_T(bar, 4, 2, lambda h, s_, free: bar,
                   lambda h, kc: KMs[:, h, kc * 128:(kc + 1) * 128],
                   lambda h, qb: QMs[:, h, qb * 512:(qb + 1) * 512],
                   lambda h, kc: VMMs[:, kc, h * 128:(h + 1) * 128],
                   lambda h, qb: OMs[:, h, qb * 512:(qb + 1) * 512], 128.0 ** -0.5, bufs)

        def mem_out(bar):
            offs = carve(MEM_FREE, [4096, 4096, 4096, 4096])
            tm_ring = Ring(P, [V(offs[0], 2048), V(offs[1], 2048)], bar)
            st_ring = Ring(P, [V(offs[2], 1024, F32), V(offs[3], 1024, F32)], bar)
            tmev = TMEvac(FO, st_ring, tgs=2)
            gemm_tm(lambda k, t: OMs[:, k, t * 128:(t + 1) * 128], 4, NT, 8,
                    lambda cb: wmo_d.rearrange("(k p) n -> p k n", p=128)[:, :, cb * 512:(cb + 1) * 512],
                    tm_ring, tmev, lhs_ready=bar, tgs=2)

        stage(lambda bar: prenorm_to_AT(xo_d, 0, [bar, t_const]), name="prenorm_o")
        stage(lambda bar: ffn(w1gu_d, w1d_d, bar), name="ffn1_o")
        stage(lambda bar: epilogue(FO, 1, 0.5, 2, at_free=bar, src_ready=bar, h_src=xo_d, store_h=False), name="epi1_o")
        stage(inproj, False)
        stage(lambda bar: prenorm_to_AT(x_d, 0, bar), name="prenorm")
        stage(lambda bar: ffn(w1gu_d, w1d_d, bar), name="ffn1")
        stage(lambda bar: epilogue(FO, 1, 0.5, 2, at_free=bar, src_ready=bar, h_src=x_d), name="epi1")
        stage(inproj, True)
        stage(mla_prep)
        stage(mla_attn)
        stage(na_attn)
        stage(out_proj)
        stage(lambda bar: epilogue(FO, 3, 1.0, 4, at_free=bar, src_ready=bar), name="epi2")
        stage(mem_prep)
        stage(mem_attn)
        stage(mem_out)
        stage(lambda bar: epilogue(FO, 5, 1.0, 6, at_free=bar, src_ready=bar), name="epi3")
        stage(lambda bar: ffn(w2gu_d, w2d_d, bar), name="ffn2")
        stage(lambda bar: epilogue(FO, 7, 0.5, 8, final=True, at_free=bar, src_ready=bar), name="epi4")
        if DEBUG:
            ATD = dscr("ATD", [D, T], BF16)
            dsem = P.newsem()
            dbar = P.barrier()
            P.dma("sp", ATD.rearrange("(c p) t -> p c t", p=128), AT, dsem, [dbar])
        fbar = P.barrier()
        P.q["sp"].append((None, (fbar,), None, "end"))

        with nc.Block() as block:
            P.replay(block)
    return nc


def _fm(W):
    K, N = W.shape
    return np.ascontiguousarray(W.reshape(K // 128, 128, N // 128, 128).transpose(2, 1, 0, 3)).reshape(N // 128, 128, -1)


def _tm(W):
    K, N = W.shape
    return np.ascontiguousarray(W.reshape(K // 128, 128, N // 512, 512).transpose(2, 1, 0, 3)).reshape(N // 512, 128, -1)


def _gu(Wg, Wu):
    a = Wg.reshape(DC, 128, FC, 128).transpose(2, 1, 0, 3)
    b = Wu.reshape(DC, 128, FC, 128).transpose(2, 1, 0, 3)
    return np.ascontiguousarray(np.stack([a, b], axis=2)).reshape(FC, 128, 2 * DC * 128)


def _na_bias(rpb, h):
    out = np.full((16, 16, 64, 12, 64), NEG, dtype=np.float32)
    q = np.arange(64)[:, None]
    kc = np.arange(64)[None, :]
    cstart = np.clip(q - 8, 0, 48)
    col_ok = (kc >= cstart) & (kc < cstart + 16)
    col_idx = np.clip(kc - q + 15, 0, 30)
    for r in range(16):
        su = r if r < 4 else (r - 2 if r <= 12 else r - 4)
        R = 16 * h + r
        rs = min(max(R - 4, 0), 24)
        for i in range(12):
            j = su + i
            if j < 4:
                g = 16 * (1 - h) + 12 + j
            elif j < 20:
                g = 16 * h + j - 4
            else:
                g = 16 * (1 - h) + j - 20
            if not (rs <= g < rs + 8):
                continue
            ri = g - R + 7
            vals = rpb[:, ri][:, col_idx]
            out[:, r, :, i, :] = np.where(col_ok[None], vals, np.float32(NEG))
    return out.reshape(16, 16, 64, 768)


_CACHE = {}


def kernel(x, mem, positions, ffn1_w_gate, ffn1_w_up, ffn1_w_down, g_ffn1, w_in, g_q_a, w_q_b, g_kv_a, w_kv_b,
           na_rpb, w_out, g_mix, g_mem_in, w_mem_q, w_mem_kv, w_mem_o, g_mem_attn,
           ffn2_w_gate, ffn2_w_up, ffn2_w_down, g_ffn2, g_final):
    f32 = lambda a: np.asarray(a, dtype=np.float32)
    x = f32(x); mem = f32(mem); positions = np.asarray(positions).astype(np.int32)
    w_in = f32(w_in)[0]
    shared = {}
    shared["gains"] = np.ascontiguousarray(np.stack([
        f32(g_ffn1)[0, 0], f32(g_ffn1)[0, 1], f32(g_mix)[0, 0], f32(g_mix)[0, 1], f32(g_mem_attn)[0, 0],
        f32(g_mem_attn)[0, 1], f32(g_ffn2)[0, 0], f32(g_ffn2)[0, 1], f32(g_final)[0], f32(g_mem_in)[0]]))
    shared["w1gu"] = _gu(f32(ffn1_w_gate)[0], f32(ffn1_w_up)[0])
    shared["w1d"] = _tm(f32(ffn1_w_down)[0])
    shared["w2gu"] = _gu(f32(ffn2_w_gate)[0], f32(ffn2_w_up)[0])
    shared["w2d"] = _tm(f32(ffn2_w_down)[0])
    shared["w_q"] = _fm(w_in[:, 0:2048])
    shared["w_k"] = _fm(w_in[:, 2048:4096])
    shared["w_v"] = _tm(w_in[:, 4096:6144])
    shared["w_cq"] = _fm(w_in[:, 6144:7168])
    kr = w_in[:, 7680:7744]
    shared["w_ckr"] = _fm(np.concatenate([w_in[:, 7168:7680], kr, kr[:, 32:], kr[:, :32]], axis=1))
    shared["g_q_a"] = np.ascontiguousarray(f32(g_q_a)[0].reshape(8, 128).T)
    shared["g_kv_a"] = np.ascontiguousarray(f32(g_kv_a)[0].reshape(4, 128).T)
    wqb = f32(w_q_b)[0].reshape(1024, 16, 192)
    shared["w_qb_nope"] = _fm(np.ascontiguousarray(wqb[:, :, :128]).reshape(1024, 2048))
    pe = wqb[:, :, 128:]
    shared["w_qb_pe"] = _fm(np.ascontiguousarray(np.concatenate([pe, pe[:, :, 32:], pe[:, :, :32]], axis=2)).reshape(1024, 2048))
    wkvb = f32(w_kv_b)[0].reshape(512, 16, 256)
    shared["w_kvb_k"] = _fm(np.ascontiguousarray(wkvb[:, :, :128]).reshape(512, 2048))
    shared["w_kvb_v"] = np.ascontiguousarray(wkvb[:, :, 128:]).reshape(512, 2048)
    shared["w_out"] = _tm(f32(w_out)[0])
    shared["w_mem_q"] = _fm(f32(w_mem_q)[0])
    wmkv = f32(w_mem_kv)[0].reshape(D, 4, 256)
    shared["w_mem_k"] = _fm(np.ascontiguousarray(wmkv[:, :, :128]).reshape(D, 512))
    shared["w_mem_v"] = np.ascontiguousarray(wmkv[:, :, 128:]).reshape(D, 512)
    shared["w_mem_o"] = np.ascontiguousarray(f32(w_mem_o)[0])
    half = 32
    inv_freq = (1.0 / (np.float32(10000.0) ** (np.arange(half, dtype=np.float32) / np.float32(half)))).astype(np.float32)
    rc = np.zeros((64, 2), np.float32)
    rc[:, 0] = np.tile(inv_freq, 2)
    rc[:32, 1] = -1.0
    rc[32:, 1] = 1.0
    shared["rope_c"] = rc
    rpb = f32(na_rpb)[0]
    biases = [_na_bias(rpb, 0), _na_bias(rpb, 1)]
    perm = np.concatenate([np.arange(0, 256), np.arange(768, 1024), np.arange(256, 768)])
    in_maps = []
    for c in range(8):
        b, h = c // 2, c % 2
        own = slice(h * T, (h + 1) * T)
        oth = slice((1 - h) * T, (2 - h) * T)
        m = dict(shared)
        m["x"] = np.ascontiguousarray(x[b, own])
        m["xo"] = np.ascontiguousarray(x[b, oth][perm])
        m["mem"] = np.ascontiguousarray(mem[b])
        m["pos_all"] = np.ascontiguousarray(np.concatenate([positions[b, own], positions[b, oth][perm]])[None, :])
        m["na_bias"] = biases[h]
        in_maps.append(m)
    if _CACHE.get("prep_only"):
        return in_maps
    if "nc" not in _CACHE:
        _CACHE["nc"] = build_program()
    res = run_bass_kernel_spmd(_CACHE["nc"], in_maps, core_ids=list(range(8)))
    out = np.empty((4, S, D), np.float32)
    for c in range(8):
        b, h = c // 2, c % 2
        out[b, h * T:(h + 1) * T] = np.asarray(res.results[c]["out"], dtype=np.float32)
    return out
```

```python
import contextlib
import numpy as np
import concourse.bass as bass
import concourse.mybir as mybir
from concourse.bass_utils import run_bass_kernel_spmd

F32 = mybir.dt.float32
BF16 = mybir.dt.bfloat16
I32 = mybir.dt.int32
AF = mybir.ActivationFunctionType
ALU = mybir.AluOpType
AX = mybir.AxisListType

D = 4096
DC = 32
T = 1024
NT = 8
S = 2048
FF = 11008
FC = 86
EPS = 1e-6
FPARTS = [15, 15, 14, 14, 14, 14]
NEG = -30000.0
SAME_ENGINE_SYNC = True
DEBUG = False
STOP = None
SCOPES = False

ENGS = ("pe", "act", "dve", "pool", "sp")


class Sem:
    def __init__(self, h):
        self.h = h
        self.n = 0


class Prog:
    def __init__(self, nc, sem_handles):
        self.nc = nc
        self.free_sems = list(sem_handles)
        self.q = {e: [] for e in ENGS}
        self.esem = {e: Sem(self.free_sems.pop()) for e in ENGS}
        self.bank_free = [None] * 8
        self.dma_sems = []
        self.sem_pool = []
        self.alloc_log = []
        self.bsc = None
        self.stage = "init"

    def newsem(self):
        if self.sem_pool:
            s_ = self.sem_pool.pop()
        else:
            s_ = Sem(self.free_sems.pop())
            self.dma_sems.append(s_)
        self.alloc_log.append(s_)
        return s_

    def mark(self):
        return len(self.alloc_log)

    def reset_to(self, m):
        while len(self.alloc_log) > m:
            self.sem_pool.append(self.alloc_log.pop())

    @staticmethod
    def flat(waits):
        out = []
        for w in waits:
            if w is None:
                continue
            if isinstance(w, list):
                out.extend(Prog.flat(w))
            else:
                out.append(w)
        return tuple(out)

    def barrier(self, extra=()):
        ws = [(self.esem[e], self.esem[e].n) for e in ENGS if self.esem[e].n > 0]
        ws += [(s_, s_.n) for s_ in self.dma_sems if s_.n > 0]
        ws += list(extra)
        bs = self.bsc
        return self.op("dve", lambda e: e.tensor_copy(out=bs, in_=bs), ws)

    def op(self, eng, fn, waits=(), signal=True):
        ws = Prog.flat(waits)
        if signal:
            sem = self.esem[eng]
            sem.n += 1
            self.q[eng].append((fn, ws, (sem, 1), self.stage))
            return (sem, sem.n)
        self.q[eng].append((fn, ws, None, self.stage))
        return None

    def dma(self, eng, out, in_, sem, waits=(), **kw):
        ws = Prog.flat(waits)
        sem.n += 16
        self.q[eng].append((lambda e: e.dma_start(out=out, in_=in_, **kw), ws, (sem, 16), self.stage))
        return (sem, sem.n)

    def replay(self, block):
        nc = self.nc
        engobj = {"pe": "tensor", "act": "scalar", "dve": "vector", "pool": "gpsimd", "sp": "sync"}

        def mk(ename):
            lst = self.q[ename]
            own = self.esem[ename]

            def body(e):
                seen = {}
                cur = [None, None]
                for fn, waits, sig, stg in lst:
                    if SCOPES and stg != cur[0]:
                        if cur[1] is not None:
                            cur[1].__exit__(None, None, None)
                        cur[0] = stg
                        cur[1] = nc.named_scope(stg)
                        cur[1].__enter__()
                    for (sem, v) in waits:
                        if sem is own and not SAME_ENGINE_SYNC:
                            continue
                        if seen.get(id(sem), 0) < v:
                            e.wait_ge(sem.h, v)
                            seen[id(sem)] = v
                    if fn is None:
                        continue
                    ins = fn(e)
                    if sig is not None:
                        ins.then_inc(sig[0].h, sig[1])
                if cur[1] is not None:
                    cur[1].__exit__(None, None, None)
            return body

        for ename in ENGS:
            getattr(block, engobj[ename])(mk(ename))


class Ring:
    def __init__(self, P, aps, init=None):
        self.aps = aps
        self.sems = [P.newsem() for _ in aps]
        self.free = [init] * len(aps)
        self.i = 0

    def done(self):
        return [(s_, s_.n) for s_ in self.sems if s_.n > 0]

    def next(self):
        i = self.i
        self.i = (self.i + 1) % len(self.aps)
        return i


def build_program():
    nc = bass.Bass("TRN2", target_bir_lowering=False)

    def din(name, shape, dt=F32):
        return nc.dram_tensor(name, list(shape), dt, kind="ExternalInput").ap()

    def dscr(name, shape, dt=F32):
        kind = "ExternalOutput" if DEBUG else "Internal"
        return nc.dram_tensor(name, list(shape), dt, kind=kind).ap()

    x_d = din("x", [T, D])
    xo_d = din("xo", [T, D])
    mem_d = din("mem", [256, D])
    posall_d = din("pos_all", [1, S], I32)
    gains_d = din("gains", [10, D])
    w1gu_d = din("w1gu", [FC, 128, 2 * DC * 128])
    w1d_d = din("w1d", [8, 128, FC * 512])
    w2gu_d = din("w2gu", [FC, 128, 2 * DC * 128])
    w2d_d = din("w2d", [8, 128, FC * 512])
    wq_d = din("w_q", [16, 128, DC * 128])
    wk_d = din("w_k", [16, 128, DC * 128])
    wv_d = din("w_v", [4, 128, DC * 512])
    wcq_d = din("w_cq", [8, 128, DC * 128])
    wckr_d = din("w_ckr", [5, 128, DC * 128])
    gqa_d = din("g_q_a", [128, 8])
    gkva_d = din("g_kv_a", [128, 4])
    wqbn_d = din("w_qb_nope", [16, 128, 8 * 128])
    wqbp_d = din("w_qb_pe", [16, 128, 8 * 128])
    wkvbk_d = din("w_kvb_k", [16, 128, 4 * 128])
    wkvbv_d = din("w_kvb_v", [512, 2048])
    nabias_d = din("na_bias", [16, 16, 64, 768])
    wout_d = din("w_out", [8, 128, DC * 512])
    wmq_d = din("w_mem_q", [4, 128, DC * 128])
    wmk_d = din("w_mem_k", [4, 128, DC * 128])
    wmv_d = din("w_mem_v", [D, 512])
    wmo_d = din("w_mem_o", [512, D])
    ropec_d = din("rope_c", [64, 2])
    out_d = nc.dram_tensor("out", [T, D], F32, kind="ExternalOutput").ap()

    FO = dscr("FO", [T, D])
    H = dscr("H", [T, D])
    QN = dscr("QN", [2048, T], BF16)
    KE = dscr("KE", [2048, 1536], BF16)
    VE = dscr("VE", [1536, 2048], BF16)
    CQ = dscr("CQ", [1024, T])
    CKV = dscr("CKV", [512, S])
    KR = dscr("KR", [2, 64, S])
    QNP = dscr("QNP", [2048, T], BF16)
    QPE = dscr("QPE", [16, 64, T], BF16)
    KNP = dscr("KNP", [2048, S], BF16)
    VM = dscr("VM", [S, 2048], BF16)

    ARENA = 211968
    es = contextlib.ExitStack()
    with es:
        arena = es.enter_context(nc.sbuf_tensor("arena", [128, ARENA // 2], BF16))
        psum = es.enter_context(nc.psum_tensor("psum", [128, 8, 512], F32))
        sems = [es.enter_context(nc.semaphore("s%d" % i)) for i in range(96)]
        P = Prog(nc, sems)

        def V(off, n, dt=BF16):
            if dt == BF16:
                assert off % 2 == 0
                return arena[:, off // 2: off // 2 + n]
            assert off % 4 == 0
            return arena[:, off // 2: off // 2 + 2 * n].bitcast(dt)

        AT = V(0, DC * T).rearrange("p (c t) -> p c t", c=DC)
        CB = 65536
        ones_bf = V(CB, 128)
        ident_bf = V(CB + 256, 128)
        ones_f = V(CB + 512, 128, F32)
        sc = V(CB + 1024, 64, F32)
        P.bsc = sc[:, 62:63]
        ropec = V(CB + 1288, 2, F32)
        gqa_sb = V(CB + 1296, 8, F32)
        gkva_sb = V(CB + 1328, 4, F32)
        ident_f = V(CB + 1536, 128, F32)
        R0 = CB + 4096

        misc_sem = P.newsem()

        def c_ones(e):
            return e.memset(ones_bf, 1.0)
        t_ob = P.op("pool", c_ones)
        t_of = P.op("pool", lambda e: e.memset(ones_f, 1.0))
        t_i1 = P.op("pool", lambda e: e.iota(ident_f.bitcast(I32), [[1, 128]], base=0, channel_multiplier=-1))
        t_i2 = P.op("dve", lambda e: e.tensor_copy(out=ident_f, in_=ident_f.bitcast(I32)), [t_i1])
        t_id = P.op("dve", lambda e: e.tensor_scalar(out=ident_bf, in0=ident_f, scalar1=0.0, scalar2=None,
                                                     op0=ALU.is_equal), [t_i2])
        t_c2 = P.dma("sp", ropec[0:64, :], ropec_d[:, :], misc_sem)
        t_c3 = P.dma("sp", gqa_sb, gqa_d[:, :], misc_sem)
        t_c4 = P.dma("sp", gkva_sb, gkva_d[:, :], misc_sem)
        const_tok = t_c4

        rr = {"ev": 0}

        def evac_engine():
            rr["ev"] += 1
            return "act" if rr["ev"] % 2 else "dve"

        def copy_op(eng, out, in_):
            if eng == "act":
                return lambda e: e.activation(out=out, in_=in_, func=AF.Copy)
            return lambda e: e.tensor_copy(out=out, in_=in_)

        def gemm_fm(rhs_fn, nK, ntb, n_fo, w_src_fn, slot_ring, evac_fn, sub=1, rhs_ready=None,
                    bank_sets=((0, 1), (2, 3), (4, 5), (6, 7)), mcols=None, ntok=512):
            u = gemm_fm.counter
            last_pe = None
            for fo in range(n_fo):
                si = slot_ring.next()
                slot = slot_ring.aps[si]
                wsrc = w_src_fn(fo)
                ld = P.dma("pool", slot[:, 0:wsrc.shape[1]], wsrc, slot_ring.sems[si], [slot_ring.free[si]])
                for tb in range(ntb):
                    bs = bank_sets[u % len(bank_sets)]
                    u += 1
                    for s_ in range(sub):
                        b = bs[s_]
                        lo, hi = (0, 128) if mcols is None else mcols[s_]
                        wsub = s_ if mcols is None else 0
                        for k in range(nK):
                            first, last = (k == 0), (k == nK - 1)
                            ws = []
                            if first:
                                ws = [ld, P.bank_free[b], rhs_ready]
                            base = (wsub * nK + k) * 128

                            def mm(e, b=b, base=base, lo=lo, hi=hi, k=k, tb=tb, first=first, last=last, slot=slot):
                                return e.matmul(psum[0:hi - lo, b, 0:ntok], lhsT=slot[:, base + lo: base + hi],
                                                rhs=rhs_fn(k, tb), start=first, stop=last)
                            tok = P.op("pe", mm, ws, signal=(last and s_ == sub - 1))
                    last_pe = tok
                    rel = evac_fn(fo, tb, bs[:sub], tok)
                    for b in bs[:sub]:
                        P.bank_free[b] = rel
                slot_ring.free[si] = last_pe
            gemm_fm.counter = u
            return last_pe
        gemm_fm.counter = 0

        def gemm_tm(lhs_fn, nK, nt, n_cb, w_src_fn, slot_ring, evac_fn, lhs_ready=None, ncol=512, tgs=2):
            g = gemm_tm.counter
            last_pe = None
            for cb in range(n_cb):
                si = slot_ring.next()
                slot = slot_ring.aps[si]
                wsrc = w_src_fn(cb)
                wdst = slot[:, 0:nK * ncol]
                if len(wsrc.shape) == 3:
                    wdst = wdst.rearrange("p (k n) -> p k n", k=nK)
                ld = P.dma("pool", wdst, wsrc, slot_ring.sems[si], [slot_ring.free[si]])
                for tg in range(nt // tgs):
                    bs = [((g % (8 // tgs)) * tgs + j) for j in range(tgs)]
                    g += 1
                    for k in range(nK):
                        first, last = (k == 0), (k == nK - 1)
                        for j in range(tgs):
                            b = bs[j]
                            t = tg * tgs + j
                            ws = [ld, P.bank_free[b], lhs_ready] if first else []

                            def mm(e, b=b, k=k, t=t, first=first, last=last, slot=slot):
                                return e.matmul(psum[:, b, 0:ncol], lhsT=lhs_fn(k, t),
                                                rhs=slot[:, k * ncol:(k + 1) * ncol], start=first, stop=last)
                            tok = P.op("pe", mm, ws, signal=(last and j == tgs - 1))
                    last_pe = tok
                    rel = evac_fn(cb, tg, bs, tok)
                    for b in bs:
                        P.bank_free[b] = rel
                slot_ring.free[si] = last_pe
            gemm_tm.counter = g
            return last_pe
        gemm_tm.counter = 0

        def fm_src(wd, nK):
            return lambda fo: wd[fo]

        class TMEvac:
            def __init__(self, dst, stage_ring, in_ring=None, tgs=2, ncol=512, dt=F32):
                self.dst, self.ring, self.in_ring, self.tgs, self.ncol, self.dt = dst, stage_ring, in_ring, tgs, ncol, dt
                self.store_tok = {}

            def __call__(self, cb, tg, bs, pe_tok, accumulate=False):
                tgs, ncol = self.tgs, self.ncol
                si = self.ring.next()
                st = self.ring.aps[si]
                dview = self.dst[tg * tgs * 128:(tg + 1) * tgs * 128, cb * ncol:(cb + 1) * ncol].rearrange(
                    "(t p) n -> p t n", p=128)
                src = psum[:, bs[0]:bs[0] + tgs, 0:ncol]
                stv = st.rearrange("p (t n) -> p t n", t=tgs)
                if accumulate:
                    ii = self.in_ring.next()
                    sin = self.in_ring.aps[ii].rearrange("p (t n) -> p t n", t=tgs)
                    ldt = P.dma("act", sin, dview, self.in_ring.sems[ii],
                                [self.in_ring.free[ii], self.store_tok.get((cb, tg))])
                    ct = P.op("dve", lambda e: e.tensor_tensor(out=stv, in0=src, in1=sin, op=ALU.add),
                              [pe_tok, ldt, self.ring.free[si]])
                    self.in_ring.free[ii] = ct
                else:
                    eng = evac_engine()
                    ct = P.op(eng, copy_op(eng, stv, src), [pe_tok, self.ring.free[si]])
                stt = P.dma("sp", dview, stv, self.ring.sems[si], [ct])
                self.ring.free[si] = stt
                self.store_tok[(cb, tg)] = stt
                self.last_store = stt
                return ct

        class FMEvac:
            def __init__(self, dst_fn, stage_ring, ntok=512):
                self.dst_fn, self.ring, self.ntok = dst_fn, stage_ring, ntok
                self.last_store = None
                self.stores = []

            def __call__(self, fo, tb, bs, pe_tok):
                si = self.ring.next()
                st = self.ring.aps[si]
                eng = evac_engine()
                ct = P.op(eng, copy_op(eng, st[:, 0:self.ntok], psum[:, bs[0], 0:self.ntok]), [pe_tok, self.ring.free[si]])
                stt = P.dma("sp", self.dst_fn(fo, tb), st[:, 0:self.ntok], self.ring.sems[si], [ct])
                self.ring.free[si] = stt
                self.last_store = stt
                self.stores.append(stt)
                return ct

        def carve(base, sizes):
            offs = []
            o = base
            for s_ in sizes:
                offs.append(o)
                o += s_
            assert o <= ARENA, (o, ARENA)
            return offs

        def load_gain(dst, idx, sem, waits=()):
            return P.dma("sp", dst, gains_d[idx:idx + 1, :].partition_broadcast(128), sem, waits)

        def rstd_from_ss(ss, tmp, rstd, n, waits, npart=128):
            a = P.op("dve", lambda e: e.tensor_scalar(out=tmp, in0=ss, scalar1=1.0 / n, scalar2=EPS,
                                                      op0=ALU.mult, op1=ALU.add), waits)
            b = P.op("act", lambda e: e.activation(out=tmp, in_=tmp, func=AF.Sqrt), [a])
            c = P.op("dve", lambda e: e.reciprocal(out=rstd, in_=tmp), [b])
            return c

        def transpose_into_AT(ubf, t, u_tok, at_free):
            last = None
            for grp in range(8):
                b = grp % 8
                pb = psum[:, b, :].bitcast(BF16)
                for j in range(4):
                    c = grp * 4 + j
                    ws = [u_tok, P.bank_free[b]] if j == 0 else []

                    def tr(e, c=c, j=j, pb=pb):
                        return e.transpose(pb[:, j * 128:(j + 1) * 128], ubf[:, c * 128:(c + 1) * 128], ident_bf)
                    tok = P.op("pe", tr, ws, signal=(j == 3))
                eng = evac_engine()
                dst = AT[:, grp * 4:(grp + 1) * 4, t * 128:(t + 1) * 128]
                srcv = pb[:, 0:512].rearrange("p (c n) -> p c n", c=4)
                ct = P.op(eng, copy_op(eng, dst, srcv), [tok, at_free])
                P.bank_free[b] = ct
                last = ct
            return last

        def epilogue(src, g_post, res_w, g_pre, final=False, at_free=None, src_ready=None, n_tiles=NT,
                     h_src=None, store_h=True):
            offs = carve(R0, [16384, 16384, 16384, 16384, 16384, 16384, 8192, 8192, 8192])
            fbuf = [V(offs[0], D, F32), V(offs[1], D, F32)]
            hbuf = [V(offs[2], D, F32), V(offs[3], D, F32)]
            gpost = V(offs[4], D, F32)
            gpre = V(offs[5], D, F32)
            ubf = [V(offs[6], D), V(offs[7], D)]
            junk = V(offs[8], D)
            gs = P.newsem()
            fs = [P.newsem(), P.newsem()]
            hs = [P.newsem(), P.newsem()]
            os_ = [P.newsem(), P.newsem()]
            tg1 = load_gain(gpost, g_post, gs, [at_free])
            tg2 = load_gain(gpre, g_pre, gs, [at_free])
            ffree = [at_free, at_free]
            hfree = [at_free, at_free]
            ufree = [at_free, at_free]
            hsrc = H if h_src is None else h_src
            stt = {}
            res = {}

            def p1(t):
                i = t % 2
                rows = slice(t * 128, (t + 1) * 128)
                lf = P.dma("sp", fbuf[i], src[rows, :], fs[i], [ffree[i], src_ready])
                lh = P.dma("sp", hbuf[i], hsrc[rows, :], hs[i], [hfree[i]])
                c0 = 8 * i
                ss1, tmp1, r1, ss2, tmp2, r2 = [sc[:, c0 + k:c0 + k + 1] for k in range(6)]
                a1 = P.op("act", lambda e, i=i, ss1=ss1: e.activation(out=junk, in_=fbuf[i], func=AF.Square,
                                                                      accum_out=ss1), [lf])
                rt1 = rstd_from_ss(ss1, tmp1, r1, D, [a1])
                y = P.op("dve", lambda e, i=i, r1=r1: e.scalar_tensor_tensor(
                    out=fbuf[i], in0=fbuf[i], scalar=r1, in1=gpost, op0=ALU.mult, op1=ALU.mult), [rt1, tg2])
                hh = P.op("dve", lambda e, i=i: e.scalar_tensor_tensor(
                    out=hbuf[i], in0=fbuf[i], scalar=float(res_w), in1=hbuf[i], op0=ALU.mult, op1=ALU.add), [y, lh])
                if not final:
                    ffree[i] = hh
                sth = None
                if not final and store_h:
                    sth = P.dma("sp", H[rows, :], hbuf[i], os_[i], [hh])
                a2 = P.op("act", lambda e, i=i, ss2=ss2: e.activation(out=junk, in_=hbuf[i], func=AF.Square,
                                                                      accum_out=ss2), [hh])
                stt[t] = (a2, sth, ss2, tmp2, r2)

            def p2(t):
                i = t % 2
                rows = slice(t * 128, (t + 1) * 128)
                a2, sth, ss2, tmp2, r2 = stt.pop(t)
                rt2 = rstd_from_ss(ss2, tmp2, r2, D, [a2])
                if final:
                    uo = P.op("dve", lambda e, i=i, r2=r2: e.scalar_tensor_tensor(
                        out=fbuf[i], in0=hbuf[i], scalar=r2, in1=gpre, op0=ALU.mult, op1=ALU.mult), [rt2])
                    sto = P.dma("sp", out_d[rows, :], fbuf[i], os_[i], [uo])
                    ffree[i] = sto
                    hfree[i] = uo
                    res["tok"] = sto
                else:
                    uo = P.op("dve", lambda e, i=i, r2=r2: e.scalar_tensor_tensor(
                        out=ubf[i], in0=hbuf[i], scalar=r2, in1=gpre, op0=ALU.mult, op1=ALU.mult), [rt2, ufree[i]])
                    hfree[i] = [sth, uo]
                    tr = transpose_into_AT(ubf[i], t, uo, at_free)
                    ufree[i] = tr
                    res["tok"] = tr

            for it in range(n_tiles + 1):
                if it < n_tiles:
                    p1(it)
                if it >= 1:
                    p2(it - 1)
            return res["tok"]

        def ffn(wgu_d, wd_d, at_ready):
            nPmax = max(FPARTS)
            offs = carve(R0, [nPmax * 2048, 16384, 16384, 16384, nPmax * 1024, nPmax * 1024,
                              2048, 2048, 4096, 4096, 4096, 4096, 4096, 4096])
            HT = V(offs[0], nPmax * T).rearrange("p (c t) -> p c t", c=nPmax)
            gu_ring = Ring(P, [V(offs[1], 8192), V(offs[2], 8192), V(offs[3], 8192)], at_ready)
            wd_ring = Ring(P, [V(offs[4], nPmax * 512), V(offs[5], nPmax * 512)], at_ready)
            sg = [V(offs[6], 512, F32), V(offs[7], 512, F32)]
            st_ring = Ring(P, [V(offs[8], 1024, F32), V(offs[9], 1024, F32)], at_ready)
            in_ring = Ring(P, [V(offs[10 + k], 1024, F32) for k in range(4)], at_ready)
            tmev = TMEvac(FO, st_ring, in_ring, tgs=2)
            sgfree = [None, None]
            c0 = 0
            ht_free = at_ready
            st = {"k": 0}
            last_b = None
            for pi, nP in enumerate(FPARTS):
                ht_toks = []

                def evacA(fo, tb, bs, pe_tok):
                    k = st["k"] % 2
                    st["k"] += 1
                    a = P.op("act", lambda e, k=k, bs=bs: e.activation(out=sg[k], in_=psum[:, bs[0], :], func=AF.Silu),
                             [pe_tok, sgfree[k]])
                    m = P.op("dve", lambda e, k=k, bs=bs, fo=fo, tb=tb: e.tensor_tensor(
                        out=HT[:, fo, tb * 512:(tb + 1) * 512], in0=sg[k], in1=psum[:, bs[1], :], op=ALU.mult),
                        [a, ht_free])
                    sgfree[k] = m
                    ht_toks.append(m)
                    return m

                base_stage = P.stage if pi == 0 else base_stage
                if SCOPES:
                    P.stage = base_stage + "_A%d" % pi
                gemm_fm(lambda k, tb: AT[:, k, tb * 512:(tb + 1) * 512], DC, 2, nP,
                        lambda fo, c0=c0: wgu_d[c0 + fo], gu_ring, evacA, sub=2, rhs_ready=at_ready)
                if SCOPES:
                    P.stage = base_stage + "_B%d" % pi
                ht_ready = ht_toks[-1]

                def evacB(cb, tg, bs, pe_tok, pi=pi):
                    return tmev(cb, tg, bs, pe_tok, accumulate=(pi > 0))

                last_b = gemm_tm(lambda k, t: HT[:, k, t * 128:(t + 1) * 128], nP, NT, 8,
                                 lambda cb, c0=c0, nP=nP: wd_d[cb][:, c0 * 512:(c0 + nP) * 512],
                                 wd_ring, evacB, lhs_ready=ht_ready, tgs=2)
                ht_free = last_b
                c0 += nP
            return last_b, tmev.last_store

        def prenorm_to_AT(src, gidx, bar, n_tiles=NT):
            offs = carve(R0, [16384, 16384, 16384, 8192, 8192, 8192])
            hb = [V(offs[0], D, F32), V(offs[1], D, F32)]
            gpre = V(offs[2], D, F32)
            ubf = [V(offs[3], D), V(offs[4], D)]
            junk = V(offs[5], D)
            gs = P.newsem()
            hs = [P.newsem(), P.newsem()]
            tg = load_gain(gpre, gidx, gs, [bar])
            hfree = [bar, bar]
            ufree = [bar, bar]
            for t in range(n_tiles):
                i = t % 2
                lh = P.dma("sp", hb[i], src[t * 128:(t + 1) * 128, :], hs[i], [hfree[i]])
                c0 = 8 * i
                ss, tmp, r = [sc[:, c0 + k:c0 + k + 1] for k in range(3)]
                a = P.op("act", lambda e, i=i, ss=ss: e.activation(out=junk, in_=hb[i], func=AF.Square, accum_out=ss),
                         [lh, bar])
                rt = rstd_from_ss(ss, tmp, r, D, [a])
                uo = P.op("dve", lambda e, i=i, r=r: e.scalar_tensor_tensor(
                    out=ubf[i], in0=hb[i], scalar=r, in1=gpre, op0=ALU.mult, op1=ALU.mult), [rt, tg, ufree[i]])
                hfree[i] = uo
                tr = transpose_into_AT(ubf[i], t, uo, bar)
                ufree[i] = tr

        stage_ctr = {"n": 0}

        def stage(fn, *a, name=None, **kw):
            stage_ctr["n"] += 1
            P.stage = "s%02d_%s" % (stage_ctr["n"], name or getattr(fn, "__name__", "x"))
            if STOP is not None and stage_ctr["n"] > STOP:
                return None
            bar = P.barrier()
            m = P.mark()
            r = fn(bar, *a, **kw)
            P.reset_to(m)
            return r

        t_const = P.op("dve", lambda e: e.tensor_copy(out=sc[:, 60:61], in_=sc[:, 60:61]),
                       [t_ob, t_of, t_id, const_tok])

        def inproj(bar, own):
            offs = carve(R0, [8192, 8192, 8192, 32768, 32768, 2048, 2048, 2048, 2048, 4096, 4096])
            fm_ring = Ring(P, [V(offs[0], 4096), V(offs[1], 4096), V(offs[2], 4096)], bar)
            tm_ring = Ring(P, [V(offs[3], 16384), V(offs[4], 16384)], bar)
            stf = [V(offs[5 + k], 512, F32) for k in range(4)]
            stb = [V(offs[5 + k], 512) for k in range(4)]
            fst_ring = Ring(P, stf, bar)
            bst_ring = Ring(P, stb, bar)
            bst_ring.sems = fst_ring.sems
            bst_ring.free = fst_ring.free

            class SharedRing:
                pass
            tst_ring = Ring(P, [V(offs[9], 1024), V(offs[10], 1024)], bar)

            def rhs_all(k, tb):
                return AT[:, k, tb * 512:(tb + 1) * 512]

            def fm(n_fo, wd, ring, dst_fn, ntb=2, **kw):
                ev = FMEvac(dst_fn, ring)
                gemm_fm(rhs_all, DC, ntb, n_fo, lambda fo: wd[fo], fm_ring, ev, rhs_ready=bar, **kw)
                fst_ring.i = bst_ring.i = ring.i

            coff = 0 if own else T
            fm(4, wckr_d, fst_ring, lambda fo, tb: CKV[fo * 128:(fo + 1) * 128, coff + tb * 512: coff + (tb + 1) * 512])

            class KREvac:
                def __call__(self, fo, tb, bs, pe_tok):
                    last = None
                    for s_ in range(2):
                        si = fst_ring.next()
                        bst_ring.i = fst_ring.i
                        st = fst_ring.aps[si]
                        eng = evac_engine()
                        ct = P.op(eng, copy_op(eng, st[0:64, :], psum[0:64, bs[s_], :]), [pe_tok, fst_ring.free[si]])
                        stt = P.dma("sp", KR[s_, :, coff + tb * 512: coff + (tb + 1) * 512], st[0:64, :],
                                    fst_ring.sems[si], [ct])
                        fst_ring.free[si] = stt
                        last = ct
                    return last
            gemm_fm(rhs_all, DC, 2, 1, lambda fo: wckr_d[4], fm_ring, KREvac(), sub=2, rhs_ready=bar,
                    mcols=[(0, 64), (64, 128)])
            if own:
                fm(8, wcq_d, fst_ring, lambda fo, tb: CQ[fo * 128:(fo + 1) * 128, tb * 512:(tb + 1) * 512])
                fm(16, wq_d, bst_ring, lambda fo, tb: QN[fo * 128:(fo + 1) * 128, tb * 512:(tb + 1) * 512])
                fm(16, wk_d, bst_ring, lambda fo, tb: KE[fo * 128:(fo + 1) * 128, 256 + tb * 512: 256 + (tb + 1) * 512])
                tmev = TMEvac(VE[256:1280, :], tst_ring, tgs=2, dt=BF16)
                gemm_tm(lambda k, t: AT[:, k, t * 128:(t + 1) * 128], DC, NT, 4,
                        lambda cb: wv_d[cb],
                        tm_ring, tmev, lhs_ready=bar, tgs=2)
            else:
                class KHEvac:
                    def __call__(self, fo, tb, bs, pe_tok):
                        si = bst_ring.next()
                        fst_ring.i = bst_ring.i
                        st = bst_ring.aps[si]
                        eng = evac_engine()
                        ct = P.op(eng, copy_op(eng, st, psum[:, bs[0], :]), [pe_tok, bst_ring.free[si]])
                        P.dma("sp", KE[fo * 128:(fo + 1) * 128, 1280:1536], st[:, 0:256], bst_ring.sems[si], [ct])
                        stt = P.dma("sp", KE[fo * 128:(fo + 1) * 128, 0:256], st[:, 256:512], bst_ring.sems[si], [ct])
                        bst_ring.free[si] = stt
                        return ct
                gemm_fm(rhs_all, DC, 1, 16, lambda fo: wk_d[fo], fm_ring, KHEvac(), rhs_ready=bar)

                class VHEvac:
                    def __call__(self, cb, tg, bs, pe_tok):
                        si = tst_ring.next()
                        st = tst_ring.aps[si].rearrange("p (t n) -> p t n", t=2)
                        eng = evac_engine()
                        ct = P.op(eng, copy_op(eng, st, psum[:, bs[0]:bs[0] + 2, :]), [pe_tok, tst_ring.free[si]])
                        r0 = 1280 if tg == 0 else 0
                        stt = P.dma("sp", VE[r0:r0 + 256, cb * 512:(cb + 1) * 512].rearrange("(t p) n -> p t n", p=128),
                                    st, tst_ring.sems[si], [ct])
                        tst_ring.free[si] = stt
                        return ct
                gemm_tm(lambda k, t: AT[:, k, t * 128:(t + 1) * 128], DC, 4, 4,
                        lambda cb: wv_d[cb],
                        tm_ring, VHEvac(), lhs_ready=bar, tgs=2)

        KPE_OFF = R0
        KPEs = V(KPE_OFF, S)
        ATT0 = R0 + 4096

        def mla_prep(bar):
            offs = carve(ATT0, [32768, 8192, 8192, 8192, 16384, 16384,
                                2048, 2048, 2048, 2048, 4096, 4096, 4096, 4096, 4096, 2048, 2048, 2048])
            big = V(offs[0], 8192, F32)
            cosT = V(offs[1], S, F32)
            ssinT = V(offs[2], S, F32)
            rbc = V(offs[3], S, F32)
            CQN = V(offs[4], 8 * T).rearrange("p (c t) -> p c t", c=8)
            CKVN = V(offs[5], 4 * S).rearrange("p (c t) -> p c t", c=4)
            stf = [V(offs[6 + k], 512, F32) for k in range(4)]
            stb = [V(offs[6 + k], 512) for k in range(4)]
            fst_ring = Ring(P, stf, bar)
            bst_ring = Ring(P, stb, bar)
            bst_ring.sems, bst_ring.free = fst_ring.sems, fst_ring.free
            fm_ring = Ring(P, [V(offs[10], 2048), V(offs[11], 2048), V(offs[12], 2048)], bar)
            tm_ring = Ring(P, [V(offs[13], 2048), V(offs[14], 2048)], bar)
            tst_ring = Ring(P, [V(offs[15], 1024), V(offs[16], 1024)], bar)
            sq = V(offs[17], 512, F32)
            ls = P.newsem()

            posi = big[0:64, 0:S].bitcast(I32)
            l0 = P.dma("sp", posi, posall_d[0:1, :].partition_broadcast(64), ls, [bar])
            ang = big[0:64, 2048:4096]
            tmpa = big[0:64, 4096:6144]
            TWO_PI = 2.0 * np.pi
            MAGIC = 12582912.0
            a0 = P.op("dve", lambda e: e.tensor_copy(out=ang, in_=posi), [l0])
            a1 = P.op("dve", lambda e: e.tensor_scalar(out=ang, in0=ang, scalar1=ropec[0:64, 0:1], scalar2=None,
                                                       op0=ALU.mult), [a0])

            def sin_table(dst, shift, prev):
                b0 = P.op("dve", lambda e: e.tensor_scalar(out=tmpa, in0=ang, scalar1=float(shift),
                                                           scalar2=float(1.0 / TWO_PI), op0=ALU.add, op1=ALU.mult), [prev])
                b1 = P.op("dve", lambda e: e.tensor_scalar(out=tmpa, in0=tmpa, scalar1=MAGIC, scalar2=None,
                                                           op0=ALU.add), [b0])
                b2 = P.op("dve", lambda e: e.tensor_scalar(out=tmpa, in0=tmpa, scalar1=MAGIC, scalar2=float(TWO_PI),
                                                           op0=ALU.subtract, op1=ALU.mult), [b1])
                b3 = P.op("dve", lambda e: e.scalar_tensor_tensor(out=tmpa, in0=ang, scalar=float(shift), in1=tmpa,
                                                                  op0=ALU.add, op1=ALU.subtract), [b2])
                b4 = P.op("dve", lambda e: e.tensor_scalar(out=tmpa, in0=tmpa, scalar1=3.1415925, scalar2=-3.1415925,
                                                           op0=ALU.min, op1=ALU.max), [b3])
                b5 = P.op("act", lambda e: e.activation(out=dst[0:64, :], in_=tmpa, func=AF.Sin), [b4])
                return b5
            s1 = sin_table(cosT, np.pi / 2.0, a1)
            s2 = sin_table(ssinT, 0.0, s1)
            s3 = P.op("dve", lambda e: e.tensor_scalar(out=ssinT[0:64, :], in0=ssinT[0:64, :], scalar1=ropec[0:64, 1:2],
                                                       scalar2=None, op0=ALU.mult), [s2])

            kr0 = big[0:64, 2048:4096]
            kr1 = big[0:64, 4096:6144]
            l1 = P.dma("sp", kr0, KR[0], ls, [s3])
            l2 = P.dma("sp", kr1, KR[1], ls, [s3])
            k0 = P.op("dve", lambda e: e.tensor_tensor(out=kr0, in0=kr0, in1=cosT[0:64, :], op=ALU.mult), [l1, l2])
            k1 = P.op("dve", lambda e: e.tensor_tensor(out=kr1, in0=kr1, in1=ssinT[0:64, :], op=ALU.mult), [k0])
            k2 = P.op("dve", lambda e: e.tensor_tensor(out=KPEs[0:64, :], in0=kr0, in1=kr1, op=ALU.add), [k1])
            k3 = P.dma("sp", KPEs[64:128, :], KPEs[0:64, :], ls, [k2])

            def fm_norm(src, nch, ntok, gcol, dst, prev):
                xin = big[:, 0:nch * ntok].rearrange("p (c t) -> p c t", c=nch)
                ld = P.dma("sp", xin, src.rearrange("(c p) t -> p c t", p=128), ls, [prev])
                last = ld
                for tb in range(ntok // 512):
                    b = tb % 2
                    for c in range(nch):
                        q = P.op("act", lambda e, c=c, tb=tb: e.activation(
                            out=sq, in_=xin[:, c, tb * 512:(tb + 1) * 512], func=AF.Square), [ld, last])
                        ws = [q, P.bank_free[b]] if c == 0 else [q]
                        last = P.op("pe", lambda e, b=b, c=c: e.matmul(psum[:, b, :], lhsT=ones_f, rhs=sq,
                                                                       start=(c == 0), stop=(c == nch - 1)), ws)
                    r0 = P.op("dve", lambda e, b=b, tb=tb: e.tensor_scalar(
                        out=rbc[:, tb * 512:(tb + 1) * 512], in0=psum[:, b, :], scalar1=1.0 / (nch * 128), scalar2=EPS,
                        op0=ALU.mult, op1=ALU.add), [last])
                    P.bank_free[b] = r0
                    r1 = P.op("act", lambda e, tb=tb: e.activation(out=rbc[:, tb * 512:(tb + 1) * 512],
                                                                   in_=rbc[:, tb * 512:(tb + 1) * 512], func=AF.Sqrt), [r0])
                    r2 = P.op("dve", lambda e, tb=tb: e.reciprocal(out=rbc[:, tb * 512:(tb + 1) * 512],
                                                                   in_=rbc[:, tb * 512:(tb + 1) * 512]), [r1])
                    last = r2
                for c in range(nch):
                    last = P.op("dve", lambda e, c=c: e.scalar_tensor_tensor(
                        out=dst[:, c, :], in0=xin[:, c, :], scalar=gcol[:, c:c + 1], in1=rbc[:, 0:ntok],
                        op0=ALU.mult, op1=ALU.mult), [last])
                return last
            n1 = fm_norm(CQ, 8, T, gqa_sb, CQN, k3)
            n2 = fm_norm(CKV, 4, S, gkva_sb, CKVN, n1)

            ev = FMEvac(lambda fo, tb: QNP[fo * 128:(fo + 1) * 128, tb * 512:(tb + 1) * 512], bst_ring)
            gemm_fm(lambda k, tb: CQN[:, k, tb * 512:(tb + 1) * 512], 8, 2, 16, lambda fo: wqbn_d[fo], fm_ring, ev,
                    rhs_ready=n2)
            fst_ring.i = bst_ring.i

            class QPEvac:
                def __call__(self, fo, tb, bs, pe_tok):
                    s0 = fst_ring.next()
                    s1_ = fst_ring.next()
                    bst_ring.i = fst_ring.i
                    ta, tb_ = fst_ring.aps[s0], fst_ring.aps[s1_]
                    cs = slice(tb * 512, (tb + 1) * 512)
                    o1 = P.op("dve", lambda e: e.tensor_tensor(out=ta[0:64, :], in0=psum[0:64, bs[0], :],
                                                               in1=cosT[0:64, cs], op=ALU.mult),
                              [pe_tok, fst_ring.free[s0]])
                    o2 = P.op("dve", lambda e: e.tensor_tensor(out=tb_[0:64, :], in0=psum[0:64, bs[1], :],
                                                               in1=ssinT[0:64, cs], op=ALU.mult),
                              [o1, fst_ring.free[s1_]])
                    ob = bst_ring.aps[s0]
                    o3 = P.op("dve", lambda e: e.tensor_tensor(out=ob[0:64, 0:512], in0=ta[0:64, :], in1=tb_[0:64, :],
                                                               op=ALU.add), [o2])
                    stt = P.dma("sp", QPE[fo, :, cs], ob[0:64, 0:512], fst_ring.sems[s0], [o3])
                    fst_ring.free[s0] = stt
                    fst_ring.free[s1_] = o3
                    return o2
            gemm_fm(lambda k, tb: CQN[:, k, tb * 512:(tb + 1) * 512], 8, 2, 16, lambda fo: wqbp_d[fo], fm_ring, QPEvac(),
                    sub=2, rhs_ready=n2, mcols=[(0, 64), (64, 128)])

            ev2 = FMEvac(lambda fo, tb: KNP[fo * 128:(fo + 1) * 128, tb * 512:(tb + 1) * 512], bst_ring)
            gemm_fm(lambda k, tb: CKVN[:, k, tb * 512:(tb + 1) * 512], 4, 4, 16, lambda fo: wkvbk_d[fo], fm_ring, ev2,
                    rhs_ready=n2)
            fst_ring.i = bst_ring.i
            tmev = TMEvac(VM, tst_ring, tgs=2, dt=BF16)
            gemm_tm(lambda k, t: CKVN[:, k, t * 128:(t + 1) * 128], 4, 16, 4,
                    lambda cb: wkvbv_d.rearrange("(k p) n -> p k n", p=128)[:, :, cb * 512:(cb + 1) * 512],
                    tm_ring, tmev, lhs_ready=n2, tgs=2)

        def attn_T(bar, heads, nkc, load_fn, kT_fn, q_fn, v_fn, out_fn, scale, bufs, pe_fn=None):
            pt_ring = Ring(P, bufs["pt"], bar)
            rec = bufs["rec"]
            st_banks = [0, 1, 2]
            unit = 0
            hfree = [bar, bar]
            recfree = bar
            ldts = {0: load_fn(0, 0, hfree[0])}
            for h in range(heads):
                hs_ = h % 2
                if h + 1 < heads:
                    ldts[h + 1] = load_fn(h + 1, (h + 1) % 2, hfree[(h + 1) % 2])
                ldt = ldts[h]
                last_pe = None
                for qb in range(2):
                    ob = 3 + 2 * (unit % 2)
                    sb_ = ob + 1
                    unit += 1
                    st_tok = {}

                    def emit_st(kc):
                        b = st_banks[kc % 3]
                        ws = [ldt, P.bank_free[b]]
                        kt_, qv_ = kT_fn(h, kc), q_fn(h, qb)
                        if pe_fn is None:
                            st_tok[kc] = P.op("pe", lambda e, b=b, kt_=kt_, qv_=qv_: e.matmul(
                                psum[:, b, :], lhsT=kt_, rhs=qv_, start=True, stop=True), ws)
                        else:
                            P.op("pe", lambda e, b=b, kt_=kt_, qv_=qv_: e.matmul(
                                psum[:, b, :], lhsT=kt_, rhs=qv_, start=True, stop=False), ws,
                                signal=False)
                            kp, qp = pe_fn(h, kc, qb)
                            st_tok[kc] = P.op("pe", lambda e, b=b, kp=kp, qp=qp: e.matmul(
                                psum[:, b, :], lhsT=kp, rhs=qp, start=False, stop=True))
                    for kc in range(min(3, nkc)):
                        emit_st(kc)
                    for kc in range(nkc):
                        b = st_banks[kc % 3]
                        pi = pt_ring.next()
                        pt = pt_ring.aps[pi]
                        ex = P.op("act", lambda e, b=b, pt=pt: e.activation(out=pt, in_=psum[:, b, :], func=AF.Exp,
                                                                            scale=float(scale)),
                                  [st_tok[kc], pt_ring.free[pi]])
                        P.bank_free[b] = ex
                        ws = [ex]
                        if kc == 0:
                            ws += [P.bank_free[ob], P.bank_free[sb_]]
                        vt_ = v_fn(h, kc)
                        P.op("pe", lambda e, kc=kc, pt=pt, ob=ob, vt_=vt_: e.matmul(
                            psum[:, ob, :], lhsT=vt_, rhs=pt, start=(kc == 0), stop=(kc == nkc - 1)), ws,
                            signal=False)
                        sm = P.op("pe", lambda e, kc=kc, pt=pt, sb_=sb_: e.matmul(
                            psum[:, sb_, :], lhsT=ones_bf, rhs=pt, start=(kc == 0), stop=(kc == nkc - 1)))
                        pt_ring.free[pi] = sm
                        last_pe = sm
                        if kc + 3 < nkc:
                            emit_st(kc + 3)
                    r0 = P.op("dve", lambda e, sb_=sb_: e.reciprocal(out=rec, in_=psum[:, sb_, :]), [last_pe, recfree])
                    o0 = P.op("dve", lambda e, ob=ob, h=h, qb=qb: e.tensor_tensor(
                        out=out_fn(h, qb), in0=psum[:, ob, :], in1=rec, op=ALU.mult), [r0])
                    recfree = o0
                    P.bank_free[ob] = o0
                    P.bank_free[sb_] = o0
                hfree[hs_] = last_pe

        def mla_attn(bar):
            offs = carve(ATT0, [12288, 12288, 1024, 1024, 1024, 1024, 2048])
            hb = [V(offs[0], 6144), V(offs[1], 6144)]
            bufs = {"pt": [V(offs[2 + k], 512) for k in range(4)], "rec": V(offs[6], 512, F32)}
            lsem = [P.newsem(), P.newsem()]

            def views(s_):
                b = hb[s_]
                return (b[:, 0:2048], b[:, 2048:4096].rearrange("p (c d) -> p c d", c=16), b[:, 4096:5120],
                        b[:, 5120:6144])

            def load(h, s_, free):
                knp, vv, qn, qp = views(s_)
                P.dma("sp", knp, KNP[h * 128:(h + 1) * 128, :], lsem[s_], [free])
                P.dma("sp", vv, VM[:, h * 128:(h + 1) * 128].rearrange("(c p) d -> p c d", p=128), lsem[s_], [free])
                P.dma("sp", qn, QNP[h * 128:(h + 1) * 128, :], lsem[s_], [free])
                return P.dma("sp", qp[0:64, :], QPE[h], lsem[s_], [free])

            attn_T(bar, 16, 16, load,
                   lambda h, kc: views(h % 2)[0][:, kc * 128:(kc + 1) * 128],
                   lambda h, qb: views(h % 2)[2][:, qb * 512:(qb + 1) * 512],
                   lambda h, kc: views(h % 2)[1][:, kc, :],
                   lambda h, qb: AT[:, 16 + h, qb * 512:(qb + 1) * 512],
                   192.0 ** -0.5, bufs,
                   pe_fn=lambda h, kc, qb: (KPEs[0:64, kc * 128:(kc + 1) * 128],
                                            views(h % 2)[3][0:64, qb * 512:(qb + 1) * 512]))

        def na_attn(bar):
            offs = carve(ATT0, [2048, 2048, 3072, 3072, 3072, 3072, 2816, 2816,
                                3072, 3072, 3072, 3072, 3072, 3072, 3072, 1536, 1536, 1536, 768, 768, 768])
            qh = [V(offs[0], 1024), V(offs[1], 1024)]
            kh = [V(offs[2], 1536), V(offs[3], 1536)]
            va = [V(offs[4], 1536).rearrange("p (c d) -> p c d", c=12), V(offs[5], 1536).rearrange("p (c d) -> p c d", c=12)]
            vb = [V(offs[6], 1408).rearrange("p (c d) -> p c d", c=11), V(offs[7], 1408).rearrange("p (c d) -> p c d", c=11)]
            bias_ring = Ring(P, [V(offs[8 + k], 768, F32) for k in range(4)], bar)
            sbuf_ = [V(offs[12 + k], 768, F32) for k in range(3)]
            pn = [V(offs[15 + k], 768) for k in range(3)]
            pts = [V(offs[18 + k], 384) for k in range(3)]
            lsem = [P.newsem(), P.newsem()]
            hfree = [bar, bar]
            sbfree = [bar] * 3
            pnfree = [bar] * 3
            ptfree = [bar] * 3
            scale = 128.0 ** -0.5
            units = [(hd, r) for hd in range(16) for r in range(16)]
            N = len(units)
            stt = {}
            ldts = {}

            def load_head(hd):
                s_ = hd % 2
                P.dma("sp", qh[s_], QN[hd * 128:(hd + 1) * 128, :], lsem[s_], [hfree[s_]])
                P.dma("sp", kh[s_], KE[hd * 128:(hd + 1) * 128, :], lsem[s_], [hfree[s_]])
                P.dma("sp", va[s_], VE[:, hd * 128:(hd + 1) * 128].rearrange("(c p) d -> p c d", p=128), lsem[s_],
                      [hfree[s_]])
                ldts[hd] = P.dma("sp", vb[s_], VE[64:64 + 1408, hd * 128:(hd + 1) * 128].rearrange(
                    "(c p) d -> p c d", p=128), lsem[s_], [hfree[s_]])

            def phase1(u):
                hd, r = units[u]
                s_ = hd % 2
                su = r if r < 4 else (r - 2 if r <= 12 else r - 4)
                k0 = su * 64
                i = u % 3
                ba = 2 * i
                bi = bias_ring.next()
                bt = bias_ring.aps[bi]
                lb = P.dma("sp", bt[0:64, :], nabias_d[hd, r], bias_ring.sems[bi], [bias_ring.free[bi]])
                qv_ = qh[s_][:, r * 64:(r + 1) * 64]
                k1_, k2_ = kh[s_][:, k0:k0 + 512], kh[s_][:, k0 + 512:k0 + 768]
                P.op("pe", lambda e, ba=ba, qv_=qv_, k1_=k1_: e.matmul(
                    psum[0:64, ba, :], lhsT=qv_, rhs=k1_, start=True, stop=True),
                    [ldts[hd], P.bank_free[ba], P.bank_free[ba + 1]], signal=False)
                s1 = P.op("pe", lambda e, ba=ba, qv_=qv_, k2_=k2_: e.matmul(
                    psum[0:64, ba + 1, 0:256], lhsT=qv_, rhs=k2_, start=True, stop=True))
                sflat = psum[0:64, ba:ba + 2, :].rearrange("p a n -> p (a n)")[:, 0:768]
                sbv = sbuf_[i][0:64, :]
                d0 = P.op("dve", lambda e, sflat=sflat, sbv=sbv, bt=bt: e.scalar_tensor_tensor(
                    out=sbv, in0=sflat, scalar=float(scale), in1=bt[0:64, :], op0=ALU.mult, op1=ALU.add),
                    [s1, lb, sbfree[i]])
                bias_ring.free[bi] = d0
                c0 = 16 + 4 * i
                nmx, rs_, rinv = [sc[0:64, c0 + k:c0 + k + 1] for k in range(3)]
                d1 = P.op("dve", lambda e, sbv=sbv, nmx=nmx: e.tensor_reduce(out=nmx, in_=sbv, op=ALU.max, axis=AX.X,
                                                                           negate=True), [d0])
                a0 = P.op("act", lambda e, sbv=sbv, nmx=nmx, rs_=rs_: e.activation(
                    out=sbv, in_=sbv, func=AF.Exp, bias=nmx, scale=1.0, accum_out=rs_), [d1])
                stt[u] = dict(a0=a0, sbv=sbv, rs_=rs_, rinv=rinv, su=su, i=i, d0=d0)

            def phase2(u):
                st_ = stt[u]
                i = st_["i"]
                ba = 2 * i
                sbv, rs_, rinv = st_["sbv"], st_["rs_"], st_["rinv"]
                d2 = P.op("dve", lambda e, rs_=rs_, rinv=rinv: e.reciprocal(out=rinv, in_=rs_), [st_["a0"]])
                pnv = pn[i][0:64, :]
                d3 = P.op("act", lambda e, sbv=sbv, pnv=pnv, rinv=rinv: e.activation(
                    out=pnv, in_=sbv, func=AF.Copy, scale=rinv), [d2, pnfree[i]])
                sbfree[i] = d3
                pbv = psum[:, ba + 1, :].bitcast(BF16)[:, 512:1024]
                for j in range(6):
                    ws = [d3, st_["d0"]] if j == 0 else []
                    tr = P.op("pe", lambda e, j=j, pnv=pnv, pbv=pbv: e.transpose(
                        pbv[:, j * 64:(j + 1) * 64], pnv[:, j * 128:(j + 1) * 128], ident_bf[0:64, 0:64]), ws,
                        signal=(j == 5))
                pnfree[i] = tr
                st_["tr"] = tr
                st_["pbv"] = pbv

            def phase3(u):
                hd, r = units[u]
                s_ = hd % 2
                st_ = stt.pop(u)
                su, i, pbv = st_["su"], st_["i"], st_["pbv"]
                ba = 2 * i
                ptv = pts[i]
                c1 = P.op("act", lambda e, ptv=ptv, pbv=pbv: e.activation(out=ptv, in_=pbv[:, 0:384], func=AF.Copy),
                          [st_["tr"], ptfree[i]])
                P.bank_free[ba] = c1
                P.bank_free[ba + 1] = c1
                ob = 6 + (r // 8)
                for j in range(6):
                    if su % 2 == 0:
                        vt = va[s_][:, su // 2 + j, :]
                    else:
                        vt = vb[s_][:, (su - 1) // 2 + j, :]
                    ws = [c1] if j else [c1, P.bank_free[ob]]
                    pv = P.op("pe", lambda e, vt=vt, ptv=ptv, j=j, ob=ob, r=r: e.matmul(
                        psum[:, ob, (r % 8) * 64:(r % 8 + 1) * 64], lhsT=vt, rhs=ptv[:, j * 64:(j + 1) * 64],
                        start=(j == 0), stop=(j == 5), skip_group_check=True), ws, signal=(j == 5))
                ptfree[i] = pv
                if r % 8 == 7:
                    eng = evac_engine()
                    ev = P.op(eng, copy_op(eng, AT[:, hd, (r // 8) * 512:(r // 8 + 1) * 512], psum[:, ob, :]), [pv])
                    P.bank_free[ob] = ev
                if r == 15:
                    hfree[s_] = pv

            load_head(0)
            load_head(1)
            for it in range(N + 3):
                if 0 <= it - 3 < N:
                    phase3(it - 3)
                if 0 <= it - 2 < N:
                    phase2(it - 2)
                if it < N:
                    hd, r = units[it]
                    if r == 3 and hd >= 1 and hd + 1 < 16:
                        load_head(hd + 1)
                    phase1(it)

        def out_proj(bar):
            offs = carve(R0, [32768, 32768, 4096, 4096])
            tm_ring = Ring(P, [V(offs[0], 16384), V(offs[1], 16384)], bar)
            st_ring = Ring(P, [V(offs[2], 1024, F32), V(offs[3], 1024, F32)], bar)
            tmev = TMEvac(FO, st_ring, tgs=2)
            gemm_tm(lambda k, t: AT[:, k, t * 128:(t + 1) * 128], DC, NT, 8,
                    lambda cb: wout_d[cb],
                    tm_ring, tmev, lhs_ready=bar, tgs=2)

        MEMT_OFF = R0
        KM_OFF = R0 + 16384
        VMM_OFF = KM_OFF + 2048
        QM_OFF = VMM_OFF + 2048
        OM_OFF = QM_OFF + 8192
        MEM_FREE = OM_OFF + 8192
        KMs = V(KM_OFF, 1024).rearrange("p (h k) -> p h k", h=4)
        VMMs = V(VMM_OFF, 1024).rearrange("p (c n) -> p c n", c=2)
        QMs = V(QM_OFF, 4096).rearrange("p (h t) -> p h t", h=4)
        OMs = V(OM_OFF, 4096).rearrange("p (h t) -> p h t", h=4)
        MEMT = V(MEMT_OFF, 8192).rearrange("p (c t) -> p c t", c=DC)

        def mem_prep(bar):
            offs = carve(MEM_FREE, [8192, 8192, 8192, 16384, 16384, 16384, 8192, 8192, 8192])
            fm_ring = Ring(P, [V(offs[0], 4096), V(offs[1], 4096), V(offs[2], 4096)], bar)
            hb = [V(offs[3], D, F32), V(offs[4], D, F32)]
            gpre = V(offs[5], D, F32)
            ubf = [V(offs[6], D), V(offs[7], D)]
            junk = V(offs[8], D)
            ls = P.newsem()
            gs = P.newsem()

            def qev(fo, tb, bs, pe_tok):
                eng = evac_engine()
                return P.op(eng, copy_op(eng, QMs[:, fo, tb * 512:(tb + 1) * 512], psum[:, bs[0], :]), [pe_tok, bar])
            gemm_fm(lambda k, tb: AT[:, k, tb * 512:(tb + 1) * 512], DC, 2, 4, lambda fo: wmq_d[fo], fm_ring, qev,
                    rhs_ready=bar)
            tg = load_gain(gpre, 9, gs, [bar])
            last = None
            for t in range(2):
                lh = P.dma("sp", hb[t], mem_d[t * 128:(t + 1) * 128, :], ls, [bar])
                ss, tmp, r = [sc[:, 8 * t + k:8 * t + k + 1] for k in range(3)]
                a = P.op("act", lambda e, t=t, ss=ss: e.activation(out=junk, in_=hb[t], func=AF.Square, accum_out=ss),
                         [lh])
                rt = rstd_from_ss(ss, tmp, r, D, [a])
                uo = P.op("dve", lambda e, t=t, r=r: e.scalar_tensor_tensor(
                    out=ubf[t], in0=hb[t], scalar=r, in1=gpre, op0=ALU.mult, op1=ALU.mult), [rt, tg])
                for grp in range(8):
                    b = grp % 8
                    pb = psum[:, b, :].bitcast(BF16)
                    for j in range(4):
                        c = grp * 4 + j
                        ws = [uo, P.bank_free[b]] if j == 0 else []
                        tok = P.op("pe", lambda e, c=c, j=j, pb=pb, t=t: e.transpose(
                            pb[:, j * 128:(j + 1) * 128], ubf[t][:, c * 128:(c + 1) * 128], ident_bf), ws,
                            signal=(j == 3))
                    eng = evac_engine()
                    ct = P.op(eng, copy_op(eng, MEMT[:, grp * 4:(grp + 1) * 4, t * 128:(t + 1) * 128],
                                           pb[:, 0:512].rearrange("p (c n) -> p c n", c=4)), [tok, bar])
                    P.bank_free[b] = ct
                    last = ct
            memt_ready = P.barrier()

            def kev(fo, tb, bs, pe_tok):
                eng = evac_engine()
                return P.op(eng, copy_op(eng, KMs[:, fo, :], psum[:, bs[0], 0:256]), [pe_tok])
            gemm_fm(lambda k, tb: MEMT[:, k, :], DC, 1, 4, lambda fo: wmk_d[fo], fm_ring, kev, rhs_ready=memt_ready,
                    ntok=256)
            wvs = V(offs[3], 16384)
            lw = P.dma("pool", wvs.rearrange("p (k n) -> p k n", k=DC), wmv_d.rearrange("(k p) n -> p k n", p=128), ls,
                       [memt_ready])
            for t in range(2):
                b = 4 + t
                for k in range(DC):
                    ws = [lw, P.bank_free[b]] if k == 0 else []
                    tok = P.op("pe", lambda e, k=k, t=t, b=b: e.matmul(
                        psum[:, b, :], lhsT=MEMT[:, k, t * 128:(t + 1) * 128], rhs=wvs[:, k * 512:(k + 1) * 512],
                        start=(k == 0), stop=(k == DC - 1)), ws, signal=(k == DC - 1))
                eng = evac_engine()
                ct = P.op(eng, copy_op(eng, VMMs[:, t, :], psum[:, b, :]), [tok])
                P.bank_free[b] = ct

        def mem_attn(bar):
            offs = carve(MEM_FREE, [1024, 1024, 1024, 1024, 2048])
            bufs = {"pt": [V(offs[k], 512) for k in range(4)], "rec": V(offs[4], 512, F32)}
            attn_T(bar, 4, 2, lambda h, s_, free: bar,
                   lambda h, kc: KMs[:, h, kc * 128:(kc + 1) * 128],
                   lambda h, qb: QMs[:, h, qb * 512:(qb + 1) * 512],
                   lambda h, kc: VMMs[:, kc, h * 128:(h + 1) * 128],
                   lambda h, qb: OMs[:, h, qb * 512:(qb + 1) * 512], 128.0 ** -0.5, bufs)

        def mem_out(bar):
            offs = carve(MEM_FREE, [4096, 4096, 4096, 4096])
            tm_ring = Ring(P, [V(offs[0], 2048), V(offs[1], 2048)], bar)
            st_ring = Ring(P, [V(offs[2], 1024, F32), V(offs[3], 1024, F32)], bar)
            tmev = TMEvac(FO, st_ring, tgs=2)
            gemm_tm(lambda k, t: OMs[:, k, t * 128:(t + 1) * 128], 4, NT, 8,
                    lambda cb: wmo_d.rearrange("(k p) n -> p k n", p=128)[:, :, cb * 512:(cb + 1) * 512],
                    tm_ring, tmev, lhs_ready=bar, tgs=2)

        stage(lambda bar: prenorm_to_AT(xo_d, 0, [bar, t_const]), name="prenorm_o")
        stage(lambda bar: ffn(w1gu_d, w1d_d, bar), name="ffn1_o")
        stage(lambda bar: epilogue(FO, 1, 0.5, 2, at_free=bar, src_ready=bar, h_src=xo_d, store_h=False), name="epi1_o")
        stage(inproj, False)
        stage(lambda bar: prenorm_to_AT(x_d, 0, bar), name="prenorm")
        stage(lambda bar: ffn(w1gu_d, w1d_d, bar), name="ffn1")
        stage(lambda bar: epilogue(FO, 1, 0.5, 2, at_free=bar, src_ready=bar, h_src=x_d), name="epi1")
        stage(inproj, True)
        stage(mla_prep)
        stage(mla_attn)
        stage(na_attn)
        stage(out_proj)
        stage(lambda bar: epilogue(FO, 3, 1.0, 4, at_free=bar, src_ready=bar), name="epi2")
        stage(mem_prep)
        stage(mem_attn)
        stage(mem_out)
        stage(lambda bar: epilogue(FO, 5, 1.0, 6, at_free=bar, src_ready=bar), name="epi3")
        stage(lambda bar: ffn(w2gu_d, w2d_d, bar), name="ffn2")
        stage(lambda bar: epilogue(FO, 7, 0.5, 8, final=True, at_free=bar, src_ready=bar), name="epi4")
        if DEBUG:
            ATD = dscr("ATD", [D, T], BF16)
            dsem = P.newsem()
            dbar = P.barrier()
            P.dma("sp", ATD.rearrange("(c p) t -> p c t", p=128), AT, dsem, [dbar])
        fbar = P.barrier()
        P.q["sp"].append((None, (fbar,), None, "end"))

        with nc.Block() as block:
            P.replay(block)
    return nc


def _fm(W):
    K, N = W.shape
    return np.ascontiguousarray(W.reshape(K // 128, 128, N // 128, 128).transpose(2, 1, 0, 3)).reshape(N // 128, 128, -1)


def _tm(W):
    K, N = W.shape
    return np.ascontiguousarray(W.reshape(K // 128, 128, N // 512, 512).transpose(2, 1, 0, 3)).reshape(N // 512, 128, -1)


def _gu(Wg, Wu):
    a = Wg.reshape(DC, 128, FC, 128).transpose(2, 1, 0, 3)
    b = Wu.reshape(DC, 128, FC, 128).transpose(2, 1, 0, 3)
    return np.ascontiguousarray(np.stack([a, b], axis=2)).reshape(FC, 128, 2 * DC * 128)


def _na_bias(rpb, h):
    out = np.full((16, 16, 64, 12, 64), NEG, dtype=np.float32)
    q = np.arange(64)[:, None]
    kc = np.arange(64)[None, :]
    cstart = np.clip(q - 8, 0, 48)
    col_ok = (kc >= cstart) & (kc < cstart + 16)
    col_idx = np.clip(kc - q + 15, 0, 30)
    for r in range(16):
        su = r if r < 4 else (r - 2 if r <= 12 else r - 4)
        R = 16 * h + r
        rs = min(max(R - 4, 0), 24)
        for i in range(12):
            j = su + i
            if j < 4:
                g = 16 * (1 - h) + 12 + j
            elif j < 20:
                g = 16 * h + j - 4
            else:
                g = 16 * (1 - h) + j - 20
            if not (rs <= g < rs + 8):
                continue
            ri = g - R + 7
            vals = rpb[:, ri][:, col_idx]
            out[:, r, :, i, :] = np.where(col_ok[None], vals, np.float32(NEG))
    return out.reshape(16, 16, 64, 768)


_CACHE = {}


def kernel(x, mem, positions, ffn1_w_gate, ffn1_w_up, ffn1_w_down, g_ffn1, w_in, g_q_a, w_q_b, g_kv_a, w_kv_b,
           na_rpb, w_out, g_mix, g_mem_in, w_mem_q, w_mem_kv, w_mem_o, g_mem_attn,
           ffn2_w_gate, ffn2_w_up, ffn2_w_down, g_ffn2, g_final):
    f32 = lambda a: np.asarray(a, dtype=np.float32)
    x = f32(x); mem = f32(mem); positions = np.asarray(positions).astype(np.int32)
    w_in = f32(w_in)[0]
    shared = {}
    shared["gains"] = np.ascontiguousarray(np.stack([
        f32(g_ffn1)[0, 0], f32(g_ffn1)[0, 1], f32(g_mix)[0, 0], f32(g_mix)[0, 1], f32(g_mem_attn)[0, 0],
        f32(g_mem_attn)[0, 1], f32(g_ffn2)[0, 0], f32(g_ffn2)[0, 1], f32(g_final)[0], f32(g_mem_in)[0]]))
    shared["w1gu"] = _gu(f32(ffn1_w_gate)[0], f32(ffn1_w_up)[0])
    shared["w1d"] = _tm(f32(ffn1_w_down)[0])
    shared["w2gu"] = _gu(f32(ffn2_w_gate)[0], f32(ffn2_w_up)[0])
    shared["w2d"] = _tm(f32(ffn2_w_down)[0])
    shared["w_q"] = _fm(w_in[:, 0:2048])
    shared["w_k"] = _fm(w_in[:, 2048:4096])
    shared["w_v"] = _tm(w_in[:, 4096:6144])
    shared["w_cq"] = _fm(w_in[:, 6144:7168])
    kr = w_in[:, 7680:7744]
    shared["w_ckr"] = _fm(np.concatenate([w_in[:, 7168:7680], kr, kr[:, 32:], kr[:, :32]], axis=1))
    shared["g_q_a"] = np.ascontiguousarray(f32(g_q_a)[0].reshape(8, 128).T)
    shared["g_kv_a"] = np.ascontiguousarray(f32(g_kv_a)[0].reshape(4, 128).T)
    wqb = f32(w_q_b)[0].reshape(1024, 16, 192)
    shared["w_qb_nope"] = _fm(np.ascontiguousarray(wqb[:, :, :128]).reshape(1024, 2048))
    pe = wqb[:, :, 128:]
    shared["w_qb_pe"] = _fm(np.ascontiguousarray(np.concatenate([pe, pe[:, :, 32:], pe[:, :, :32]], axis=2)).reshape(1024, 2048))
    wkvb = f32(w_kv_b)[0].reshape(512, 16, 256)
    shared["w_kvb_k"] = _fm(np.ascontiguousarray(wkvb[:, :, :128]).reshape(512, 2048))
    shared["w_kvb_v"] = np.ascontiguousarray(wkvb[:, :, 128:]).reshape(512, 2048)
    shared["w_out"] = _tm(f32(w_out)[0])
    shared["w_mem_q"] = _fm(f32(w_mem_q)[0])
    wmkv = f32(w_mem_kv)[0].reshape(D, 4, 256)
    shared["w_mem_k"] = _fm(np.ascontiguousarray(wmkv[:, :, :128]).reshape(D, 512))
    shared["w_mem_v"] = np.ascontiguousarray(wmkv[:, :, 128:]).reshape(D, 512)
    shared["w_mem_o"] = np.ascontiguousarray(f32(w_mem_o)[0])
    half = 32
    inv_freq = (1.0 / (np.float32(10000.0) ** (np.arange(half, dtype=np.float32) / np.float32(half)))).astype(np.float32)
    rc = np.zeros((64, 2), np.float32)
    rc[:, 0] = np.tile(inv_freq, 2)
    rc[:32, 1] = -1.0
    rc[32:, 1] = 1.0
    shared["rope_c"] = rc
    rpb = f32(na_rpb)[0]
    biases = [_na_bias(rpb, 0), _na_bias(rpb, 1)]
    perm = np.concatenate([np.arange(0, 256), np.arange(768, 1024), np.arange(256, 768)])
    in_maps = []
    for c in range(8):
        b, h = c // 2, c % 2
        own = slice(h * T, (h + 1) * T)
        oth = slice((1 - h) * T, (2 - h) * T)
        m = dict(shared)
        m["x"] = np.ascontiguousarray(x[b, own])
        m["xo"] = np.ascontiguousarray(x[b, oth][perm])
        m["mem"] = np.ascontiguousarray(mem[b])
        m["pos_all"] = np.ascontiguousarray(np.concatenate([positions[b, own], positions[b, oth][perm]])[None, :])
        m["na_bias"] = biases[h]
        in_maps.append(m)
    if _CACHE.get("prep_only"):
        return in_maps
    if "nc" not in _CACHE:
        _CACHE["nc"] = build_program()
    res = run_bass_kernel_spmd(_CACHE["nc"], in_maps, core_ids=list(range(8)))
    out = np.empty((4, S, D), np.float32)
    for c in range(8):
        b, h = c // 2, c % 2
        out[b, h * T:(h + 1) * T] = np.asarray(res.results[c]["out"], dtype=np.float32)
    return out
```

```python
import contextlib
import numpy as np
import concourse.bass as bass
import concourse.mybir as mybir
from concourse.bass_utils import run_bass_kernel_spmd

F32 = mybir.dt.float32
BF16 = mybir.dt.bfloat16
I32 = mybir.dt.int32
AF = mybir.ActivationFunctionType
ALU = mybir.AluOpType
AX = mybir.AxisListType

D = 4096
DC = 32
T = 1024
NT = 8
S = 2048
FF = 11008
FC = 86
EPS = 1e-6
FPARTS = [15, 15, 14, 14, 14, 14]
NEG = -30000.0
SAME_ENGINE_SYNC = True
DEBUG = False
STOP = None
SCOPES = False

ENGS = ("pe", "act", "dve", "pool", "sp")


class Sem:
    def __init__(self, h):
        self.h = h
        self.n = 0


class Prog:
    def __init__(self, nc, sem_handles):
        self.nc = nc
        self.free_sems = list(sem_handles)
        self.q = {e: [] for e in ENGS}
        self.esem = {e: Sem(self.free_sems.pop()) for e in ENGS}
        self.bank_free = [None] * 8
        self.dma_sems = []
        self.sem_pool = []
        self.alloc_log = []
        self.bsc = None
        self.stage = "init"

    def newsem(self):
        if self.sem_pool:
            s_ = self.sem_pool.pop()
        else:
            s_ = Sem(self.free_sems.pop())
            self.dma_sems.append(s_)
        self.alloc_log.append(s_)
        return s_

    def mark(self):
        return len(self.alloc_log)

    def reset_to(self, m):
        while len(self.alloc_log) > m:
            self.sem_pool.append(self.alloc_log.pop())

    @staticmethod
    def flat(waits):
        out = []
        for w in waits:
            if w is None:
                continue
            if isinstance(w, list):
                out.extend(Prog.flat(w))
            else:
                out.append(w)
        return tuple(out)

    def barrier(self, extra=()):
        ws = [(self.esem[e], self.esem[e].n) for e in ENGS if self.esem[e].n > 0]
        ws += [(s_, s_.n) for s_ in self.dma_sems if s_.n > 0]
        ws += list(extra)
        bs = self.bsc
        return self.op("dve", lambda e: e.tensor_copy(out=bs, in_=bs), ws)

    def op(self, eng, fn, waits=(), signal=True):
        ws = Prog.flat(waits)
        if signal:
            sem = self.esem[eng]
            sem.n += 1
            self.q[eng].append((fn, ws, (sem, 1), self.stage))
            return (sem, sem.n)
        self.q[eng].append((fn, ws, None, self.stage))
        return None

    def dma(self, eng, out, in_, sem, waits=(), **kw):
        ws = Prog.flat(waits)
        sem.n += 16
        self.q[eng].append((lambda e: e.dma_start(out=out, in_=in_, **kw), ws, (sem, 16), self.stage))
        return (sem, sem.n)

    def replay(self, block):
        nc = self.nc
        engobj = {"pe": "tensor", "act": "scalar", "dve": "vector", "pool": "gpsimd", "sp": "sync"}

        def mk(ename):
            lst = self.q[ename]
            own = self.esem[ename]

            def body(e):
                seen = {}
                cur = [None, None]
                for fn, waits, sig, stg in lst:
                    if SCOPES and stg != cur[0]:
                        if cur[1] is not None:
                            cur[1].__exit__(None, None, None)
                        cur[0] = stg
                        cur[1] = nc.named_scope(stg)
                        cur[1].__enter__()
                    for (sem, v) in waits:
                        if sem is own and not SAME_ENGINE_SYNC:
                            continue
                        if seen.get(id(sem), 0) < v:
                            e.wait_ge(sem.h, v)
                            seen[id(sem)] = v
                    if fn is None:
                        continue
                    ins = fn(e)
                    if sig is not None:
                        ins.then_inc(sig[0].h, sig[1])
                if cur[1] is not None:
                    cur[1].__exit__(None, None, None)
            return body

        for ename in ENGS:
            getattr(block, engobj[ename])(mk(ename))


class Ring:
    def __init__(self, P, aps, init=None):
        self.aps = aps
        self.sems = [P.newsem() for _ in aps]
        self.free = [init] * len(aps)
        self.i = 0

    def done(self):
        return [(s_, s_.n) for s_ in self.sems if s_.n > 0]

    def next(self):
        i = self.i
        self.i = (self.i + 1) % len(self.aps)
        return i


def build_program():
    nc = bass.Bass("TRN2", target_bir_lowering=False)

    def din(name, shape, dt=F32):
        return nc.dram_tensor(name, list(shape), dt, kind="ExternalInput").ap()

    def dscr(name, shape, dt=F32):
        kind = "ExternalOutput" if DEBUG else "Internal"
        return nc.dram_tensor(name, list(shape), dt, kind=kind).ap()

    x_d = din("x", [T, D])
    xo_d = din("xo", [T, D])
    mem_d = din("mem", [256, D])
    posall_d = din("pos_all", [1, S], I32)
    gains_d = din("gains", [10, D])
    w1gu_d = din("w1gu", [FC, 128, 2 * DC * 128])
    w1d_d = din("w1d", [FF, D])
    w2gu_d = din("w2gu", [FC, 128, 2 * DC * 128])
    w2d_d = din("w2d", [FF, D])
    wq_d = din("w_q", [16, 128, DC * 128])
    wk_d = din("w_k", [16, 128, DC * 128])
    wv_d = din("w_v", [D, 2048])
    wcq_d = din("w_cq", [8, 128, DC * 128])
    wckr_d = din("w_ckr", [5, 128, DC * 128])
    gqa_d = din("g_q_a", [128, 8])
    gkva_d = din("g_kv_a", [128, 4])
    wqbn_d = din("w_qb_nope", [16, 128, 8 * 128])
    wqbp_d = din("w_qb_pe", [16, 128, 8 * 128])
    wkvbk_d = din("w_kvb_k", [16, 128, 4 * 128])
    wkvbv_d = din("w_kvb_v", [512, 2048])
    nabias_d = din("na_bias", [16, 16, 64, 768])
    wout_d = din("w_out", [D, D])
    wmq_d = din("w_mem_q", [4, 128, DC * 128])
    wmk_d = din("w_mem_k", [4, 128, DC * 128])
    wmv_d = din("w_mem_v", [D, 512])
    wmo_d = din("w_mem_o", [512, D])
    ropec_d = din("rope_c", [64, 2])
    out_d = nc.dram_tensor("out", [T, D], F32, kind="ExternalOutput").ap()

    FO = dscr("FO", [T, D])
    H = dscr("H", [T, D])
    QN = dscr("QN", [2048, T], BF16)
    KE = dscr("KE", [2048, 1536], BF16)
    VE = dscr("VE", [1536, 2048], BF16)
    CQ = dscr("CQ", [1024, T])
    CKV = dscr("CKV", [512, S])
    KR = dscr("KR", [2, 64, S])
    QNP = dscr("QNP", [2048, T], BF16)
    QPE = dscr("QPE", [16, 64, T], BF16)
    KNP = dscr("KNP", [2048, S], BF16)
    VM = dscr("VM", [S, 2048], BF16)

    ARENA = 211968
    es = contextlib.ExitStack()
    with es:
        arena = es.enter_context(nc.sbuf_tensor("arena", [128, ARENA // 2], BF16))
        psum = es.enter_context(nc.psum_tensor("psum", [128, 8, 512], F32))
        sems = [es.enter_context(nc.semaphore("s%d" % i)) for i in range(96)]
        P = Prog(nc, sems)

        def V(off, n, dt=BF16):
            if dt == BF16:
                assert off % 2 == 0
                return arena[:, off // 2: off // 2 + n]
            assert off % 4 == 0
            return arena[:, off // 2: off // 2 + 2 * n].bitcast(dt)

        AT = V(0, DC * T).rearrange("p (c t) -> p c t", c=DC)
        CB = 65536
        ones_bf = V(CB, 128)
        ident_bf = V(CB + 256, 128)
        ones_f = V(CB + 512, 128, F32)
        sc = V(CB + 1024, 64, F32)
        P.bsc = sc[:, 62:63]
        ropec = V(CB + 1288, 2, F32)
        gqa_sb = V(CB + 1296, 8, F32)
        gkva_sb = V(CB + 1328, 4, F32)
        ident_f = V(CB + 1536, 128, F32)
        R0 = CB + 4096

        misc_sem = P.newsem()

        def c_ones(e):
            return e.memset(ones_bf, 1.0)
        t_ob = P.op("pool", c_ones)
        t_of = P.op("pool", lambda e: e.memset(ones_f, 1.0))
        t_i1 = P.op("pool", lambda e: e.iota(ident_f.bitcast(I32), [[1, 128]], base=0, channel_multiplier=-1))
        t_i2 = P.op("dve", lambda e: e.tensor_copy(out=ident_f, in_=ident_f.bitcast(I32)), [t_i1])
        t_id = P.op("dve", lambda e: e.tensor_scalar(out=ident_bf, in0=ident_f, scalar1=0.0, scalar2=None,
                                                     op0=ALU.is_equal), [t_i2])
        t_c2 = P.dma("sp", ropec[0:64, :], ropec_d[:, :], misc_sem)
        t_c3 = P.dma("sp", gqa_sb, gqa_d[:, :], misc_sem)
        t_c4 = P.dma("sp", gkva_sb, gkva_d[:, :], misc_sem)
        const_tok = t_c4

        rr = {"ev": 0}

        def evac_engine():
            rr["ev"] += 1
            return "act" if rr["ev"] % 2 else "dve"

        def copy_op(eng, out, in_):
            if eng == "act":
                return lambda e: e.activation(out=out, in_=in_, func=AF.Copy)
            return lambda e: e.tensor_copy(out=out, in_=in_)

        def gemm_fm(rhs_fn, nK, ntb, n_fo, w_src_fn, slot_ring, evac_fn, sub=1, rhs_ready=None,
                    bank_sets=((0, 1), (2, 3), (4, 5), (6, 7)), mcols=None, ntok=512):
            u = gemm_fm.counter
            last_pe = None
            for fo in range(n_fo):
                si = slot_ring.next()
                slot = slot_ring.aps[si]
                wsrc = w_src_fn(fo)
                ld = P.dma("pool", slot[:, 0:wsrc.shape[1]], wsrc, slot_ring.sems[si], [slot_ring.free[si]])
                for tb in range(ntb):
                    bs = bank_sets[u % len(bank_sets)]
                    u += 1
                    for s_ in range(sub):
                        b = bs[s_]
                        lo, hi = (0, 128) if mcols is None else mcols[s_]
                        wsub = s_ if mcols is None else 0
                        for k in range(nK):
                            first, last = (k == 0), (k == nK - 1)
                            ws = []
                            if first:
                                ws = [ld, P.bank_free[b], rhs_ready]
                            base = (wsub * nK + k) * 128

                            def mm(e, b=b, base=base, lo=lo, hi=hi, k=k, tb=tb, first=first, last=last, slot=slot):
                                return e.matmul(psum[0:hi - lo, b, 0:ntok], lhsT=slot[:, base + lo: base + hi],
                                                rhs=rhs_fn(k, tb), start=first, stop=last)
                            tok = P.op("pe", mm, ws, signal=(last and s_ == sub - 1))
                    last_pe = tok
                    rel = evac_fn(fo, tb, bs[:sub], tok)
                    for b in bs[:sub]:
                        P.bank_free[b] = rel
                slot_ring.free[si] = last_pe
            gemm_fm.counter = u
            return last_pe
        gemm_fm.counter = 0

        def gemm_tm(lhs_fn, nK, nt, n_cb, w_src_fn, slot_ring, evac_fn, lhs_ready=None, ncol=512, tgs=2):
            g = gemm_tm.counter
            last_pe = None
            for cb in range(n_cb):
                si = slot_ring.next()
                slot = slot_ring.aps[si]
                wsrc = w_src_fn(cb)
                wdst = slot[:, 0:nK * ncol]
                if len(wsrc.shape) == 3:
                    wdst = wdst.rearrange("p (k n) -> p k n", k=nK)
                ld = P.dma("pool", wdst, wsrc, slot_ring.sems[si], [slot_ring.free[si]])
                for tg in range(nt // tgs):
                    bs = [((g % (8 // tgs)) * tgs + j) for j in range(tgs)]
                    g += 1
                    for k in range(nK):
                        first, last = (k == 0), (k == nK - 1)
                        for j in range(tgs):
                            b = bs[j]
                            t = tg * tgs + j
                            ws = [ld, P.bank_free[b], lhs_ready] if first else []

                            def mm(e, b=b, k=k, t=t, first=first, last=last, slot=slot):
                                return e.matmul(psum[:, b, 0:ncol], lhsT=lhs_fn(k, t),
                                                rhs=slot[:, k * ncol:(k + 1) * ncol], start=first, stop=last)
                            tok = P.op("pe", mm, ws, signal=(last and j == tgs - 1))
                    last_pe = tok
                    rel = evac_fn(cb, tg, bs, tok)
                    for b in bs:
                        P.bank_free[b] = rel
                slot_ring.free[si] = last_pe
            gemm_tm.counter = g
            return last_pe
        gemm_tm.counter = 0

        def fm_src(wd, nK):
            return lambda fo: wd[fo]

        class TMEvac:
            def __init__(self, dst, stage_ring, in_ring=None, tgs=2, ncol=512, dt=F32):
                self.dst, self.ring, self.in_ring, self.tgs, self.ncol, self.dt = dst, stage_ring, in_ring, tgs, ncol, dt
                self.store_tok = {}

            def __call__(self, cb, tg, bs, pe_tok, accumulate=False):
                tgs, ncol = self.tgs, self.ncol
                si = self.ring.next()
                st = self.ring.aps[si]
                dview = self.dst[tg * tgs * 128:(tg + 1) * tgs * 128, cb * ncol:(cb + 1) * ncol].rearrange(
                    "(t p) n -> p t n", p=128)
                src = psum[:, bs[0]:bs[0] + tgs, 0:ncol]
                stv = st.rearrange("p (t n) -> p t n", t=tgs)
                if accumulate:
                    ii = self.in_ring.next()
                    sin = self.in_ring.aps[ii].rearrange("p (t n) -> p t n", t=tgs)
                    ldt = P.dma("act", sin, dview, self.in_ring.sems[ii],
                                [self.in_ring.free[ii], self.store_tok.get((cb, tg))])
                    ct = P.op("dve", lambda e: e.tensor_tensor(out=stv, in0=src, in1=sin, op=ALU.add),
                              [pe_tok, ldt, self.ring.free[si]])
                    self.in_ring.free[ii] = ct
                else:
                    eng = evac_engine()
                    ct = P.op(eng, copy_op(eng, stv, src), [pe_tok, self.ring.free[si]])
                stt = P.dma("sp", dview, stv, self.ring.sems[si], [ct])
                self.ring.free[si] = stt
                self.store_tok[(cb, tg)] = stt
                self.last_store = stt
                return ct

        class FMEvac:
            def __init__(self, dst_fn, stage_ring, ntok=512):
                self.dst_fn, self.ring, self.ntok = dst_fn, stage_ring, ntok
                self.last_store = None
                self.stores = []

            def __call__(self, fo, tb, bs, pe_tok):
                si = self.ring.next()
                st = self.ring.aps[si]
                eng = evac_engine()
                ct = P.op(eng, copy_op(eng, st[:, 0:self.ntok], psum[:, bs[0], 0:self.ntok]), [pe_tok, self.ring.free[si]])
                stt = P.dma("sp", self.dst_fn(fo, tb), st[:, 0:self.ntok], self.ring.sems[si], [ct])
                self.ring.free[si] = stt
                self.last_store = stt
                self.stores.append(stt)
                return ct

        def carve(base, sizes):
            offs = []
            o = base
            for s_ in sizes:
                offs.append(o)
                o += s_
            assert o <= ARENA, (o, ARENA)
            return offs

        def load_gain(dst, idx, sem, waits=()):
            return P.dma("sp", dst, gains_d[idx:idx + 1, :].partition_broadcast(128), sem, waits)

        def rstd_from_ss(ss, tmp, rstd, n, waits, npart=128):
            a = P.op("dve", lambda e: e.tensor_scalar(out=tmp, in0=ss, scalar1=1.0 / n, scalar2=EPS,
                                                      op0=ALU.mult, op1=ALU.add), waits)
            b = P.op("act", lambda e: e.activation(out=tmp, in_=tmp, func=AF.Sqrt), [a])
            c = P.op("dve", lambda e: e.reciprocal(out=rstd, in_=tmp), [b])
            return c

        def transpose_into_AT(ubf, t, u_tok, at_free):
            last = None
            for grp in range(8):
                b = grp % 8
                pb = psum[:, b, :].bitcast(BF16)
                for j in range(4):
                    c = grp * 4 + j
                    ws = [u_tok, P.bank_free[b]] if j == 0 else []

                    def tr(e, c=c, j=j, pb=pb):
                        return e.transpose(pb[:, j * 128:(j + 1) * 128], ubf[:, c * 128:(c + 1) * 128], ident_bf)
                    tok = P.op("pe", tr, ws, signal=(j == 3))
                eng = evac_engine()
                dst = AT[:, grp * 4:(grp + 1) * 4, t * 128:(t + 1) * 128]
                srcv = pb[:, 0:512].rearrange("p (c n) -> p c n", c=4)
                ct = P.op(eng, copy_op(eng, dst, srcv), [tok, at_free])
                P.bank_free[b] = ct
                last = ct
            return last

        def epilogue(src, g_post, res_w, g_pre, final=False, at_free=None, src_ready=None, n_tiles=NT,
                     h_src=None, store_h=True):
            offs = carve(R0, [16384, 16384, 16384, 16384, 16384, 16384, 8192, 8192, 8192])
            fbuf = [V(offs[0], D, F32), V(offs[1], D, F32)]
            hbuf = [V(offs[2], D, F32), V(offs[3], D, F32)]
            gpost = V(offs[4], D, F32)
            gpre = V(offs[5], D, F32)
            ubf = [V(offs[6], D), V(offs[7], D)]
            junk = V(offs[8], D)
            gs = P.newsem()
            fs = [P.newsem(), P.newsem()]
            hs = [P.newsem(), P.newsem()]
            os_ = [P.newsem(), P.newsem()]
            tg1 = load_gain(gpost, g_post, gs, [at_free])
            tg2 = load_gain(gpre, g_pre, gs, [at_free])
            ffree = [at_free, at_free]
            hfree = [at_free, at_free]
            ufree = [at_free, at_free]
            hsrc = H if h_src is None else h_src
            stt = {}
            res = {}

            def p1(t):
                i = t % 2
                rows = slice(t * 128, (t + 1) * 128)
                lf = P.dma("sp", fbuf[i], src[rows, :], fs[i], [ffree[i], src_ready])
                lh = P.dma("sp", hbuf[i], hsrc[rows, :], hs[i], [hfree[i]])
                c0 = 8 * i
                ss1, tmp1, r1, ss2, tmp2, r2 = [sc[:, c0 + k:c0 + k + 1] for k in range(6)]
                a1 = P.op("act", lambda e, i=i, ss1=ss1: e.activation(out=junk, in_=fbuf[i], func=AF.Square,
                                                                      accum_out=ss1), [lf])
                rt1 = rstd_from_ss(ss1, tmp1, r1, D, [a1])
                y = P.op("dve", lambda e, i=i, r1=r1: e.scalar_tensor_tensor(
                    out=fbuf[i], in0=fbuf[i], scalar=r1, in1=gpost, op0=ALU.mult, op1=ALU.mult), [rt1, tg2])
                hh = P.op("dve", lambda e, i=i: e.scalar_tensor_tensor(
                    out=hbuf[i], in0=fbuf[i], scalar=float(res_w), in1=hbuf[i], op0=ALU.mult, op1=ALU.add), [y, lh])
                if not final:
                    ffree[i] = hh
                sth = None
                if not final and store_h:
                    sth = P.dma("sp", H[rows, :], hbuf[i], os_[i], [hh])
                a2 = P.op("act", lambda e, i=i, ss2=ss2: e.activation(out=junk, in_=hbuf[i], func=AF.Square,
                                                                      accum_out=ss2), [hh])
                stt[t] = (a2, sth, ss2, tmp2, r2)

            def p2(t):
                i = t % 2
                rows = slice(t * 128, (t + 1) * 128)
                a2, sth, ss2, tmp2, r2 = stt.pop(t)
                rt2 = rstd_from_ss(ss2, tmp2, r2, D, [a2])
                if final:
                    uo = P.op("dve", lambda e, i=i, r2=r2: e.scalar_tensor_tensor(
                        out=fbuf[i], in0=hbuf[i], scalar=r2, in1=gpre, op0=ALU.mult, op1=ALU.mult), [rt2])
                    sto = P.dma("sp", out_d[rows, :], fbuf[i], os_[i], [uo])
                    ffree[i] = sto
                    hfree[i] = uo
                    res["tok"] = sto
                else:
                    uo = P.op("dve", lambda e, i=i, r2=r2: e.scalar_tensor_tensor(
                        out=ubf[i], in0=hbuf[i], scalar=r2, in1=gpre, op0=ALU.mult, op1=ALU.mult), [rt2, ufree[i]])
                    hfree[i] = [sth, uo]
                    tr = transpose_into_AT(ubf[i], t, uo, at_free)
                    ufree[i] = tr
                    res["tok"] = tr

            for it in range(n_tiles + 1):
                if it < n_tiles:
                    p1(it)
                if it >= 1:
                    p2(it - 1)
            return res["tok"]

        def ffn(wgu_d, wd_d, at_ready):
            nPmax = max(FPARTS)
            offs = carve(R0, [nPmax * 2048, 16384, 16384, 16384, nPmax * 1024, nPmax * 1024,
                              2048, 2048, 4096, 4096, 4096, 4096, 4096, 4096])
            HT = V(offs[0], nPmax * T).rearrange("p (c t) -> p c t", c=nPmax)
            gu_ring = Ring(P, [V(offs[1], 8192), V(offs[2], 8192), V(offs[3], 8192)], at_ready)
            wd_ring = Ring(P, [V(offs[4], nPmax * 512), V(offs[5], nPmax * 512)], at_ready)
            sg = [V(offs[6], 512, F32), V(offs[7], 512, F32)]
            st_ring = Ring(P, [V(offs[8], 1024, F32), V(offs[9], 1024, F32)], at_ready)
            in_ring = Ring(P, [V(offs[10 + k], 1024, F32) for k in range(4)], at_ready)
            tmev = TMEvac(FO, st_ring, in_ring, tgs=2)
            sgfree = [None, None]
            c0 = 0
            ht_free = at_ready
            st = {"k": 0}
            last_b = None
            for pi, nP in enumerate(FPARTS):
                ht_toks = []

                def evacA(fo, tb, bs, pe_tok):
                    k = st["k"] % 2
                    st["k"] += 1
                    a = P.op("act", lambda e, k=k, bs=bs: e.activation(out=sg[k], in_=psum[:, bs[0], :], func=AF.Silu),
                             [pe_tok, sgfree[k]])
                    m = P.op("dve", lambda e, k=k, bs=bs, fo=fo, tb=tb: e.tensor_tensor(
                        out=HT[:, fo, tb * 512:(tb + 1) * 512], in0=sg[k], in1=psum[:, bs[1], :], op=ALU.mult),
                        [a, ht_free])
                    sgfree[k] = m
                    ht_toks.append(m)
                    return m

                base_stage = P.stage if pi == 0 else base_stage
                if SCOPES:
                    P.stage = base_stage + "_A%d" % pi
                gemm_fm(lambda k, tb: AT[:, k, tb * 512:(tb + 1) * 512], DC, 2, nP,
                        lambda fo, c0=c0: wgu_d[c0 + fo], gu_ring, evacA, sub=2, rhs_ready=at_ready)
                if SCOPES:
                    P.stage = base_stage + "_B%d" % pi
                ht_ready = ht_toks[-1]

                def evacB(cb, tg, bs, pe_tok, pi=pi):
                    return tmev(cb, tg, bs, pe_tok, accumulate=(pi > 0))

                last_b = gemm_tm(lambda k, t: HT[:, k, t * 128:(t + 1) * 128], nP, NT, 8,
                                 lambda cb, c0=c0, nP=nP: wd_d.rearrange("(c p) n -> p c n", p=128)[:, c0:c0 + nP, cb * 512:(cb + 1) * 512],
                                 wd_ring, evacB, lhs_ready=ht_ready, tgs=2)
                ht_free = last_b
                c0 += nP
            return last_b, tmev.last_store

        def prenorm_to_AT(src, gidx, bar, n_tiles=NT):
            offs = carve(R0, [16384, 16384, 16384, 8192, 8192, 8192])
            hb = [V(offs[0], D, F32), V(offs[1], D, F32)]
            gpre = V(offs[2], D, F32)
            ubf = [V(offs[3], D), V(offs[4], D)]
            junk = V(offs[5], D)
            gs = P.newsem()
            hs = [P.newsem(), P.newsem()]
            tg = load_gain(gpre, gidx, gs, [bar])
            hfree = [bar, bar]
            ufree = [bar, bar]
            for t in range(n_tiles):
                i = t % 2
                lh = P.dma("sp", hb[i], src[t * 128:(t + 1) * 128, :], hs[i], [hfree[i]])
                c0 = 8 * i
                ss, tmp, r = [sc[:, c0 + k:c0 + k + 1] for k in range(3)]
                a = P.op("act", lambda e, i=i, ss=ss: e.activation(out=junk, in_=hb[i], func=AF.Square, accum_out=ss),
                         [lh, bar])
                rt = rstd_from_ss(ss, tmp, r, D, [a])
                uo = P.op("dve", lambda e, i=i, r=r: e.scalar_tensor_tensor(
                    out=ubf[i], in0=hb[i], scalar=r, in1=gpre, op0=ALU.mult, op1=ALU.mult), [rt, tg, ufree[i]])
                hfree[i] = uo
                tr = transpose_into_AT(ubf[i], t, uo, bar)
                ufree[i] = tr

        stage_ctr = {"n": 0}

        def stage(fn, *a, name=None, **kw):
            stage_ctr["n"] += 1
            P.stage = "s%02d_%s" % (stage_ctr["n"], name or getattr(fn, "__name__", "x"))
            if STOP is not None and stage_ctr["n"] > STOP:
                return None
            bar = P.barrier()
            m = P.mark()
            r = fn(bar, *a, **kw)
            P.reset_to(m)
            return r

        t_const = P.op("dve", lambda e: e.tensor_copy(out=sc[:, 60:61], in_=sc[:, 60:61]),
                       [t_ob, t_of, t_id, const_tok])

        def inproj(bar, own):
            offs = carve(R0, [8192, 8192, 8192, 32768, 32768, 2048, 2048, 2048, 2048, 4096, 4096])
            fm_ring = Ring(P, [V(offs[0], 4096), V(offs[1], 4096), V(offs[2], 4096)], bar)
            tm_ring = Ring(P, [V(offs[3], 16384), V(offs[4], 16384)], bar)
            stf = [V(offs[5 + k], 512, F32) for k in range(4)]
            stb = [V(offs[5 + k], 512) for k in range(4)]
            fst_ring = Ring(P, stf, bar)
            bst_ring = Ring(P, stb, bar)
            bst_ring.sems = fst_ring.sems
            bst_ring.free = fst_ring.free

            class SharedRing:
                pass
            tst_ring = Ring(P, [V(offs[9], 1024), V(offs[10], 1024)], bar)

            def rhs_all(k, tb):
                return AT[:, k, tb * 512:(tb + 1) * 512]

            def fm(n_fo, wd, ring, dst_fn, ntb=2, **kw):
                ev = FMEvac(dst_fn, ring)
                gemm_fm(rhs_all, DC, ntb, n_fo, lambda fo: wd[fo], fm_ring, ev, rhs_ready=bar, **kw)
                fst_ring.i = bst_ring.i = ring.i

            coff = 0 if own else T
            fm(4, wckr_d, fst_ring, lambda fo, tb: CKV[fo * 128:(fo + 1) * 128, coff + tb * 512: coff + (tb + 1) * 512])

            class KREvac:
                def __call__(self, fo, tb, bs, pe_tok):
                    last = None
                    for s_ in range(2):
                        si = fst_ring.next()
                        bst_ring.i = fst_ring.i
                        st = fst_ring.aps[si]
                        eng = evac_engine()
                        ct = P.op(eng, copy_op(eng, st[0:64, :], psum[0:64, bs[s_], :]), [pe_tok, fst_ring.free[si]])
                        stt = P.dma("sp", KR[s_, :, coff + tb * 512: coff + (tb + 1) * 512], st[0:64, :],
                                    fst_ring.sems[si], [ct])
                        fst_ring.free[si] = stt
                        last = ct
                    return last
            gemm_fm(rhs_all, DC, 2, 1, lambda fo: wckr_d[4], fm_ring, KREvac(), sub=2, rhs_ready=bar,
                    mcols=[(0, 64), (64, 128)])
            if own:
                fm(8, wcq_d, fst_ring, lambda fo, tb: CQ[fo * 128:(fo + 1) * 128, tb * 512:(tb + 1) * 512])
                fm(16, wq_d, bst_ring, lambda fo, tb: QN[fo * 128:(fo + 1) * 128, tb * 512:(tb + 1) * 512])
                fm(16, wk_d, bst_ring, lambda fo, tb: KE[fo * 128:(fo + 1) * 128, 256 + tb * 512: 256 + (tb + 1) * 512])
                tmev = TMEvac(VE[256:1280, :], tst_ring, tgs=2, dt=BF16)
                gemm_tm(lambda k, t: AT[:, k, t * 128:(t + 1) * 128], DC, NT, 4,
                        lambda cb: wv_d.rearrange("(k p) n -> p k n", p=128)[:, :, cb * 512:(cb + 1) * 512],
                        tm_ring, tmev, lhs_ready=bar, tgs=2)
            else:
                class KHEvac:
                    def __call__(self, fo, tb, bs, pe_tok):
                        si = bst_ring.next()
                        fst_ring.i = bst_ring.i
                        st = bst_ring.aps[si]
                        eng = evac_engine()
                        ct = P.op(eng, copy_op(eng, st, psum[:, bs[0], :]), [pe_tok, bst_ring.free[si]])
                        P.dma("sp", KE[fo * 128:(fo + 1) * 128, 1280:1536], st[:, 0:256], bst_ring.sems[si], [ct])
                        stt = P.dma("sp", KE[fo * 128:(fo + 1) * 128, 0:256], st[:, 256:512], bst_ring.sems[si], [ct])
                        bst_ring.free[si] = stt
                        return ct
                gemm_fm(rhs_all, DC, 1, 16, lambda fo: wk_d[fo], fm_ring, KHEvac(), rhs_ready=bar)

                class VHEvac:
                    def __call__(self, cb, tg, bs, pe_tok):
                        si = tst_ring.next()
                        st = tst_ring.aps[si].rearrange("p (t n) -> p t n", t=2)
                        eng = evac_engine()
                        ct = P.op(eng, copy_op(eng, st, psum[:, bs[0]:bs[0] + 2, :]), [pe_tok, tst_ring.free[si]])
                        r0 = 1280 if tg == 0 else 0
                        stt = P.dma("sp", VE[r0:r0 + 256, cb * 512:(cb + 1) * 512].rearrange("(t p) n -> p t n", p=128),
                                    st, tst_ring.sems[si], [ct])
                        tst_ring.free[si] = stt
                        return ct
                gemm_tm(lambda k, t: AT[:, k, t * 128:(t + 1) * 128], DC, 4, 4,
                        lambda cb: wv_d.rearrange("(k p) n -> p k n", p=128)[:, :, cb * 512:(cb + 1) * 512],
                        tm_ring, VHEvac(), lhs_ready=bar, tgs=2)

        KPE_OFF = R0
        KPEs = V(KPE_OFF, S)
        ATT0 = R0 + 4096

        def mla_prep(bar):
            offs = carve(ATT0, [32768, 8192, 8192, 8192, 16384, 16384,
                                2048, 2048, 2048, 2048, 4096, 4096, 4096, 4096, 4096, 2048, 2048, 2048])
            big = V(offs[0], 8192, F32)
            cosT = V(offs[1], S, F32)
            ssinT = V(offs[2], S, F32)
            rbc = V(offs[3], S, F32)
            CQN = V(offs[4], 8 * T).rearrange("p (c t) -> p c t", c=8)
            CKVN = V(offs[5], 4 * S).rearrange("p (c t) -> p c t", c=4)
            stf = [V(offs[6 + k], 512, F32) for k in range(4)]
            stb = [V(offs[6 + k], 512) for k in range(4)]
            fst_ring = Ring(P, stf, bar)
            bst_ring = Ring(P, stb, bar)
            bst_ring.sems, bst_ring.free = fst_ring.sems, fst_ring.free
            fm_ring = Ring(P, [V(offs[10], 2048), V(offs[11], 2048), V(offs[12], 2048)], bar)
            tm_ring = Ring(P, [V(offs[13], 2048), V(offs[14], 2048)], bar)
            tst_ring = Ring(P, [V(offs[15], 1024), V(offs[16], 1024)], bar)
            sq = V(offs[17], 512, F32)
            ls = P.newsem()

            posi = big[0:64, 0:S].bitcast(I32)
            l0 = P.dma("sp", posi, posall_d[0:1, :].partition_broadcast(64), ls, [bar])
            ang = big[0:64, 2048:4096]
            tmpa = big[0:64, 4096:6144]
            TWO_PI = 2.0 * np.pi
            MAGIC = 12582912.0
            a0 = P.op("dve", lambda e: e.tensor_copy(out=ang, in_=posi), [l0])
            a1 = P.op("dve", lambda e: e.tensor_scalar(out=ang, in0=ang, scalar1=ropec[0:64, 0:1], scalar2=None,
                                                       op0=ALU.mult), [a0])

            def sin_table(dst, shift, prev):
                b0 = P.op("dve", lambda e: e.tensor_scalar(out=tmpa, in0=ang, scalar1=float(shift),
                                                           scalar2=float(1.0 / TWO_PI), op0=ALU.add, op1=ALU.mult), [prev])
                b1 = P.op("dve", lambda e: e.tensor_scalar(out=tmpa, in0=tmpa, scalar1=MAGIC, scalar2=None,
                                                           op0=ALU.add), [b0])
                b2 = P.op("dve", lambda e: e.tensor_scalar(out=tmpa, in0=tmpa, scalar1=MAGIC, scalar2=float(TWO_PI),
                                                           op0=ALU.subtract, op1=ALU.mult), [b1])
                b3 = P.op("dve", lambda e: e.scalar_tensor_tensor(out=tmpa, in0=ang, scalar=float(shift), in1=tmpa,
                                                                  op0=ALU.add, op1=ALU.subtract), [b2])
                b4 = P.op("dve", lambda e: e.tensor_scalar(out=tmpa, in0=tmpa, scalar1=3.1415925, scalar2=-3.1415925,
                                                           op0=ALU.min, op1=ALU.max), [b3])
                b5 = P.op("act", lambda e: e.activation(out=dst[0:64, :], in_=tmpa, func=AF.Sin), [b4])
                return b5
            s1 = sin_table(cosT, np.pi / 2.0, a1)
            s2 = sin_table(ssinT, 0.0, s1)
            s3 = P.op("dve", lambda e: e.tensor_scalar(out=ssinT[0:64, :], in0=ssinT[0:64, :], scalar1=ropec[0:64, 1:2],
                                                       scalar2=None, op0=ALU.mult), [s2])

            kr0 = big[0:64, 2048:4096]
            kr1 = big[0:64, 4096:6144]
            l1 = P.dma("sp", kr0, KR[0], ls, [s3])
            l2 = P.dma("sp", kr1, KR[1], ls, [s3])
            k0 = P.op("dve", lambda e: e.tensor_tensor(out=kr0, in0=kr0, in1=cosT[0:64, :], op=ALU.mult), [l1, l2])
            k1 = P.op("dve", lambda e: e.tensor_tensor(out=kr1, in0=kr1, in1=ssinT[0:64, :], op=ALU.mult), [k0])
            k2 = P.op("dve", lambda e: e.tensor_tensor(out=KPEs[0:64, :], in0=kr0, in1=kr1, op=ALU.add), [k1])
            k3 = P.dma("sp", KPEs[64:128, :], KPEs[0:64, :], ls, [k2])

            def fm_norm(src, nch, ntok, gcol, dst, prev):
                xin = big[:, 0:nch * ntok].rearrange("p (c t) -> p c t", c=nch)
                ld = P.dma("sp", xin, src.rearrange("(c p) t -> p c t", p=128), ls, [prev])
                last = ld
                for tb in range(ntok // 512):
                    b = tb % 2
                    for c in range(nch):
                        q = P.op("act", lambda e, c=c, tb=tb: e.activation(
                            out=sq, in_=xin[:, c, tb * 512:(tb + 1) * 512], func=AF.Square), [ld, last])
                        ws = [q, P.bank_free[b]] if c == 0 else [q]
                        last = P.op("pe", lambda e, b=b, c=c: e.matmul(psum[:, b, :], lhsT=ones_f, rhs=sq,
                                                                       start=(c == 0), stop=(c == nch - 1)), ws)
                    r0 = P.op("dve", lambda e, b=b, tb=tb: e.tensor_scalar(
                        out=rbc[:, tb * 512:(tb + 1) * 512], in0=psum[:, b, :], scalar1=1.0 / (nch * 128), scalar2=EPS,
                        op0=ALU.mult, op1=ALU.add), [last])
                    P.bank_free[b] = r0
                    r1 = P.op("act", lambda e, tb=tb: e.activation(out=rbc[:, tb * 512:(tb + 1) * 512],
                                                                   in_=rbc[:, tb * 512:(tb + 1) * 512], func=AF.Sqrt), [r0])
                    r2 = P.op("dve", lambda e, tb=tb: e.reciprocal(out=rbc[:, tb * 512:(tb + 1) * 512],
                                                                   in_=rbc[:, tb * 512:(tb + 1) * 512]), [r1])
                    last = r2
                for c in range(nch):
                    last = P.op("dve", lambda e, c=c: e.scalar_tensor_tensor(
                        out=dst[:, c, :], in0=xin[:, c, :], scalar=gcol[:, c:c + 1], in1=rbc[:, 0:ntok],
                        op0=ALU.mult, op1=ALU.mult), [last])
                return last
            n1 = fm_norm(CQ, 8, T, gqa_sb, CQN, k3)
            n2 = fm_norm(CKV, 4, S, gkva_sb, CKVN, n1)

            ev = FMEvac(lambda fo, tb: QNP[fo * 128:(fo + 1) * 128, tb * 512:(tb + 1) * 512], bst_ring)
            gemm_fm(lambda k, tb: CQN[:, k, tb * 512:(tb + 1) * 512], 8, 2, 16, lambda fo: wqbn_d[fo], fm_ring, ev,
                    rhs_ready=n2)
            fst_ring.i = bst_ring.i

            class QPEvac:
                def __call__(self, fo, tb, bs, pe_tok):
                    s0 = fst_ring.next()
                    s1_ = fst_ring.next()
                    bst_ring.i = fst_ring.i
                    ta, tb_ = fst_ring.aps[s0], fst_ring.aps[s1_]
                    cs = slice(tb * 512, (tb + 1) * 512)
                    o1 = P.op("dve", lambda e: e.tensor_tensor(out=ta[0:64, :], in0=psum[0:64, bs[0], :],
                                                               in1=cosT[0:64, cs], op=ALU.mult),
                              [pe_tok, fst_ring.free[s0]])
                    o2 = P.op("dve", lambda e: e.tensor_tensor(out=tb_[0:64, :], in0=psum[0:64, bs[1], :],
                                                               in1=ssinT[0:64, cs], op=ALU.mult),
                              [o1, fst_ring.free[s1_]])
                    ob = bst_ring.aps[s0]
                    o3 = P.op("dve", lambda e: e.tensor_tensor(out=ob[0:64, 0:512], in0=ta[0:64, :], in1=tb_[0:64, :],
                                                               op=ALU.add), [o2])
                    stt = P.dma("sp", QPE[fo, :, cs], ob[0:64, 0:512], fst_ring.sems[s0], [o3])
                    fst_ring.free[s0] = stt
                    fst_ring.free[s1_] = o3
                    return o2
            gemm_fm(lambda k, tb: CQN[:, k, tb * 512:(tb + 1) * 512], 8, 2, 16, lambda fo: wqbp_d[fo], fm_ring, QPEvac(),
                    sub=2, rhs_ready=n2, mcols=[(0, 64), (64, 128)])

            ev2 = FMEvac(lambda fo, tb: KNP[fo * 128:(fo + 1) * 128, tb * 512:(tb + 1) * 512], bst_ring)
            gemm_fm(lambda k, tb: CKVN[:, k, tb * 512:(tb + 1) * 512], 4, 4, 16, lambda fo: wkvbk_d[fo], fm_ring, ev2,
                    rhs_ready=n2)
            fst_ring.i = bst_ring.i
            tmev = TMEvac(VM, tst_ring, tgs=2, dt=BF16)
            gemm_tm(lambda k, t: CKVN[:, k, t * 128:(t + 1) * 128], 4, 16, 4,
                    lambda cb: wkvbv_d.rearrange("(k p) n -> p k n", p=128)[:, :, cb * 512:(cb + 1) * 512],
                    tm_ring, tmev, lhs_ready=n2, tgs=2)

        def attn_T(bar, heads, nkc, load_fn, kT_fn, q_fn, v_fn, out_fn, scale, bufs, pe_fn=None):
            pt_ring = Ring(P, bufs["pt"], bar)
            rec = bufs["rec"]
            st_banks = [0, 1, 2]
            unit = 0
            hfree = [bar, bar]
            recfree = bar
            ldts = {0: load_fn(0, 0, hfree[0])}
            for h in range(heads):
                hs_ = h % 2
                if h + 1 < heads:
                    ldts[h + 1] = load_fn(h + 1, (h + 1) % 2, hfree[(h + 1) % 2])
                ldt = ldts[h]
                last_pe = None
                for qb in range(2):
                    ob = 3 + 2 * (unit % 2)
                    sb_ = ob + 1
                    unit += 1
                    st_tok = {}

                    def emit_st(kc):
                        b = st_banks[kc % 3]
                        ws = [ldt, P.bank_free[b]]
                        kt_, qv_ = kT_fn(h, kc), q_fn(h, qb)
                        if pe_fn is None:
                            st_tok[kc] = P.op("pe", lambda e, b=b, kt_=kt_, qv_=qv_: e.matmul(
                                psum[:, b, :], lhsT=kt_, rhs=qv_, start=True, stop=True), ws)
                        else:
                            P.op("pe", lambda e, b=b, kt_=kt_, qv_=qv_: e.matmul(
                                psum[:, b, :], lhsT=kt_, rhs=qv_, start=True, stop=False), ws,
                                signal=False)
                            kp, qp = pe_fn(h, kc, qb)
                            st_tok[kc] = P.op("pe", lambda e, b=b, kp=kp, qp=qp: e.matmul(
                                psum[:, b, :], lhsT=kp, rhs=qp, start=False, stop=True))
                    for kc in range(min(3, nkc)):
                        emit_st(kc)
                    for kc in range(nkc):
                        b = st_banks[kc % 3]
                        pi = pt_ring.next()
                        pt = pt_ring.aps[pi]
                        ex = P.op("act", lambda e, b=b, pt=pt: e.activation(out=pt, in_=psum[:, b, :], func=AF.Exp,
                                                                            scale=float(scale)),
                                  [st_tok[kc], pt_ring.free[pi]])
                        P.bank_free[b] = ex
                        ws = [ex]
                        if kc == 0:
                            ws += [P.bank_free[ob], P.bank_free[sb_]]
                        vt_ = v_fn(h, kc)
                        P.op("pe", lambda e, kc=kc, pt=pt, ob=ob, vt_=vt_: e.matmul(
                            psum[:, ob, :], lhsT=vt_, rhs=pt, start=(kc == 0), stop=(kc == nkc - 1)), ws,
                            signal=False)
                        sm = P.op("pe", lambda e, kc=kc, pt=pt, sb_=sb_: e.matmul(
                            psum[:, sb_, :], lhsT=ones_bf, rhs=pt, start=(kc == 0), stop=(kc == nkc - 1)))
                        pt_ring.free[pi] = sm
                        last_pe = sm
                        if kc + 3 < nkc:
                            emit_st(kc + 3)
                    r0 = P.op("dve", lambda e, sb_=sb_: e.reciprocal(out=rec, in_=psum[:, sb_, :]), [last_pe, recfree])
                    o0 = P.op("dve", lambda e, ob=ob, h=h, qb=qb: e.tensor_tensor(
                        out=out_fn(h, qb), in0=psum[:, ob, :], in1=rec, op=ALU.mult), [r0])
                    recfree = o0
                    P.bank_free[ob] = o0
                    P.bank_free[sb_] = o0
                hfree[hs_] = last_pe

        def mla_attn(bar):
            offs = carve(ATT0, [12288, 12288, 1024, 1024, 1024, 1024, 2048])
            hb = [V(offs[0], 6144), V(offs[1], 6144)]
            bufs = {"pt": [V(offs[2 + k], 512) for k in range(4)], "rec": V(offs[6], 512, F32)}
            lsem = [P.newsem(), P.newsem()]

            def views(s_):
                b = hb[s_]
                return (b[:, 0:2048], b[:, 2048:4096].rearrange("p (c d) -> p c d", c=16), b[:, 4096:5120],
                        b[:, 5120:6144])

            def load(h, s_, free):
                knp, vv, qn, qp = views(s_)
                P.dma("sp", knp, KNP[h * 128:(h + 1) * 128, :], lsem[s_], [free])
                P.dma("sp", vv, VM[:, h * 128:(h + 1) * 128].rearrange("(c p) d -> p c d", p=128), lsem[s_], [free])
                P.dma("sp", qn, QNP[h * 128:(h + 1) * 128, :], lsem[s_], [free])
                return P.dma("sp", qp[0:64, :], QPE[h], lsem[s_], [free])

            attn_T(bar, 16, 16, load,
                   lambda h, kc: views(h % 2)[0][:, kc * 128:(kc + 1) * 128],
                   lambda h, qb: views(h % 2)[2][:, qb * 512:(qb + 1) * 512],
                   lambda h, kc: views(h % 2)[1][:, kc, :],
                   lambda h, qb: AT[:, 16 + h, qb * 512:(qb + 1) * 512],
                   192.0 ** -0.5, bufs,
                   pe_fn=lambda h, kc, qb: (KPEs[0:64, kc * 128:(kc + 1) * 128],
                                            views(h % 2)[3][0:64, qb * 512:(qb + 1) * 512]))

        def na_attn(bar):
            offs = carve(ATT0, [2048, 2048, 3072, 3072, 3072, 3072, 2816, 2816,
                                3072, 3072, 3072, 3072, 3072, 3072, 3072, 1536, 1536, 1536, 768, 768, 768])
            qh = [V(offs[0], 1024), V(offs[1], 1024)]
            kh = [V(offs[2], 1536), V(offs[3], 1536)]
            va = [V(offs[4], 1536).rearrange("p (c d) -> p c d", c=12), V(offs[5], 1536).rearrange("p (c d) -> p c d", c=12)]
            vb = [V(offs[6], 1408).rearrange("p (c d) -> p c d", c=11), V(offs[7], 1408).rearrange("p (c d) -> p c d", c=11)]
            bias_ring = Ring(P, [V(offs[8 + k], 768, F32) for k in range(4)], bar)
            sbuf_ = [V(offs[12 + k], 768, F32) for k in range(3)]
            pn = [V(offs[15 + k], 768) for k in range(3)]
            pts = [V(offs[18 + k], 384) for k in range(3)]
            lsem = [P.newsem(), P.newsem()]
            hfree = [bar, bar]
            sbfree = [bar] * 3
            pnfree = [bar] * 3
            ptfree = [bar] * 3
            scale = 128.0 ** -0.5
            units = [(hd, r) for hd in range(16) for r in range(16)]
            N = len(units)
            stt = {}
            ldts = {}

            def load_head(hd):
                s_ = hd % 2
                P.dma("sp", qh[s_], QN[hd * 128:(hd + 1) * 128, :], lsem[s_], [hfree[s_]])
                P.dma("sp", kh[s_], KE[hd * 128:(hd + 1) * 128, :], lsem[s_], [hfree[s_]])
                P.dma("sp", va[s_], VE[:, hd * 128:(hd + 1) * 128].rearrange("(c p) d -> p c d", p=128), lsem[s_],
                      [hfree[s_]])
                ldts[hd] = P.dma("sp", vb[s_], VE[64:64 + 1408, hd * 128:(hd + 1) * 128].rearrange(
                    "(c p) d -> p c d", p=128), lsem[s_], [hfree[s_]])

            def phase1(u):
                hd, r = units[u]
                s_ = hd % 2
                su = r if r < 4 else (r - 2 if r <= 12 else r - 4)
                k0 = su * 64
                i = u % 3
                ba = 2 * i
                bi = bias_ring.next()
                bt = bias_ring.aps[bi]
                lb = P.dma("sp", bt[0:64, :], nabias_d[hd, r], bias_ring.sems[bi], [bias_ring.free[bi]])
                qv_ = qh[s_][:, r * 64:(r + 1) * 64]
                k1_, k2_ = kh[s_][:, k0:k0 + 512], kh[s_][:, k0 + 512:k0 + 768]
                P.op("pe", lambda e, ba=ba, qv_=qv_, k1_=k1_: e.matmul(
                    psum[0:64, ba, :], lhsT=qv_, rhs=k1_, start=True, stop=True),
                    [ldts[hd], P.bank_free[ba], P.bank_free[ba + 1]], signal=False)
                s1 = P.op("pe", lambda e, ba=ba, qv_=qv_, k2_=k2_: e.matmul(
                    psum[0:64, ba + 1, 0:256], lhsT=qv_, rhs=k2_, start=True, stop=True))
                sflat = psum[0:64, ba:ba + 2, :].rearrange("p a n -> p (a n)")[:, 0:768]
                sbv = sbuf_[i][0:64, :]
                d0 = P.op("dve", lambda e, sflat=sflat, sbv=sbv, bt=bt: e.scalar_tensor_tensor(
                    out=sbv, in0=sflat, scalar=float(scale), in1=bt[0:64, :], op0=ALU.mult, op1=ALU.add),
                    [s1, lb, sbfree[i]])
                bias_ring.free[bi] = d0
                c0 = 16 + 4 * i
                nmx, rs_, rinv = [sc[0:64, c0 + k:c0 + k + 1] for k in range(3)]
                d1 = P.op("dve", lambda e, sbv=sbv, nmx=nmx: e.tensor_reduce(out=nmx, in_=sbv, op=ALU.max, axis=AX.X,
                                                                           negate=True), [d0])
                a0 = P.op("act", lambda e, sbv=sbv, nmx=nmx, rs_=rs_: e.activation(
                    out=sbv, in_=sbv, func=AF.Exp, bias=nmx, scale=1.0, accum_out=rs_), [d1])
                stt[u] = dict(a0=a0, sbv=sbv, rs_=rs_, rinv=rinv, su=su, i=i, d0=d0)

            def phase2(u):
                st_ = stt[u]
                i = st_["i"]
                ba = 2 * i
                sbv, rs_, rinv = st_["sbv"], st_["rs_"], st_["rinv"]
                d2 = P.op("dve", lambda e, rs_=rs_, rinv=rinv: e.reciprocal(out=rinv, in_=rs_), [st_["a0"]])
                pnv = pn[i][0:64, :]
                d3 = P.op("act", lambda e, sbv=sbv, pnv=pnv, rinv=rinv: e.activation(
                    out=pnv, in_=sbv, func=AF.Copy, scale=rinv), [d2, pnfree[i]])
                sbfree[i] = d3
                pbv = psum[:, ba + 1, :].bitcast(BF16)[:, 512:1024]
                for j in range(6):
                    ws = [d3, st_["d0"]] if j == 0 else []
                    tr = P.op("pe", lambda e, j=j, pnv=pnv, pbv=pbv: e.transpose(
                        pbv[:, j * 64:(j + 1) * 64], pnv[:, j * 128:(j + 1) * 128], ident_bf[0:64, 0:64]), ws,
                        signal=(j == 5))
                pnfree[i] = tr
                st_["tr"] = tr
                st_["pbv"] = pbv

            def phase3(u):
                hd, r = units[u]
                s_ = hd % 2
                st_ = stt.pop(u)
                su, i, pbv = st_["su"], st_["i"], st_["pbv"]
                ba = 2 * i
                ptv = pts[i]
                c1 = P.op("act", lambda e, ptv=ptv, pbv=pbv: e.activation(out=ptv, in_=pbv[:, 0:384], func=AF.Copy),
                          [st_["tr"], ptfree[i]])
                P.bank_free[ba] = c1
                P.bank_free[ba + 1] = c1
                ob = 6 + (r // 8)
                for j in range(6):
                    if su % 2 == 0:
                        vt = va[s_][:, su // 2 + j, :]
                    else:
                        vt = vb[s_][:, (su - 1) // 2 + j, :]
                    ws = [c1] if j else [c1, P.bank_free[ob]]
                    pv = P.op("pe", lambda e, vt=vt, ptv=ptv, j=j, ob=ob, r=r: e.matmul(
                        psum[:, ob, (r % 8) * 64:(r % 8 + 1) * 64], lhsT=vt, rhs=ptv[:, j * 64:(j + 1) * 64],
                        start=(j == 0), stop=(j == 5), skip_group_check=True), ws, signal=(j == 5))
                ptfree[i] = pv
                if r % 8 == 7:
                    eng = evac_engine()
                    ev = P.op(eng, copy_op(eng, AT[:, hd, (r // 8) * 512:(r // 8 + 1) * 512], psum[:, ob, :]), [pv])
                    P.bank_free[ob] = ev
                if r == 15:
                    hfree[s_] = pv

            load_head(0)
            load_head(1)
            for it in range(N + 3):
                if 0 <= it - 3 < N:
                    phase3(it - 3)
                if 0 <= it - 2 < N:
                    phase2(it - 2)
                if it < N:
                    hd, r = units[it]
                    if r == 3 and hd >= 1 and hd + 1 < 16:
                        load_head(hd + 1)
                    phase1(it)

        def out_proj(bar):
            offs = carve(R0, [32768, 32768, 4096, 4096])
            tm_ring = Ring(P, [V(offs[0], 16384), V(offs[1], 16384)], bar)
            st_ring = Ring(P, [V(offs[2], 1024, F32), V(offs[3], 1024, F32)], bar)
            tmev = TMEvac(FO, st_ring, tgs=2)
            gemm_tm(lambda k, t: AT[:, k, t * 128:(t + 1) * 128], DC, NT, 8,
                    lambda cb: wout_d.rearrange("(k p) n -> p k n", p=128)[:, :, cb * 512:(cb + 1) * 512],
                    tm_ring, tmev, lhs_ready=bar, tgs=2)

        MEMT_OFF = R0
        KM_OFF = R0 + 16384
        VMM_OFF = KM_OFF + 2048
        QM_OFF = VMM_OFF + 2048
        OM_OFF = QM_OFF + 8192
        MEM_FREE = OM_OFF + 8192
        KMs = V(KM_OFF, 1024).rearrange("p (h k) -> p h k", h=4)
        VMMs = V(VMM_OFF, 1024).rearrange("p (c n) -> p c n", c=2)
        QMs = V(QM_OFF, 4096).rearrange("p (h t) -> p h t", h=4)
        OMs = V(OM_OFF, 4096).rearrange("p (h t) -> p h t", h=4)
        MEMT = V(MEMT_OFF, 8192).rearrange("p (c t) -> p c t", c=DC)

        def mem_prep(bar):
            offs = carve(MEM_FREE, [8192, 8192, 8192, 16384, 16384, 16384, 8192, 8192, 8192])
            fm_ring = Ring(P, [V(offs[0], 4096), V(offs[1], 4096), V(offs[2], 4096)], bar)
            hb = [V(offs[3], D, F32), V(offs[4], D, F32)]
            gpre = V(offs[5], D, F32)
            ubf = [V(offs[6], D), V(offs[7], D)]
            junk = V(offs[8], D)
            ls = P.newsem()
            gs = P.newsem()

            def qev(fo, tb, bs, pe_tok):
                eng = evac_engine()
                return P.op(eng, copy_op(eng, QMs[:, fo, tb * 512:(tb + 1) * 512], psum[:, bs[0], :]), [pe_tok, bar])
            gemm_fm(lambda k, tb: AT[:, k, tb * 512:(tb + 1) * 512], DC, 2, 4, lambda fo: wmq_d[fo], fm_ring, qev,
                    rhs_ready=bar)
            tg = load_gain(gpre, 9, gs, [bar])
            last = None
            for t in range(2):
                lh = P.dma("sp", hb[t], mem_d[t * 128:(t + 1) * 128, :], ls, [bar])
                ss, tmp, r = [sc[:, 8 * t + k:8 * t + k + 1] for k in range(3)]
                a = P.op("act", lambda e, t=t, ss=ss: e.activation(out=junk, in_=hb[t], func=AF.Square, accum_out=ss),
                         [lh])
                rt = rstd_from_ss(ss, tmp, r, D, [a])
                uo = P.op("dve", lambda e, t=t, r=r: e.scalar_tensor_tensor(
                    out=ubf[t], in0=hb[t], scalar=r, in1=gpre, op0=ALU.mult, op1=ALU.mult), [rt, tg])
                for grp in range(8):
                    b = grp % 8
                    pb = psum[:, b, :].bitcast(BF16)
                    for j in range(4):
                        c = grp * 4 + j
                        ws = [uo, P.bank_free[b]] if j == 0 else []
                        tok = P.op("pe", lambda e, c=c, j=j, pb=pb, t=t: e.transpose(
                            pb[:, j * 128:(j + 1) * 128], ubf[t][:, c * 128:(c + 1) * 128], ident_bf), ws,
                            signal=(j == 3))
                    eng = evac_engine()
                    ct = P.op(eng, copy_op(eng, MEMT[:, grp * 4:(grp + 1) * 4, t * 128:(t + 1) * 128],
                                           pb[:, 0:512].rearrange("p (c n) -> p c n", c=4)), [tok, bar])
                    P.bank_free[b] = ct
                    last = ct
            memt_ready = P.barrier()

            def kev(fo, tb, bs, pe_tok):
                eng = evac_engine()
                return P.op(eng, copy_op(eng, KMs[:, fo, :], psum[:, bs[0], 0:256]), [pe_tok])
            gemm_fm(lambda k, tb: MEMT[:, k, :], DC, 1, 4, lambda fo: wmk_d[fo], fm_ring, kev, rhs_ready=memt_ready,
                    ntok=256)
            wvs = V(offs[3], 16384)
            lw = P.dma("pool", wvs.rearrange("p (k n) -> p k n", k=DC), wmv_d.rearrange("(k p) n -> p k n", p=128), ls,
                       [memt_ready])
            for t in range(2):
                b = 4 + t
                for k in range(DC):
                    ws = [lw, P.bank_free[b]] if k == 0 else []
                    tok = P.op("pe", lambda e, k=k, t=t, b=b: e.matmul(
                        psum[:, b, :], lhsT=MEMT[:, k, t * 128:(t + 1) * 128], rhs=wvs[:, k * 512:(k + 1) * 512],
                        start=(k == 0), stop=(k == DC - 1)), ws, signal=(k == DC - 1))
                eng = evac_engine()
                ct = P.op(eng, copy_op(eng, VMMs[:, t, :], psum[:, b, :]), [tok])
                P.bank_free[b] = ct

        def mem_attn(bar):
            offs = carve(MEM_FREE, [1024, 1024, 1024, 1024, 2048])
            bufs = {"pt": [V(offs[k], 512) for k in range(4)], "rec": V(offs[4], 512, F32)}
            attn_T(bar, 4, 2, lambda h, s_, free: bar,
                   lambda h, kc: KMs[:, h, kc * 128:(kc + 1) * 128],
                   lambda h, qb: QMs[:, h, qb * 512:(qb + 1) * 512],
                   lambda h, kc: VMMs[:, kc, h * 128:(h + 1) * 128],
                   lambda h, qb: OMs[:, h, qb * 512:(qb + 1) * 512], 128.0 ** -0.5, bufs)

        def mem_out(bar):
            offs = carve(MEM_FREE, [4096, 4096, 4096, 4096])
            tm_ring = Ring(P, [V(offs[0], 2048), V(offs[1], 2048)], bar)
            st_ring = Ring(P, [V(offs[2], 1024, F32), V(offs[3], 1024, F32)], bar)
            tmev = TMEvac(FO, st_ring, tgs=2)
            gemm_tm(lambda k, t: OMs[:, k, t * 128:(t + 1) * 128], 4, NT, 8,
                    lambda cb: wmo_d.rearrange("(k p) n -> p k n", p=128)[:, :, cb * 512:(cb + 1) * 512],
                    tm_ring, tmev, lhs_ready=bar, tgs=2)

        stage(lambda bar: prenorm_to_AT(xo_d, 0, [bar, t_const]), name="prenorm_o")
        stage(lambda bar: ffn(w1gu_d, w1d_d, bar), name="ffn1_o")
        stage(lambda bar: epilogue(FO, 1, 0.5, 2, at_free=bar, src_ready=bar, h_src=xo_d, store_h=False), name="epi1_o")
        stage(inproj, False)
        stage(lambda bar: prenorm_to_AT(x_d, 0, bar), name="prenorm")
        stage(lambda bar: ffn(w1gu_d, w1d_d, bar), name="ffn1")
        stage(lambda bar: epilogue(FO, 1, 0.5, 2, at_free=bar, src_ready=bar, h_src=x_d), name="epi1")
        stage(inproj, True)
        stage(mla_prep)
        stage(mla_attn)
        stage(na_attn)
        stage(out_proj)
        stage(lambda bar: epilogue(FO, 3, 1.0, 4, at_free=bar, src_ready=bar), name="epi2")
        stage(mem_prep)
        stage(mem_attn)
        stage(mem_out)
        stage(lambda bar: epilogue(FO, 5, 1.0, 6, at_free=bar, src_ready=bar), name="epi3")
        stage(lambda bar: ffn(w2gu_d, w2d_d, bar), name="ffn2")
        stage(lambda bar: epilogue(FO, 7, 0.5, 8, final=True, at_free=bar, src_ready=bar), name="epi4")
        if DEBUG:
            ATD = dscr("ATD", [D, T], BF16)
            dsem = P.newsem()
            dbar = P.barrier()
            P.dma("sp", ATD.rearrange("(c p) t -> p c t", p=128), AT, dsem, [dbar])
        fbar = P.barrier()
        P.q["sp"].append((None, (fbar,), None, "end"))

        with nc.Block() as block:
            P.replay(block)
    return nc


def _fm(W):
    K, N = W.shape
    return np.ascontiguousarray(W.reshape(K // 128, 128, N // 128, 128).transpose(2, 1, 0, 3)).reshape(N // 128, 128, -1)


def _tm(W):
    K, N = W.shape
    return np.ascontiguousarray(W.reshape(K // 128, 128, N // 512, 512).transpose(2, 1, 0, 3)).reshape(N // 512, 128, -1)


def _gu(Wg, Wu):
    a = Wg.reshape(DC, 128, FC, 128).transpose(2, 1, 0, 3)
    b = Wu.reshape(DC, 128, FC, 128).transpose(2, 1, 0, 3)
    return np.ascontiguousarray(np.stack([a, b], axis=2)).reshape(FC, 128, 2 * DC * 128)


def _na_bias(rpb, h):
    out = np.full((16, 16, 64, 12, 64), NEG, dtype=np.float32)
    q = np.arange(64)[:, None]
    kc = np.arange(64)[None, :]
    cstart = np.clip(q - 8, 0, 48)
    col_ok = (kc >= cstart) & (kc < cstart + 16)
    col_idx = np.clip(kc - q + 15, 0, 30)
    for r in range(16):
        su = r if r < 4 else (r - 2 if r <= 12 else r - 4)
        R = 16 * h + r
        rs = min(max(R - 4, 0), 24)
        for i in range(12):
            j = su + i
            if j < 4:
                g = 16 * (1 - h) + 12 + j
            elif j < 20:
                g = 16 * h + j - 4
            else:
                g = 16 * (1 - h) + j - 20
            if not (rs <= g < rs + 8):
                continue
            ri = g - R + 7
            vals = rpb[:, ri][:, col_idx]
            out[:, r, :, i, :] = np.where(col_ok[None], vals, np.float32(NEG))
    return out.reshape(16, 16, 64, 768)


_CACHE = {}


def kernel(x, mem, positions, ffn1_w_gate, ffn1_w_up, ffn1_w_down, g_ffn1, w_in, g_q_a, w_q_b, g_kv_a, w_kv_b,
           na_rpb, w_out, g_mix, g_mem_in, w_mem_q, w_mem_kv, w_mem_o, g_mem_attn,
           ffn2_w_gate, ffn2_w_up, ffn2_w_down, g_ffn2, g_final):
    f32 = lambda a: np.asarray(a, dtype=np.float32)
    x = f32(x); mem = f32(mem); positions = np.asarray(positions).astype(np.int32)
    w_in = f32(w_in)[0]
    shared = {}
    shared["gains"] = np.ascontiguousarray(np.stack([
        f32(g_ffn1)[0, 0], f32(g_ffn1)[0, 1], f32(g_mix)[0, 0], f32(g_mix)[0, 1], f32(g_mem_attn)[0, 0],
        f32(g_mem_attn)[0, 1], f32(g_ffn2)[0, 0], f32(g_ffn2)[0, 1], f32(g_final)[0], f32(g_mem_in)[0]]))
    shared["w1gu"] = _gu(f32(ffn1_w_gate)[0], f32(ffn1_w_up)[0])
    shared["w1d"] = np.ascontiguousarray(f32(ffn1_w_down)[0])
    shared["w2gu"] = _gu(f32(ffn2_w_gate)[0], f32(ffn2_w_up)[0])
    shared["w2d"] = np.ascontiguousarray(f32(ffn2_w_down)[0])
    shared["w_q"] = _fm(w_in[:, 0:2048])
    shared["w_k"] = _fm(w_in[:, 2048:4096])
    shared["w_v"] = np.ascontiguousarray(w_in[:, 4096:6144])
    shared["w_cq"] = _fm(w_in[:, 6144:7168])
    kr = w_in[:, 7680:7744]
    shared["w_ckr"] = _fm(np.concatenate([w_in[:, 7168:7680], kr, kr[:, 32:], kr[:, :32]], axis=1))
    shared["g_q_a"] = np.ascontiguousarray(f32(g_q_a)[0].reshape(8, 128).T)
    shared["g_kv_a"] = np.ascontiguousarray(f32(g_kv_a)[0].reshape(4, 128).T)
    wqb = f32(w_q_b)[0].reshape(1024, 16, 192)
    shared["w_qb_nope"] = _fm(np.ascontiguousarray(wqb[:, :, :128]).reshape(1024, 2048))
    pe = wqb[:, :, 128:]
    shared["w_qb_pe"] = _fm(np.ascontiguousarray(np.concatenate([pe, pe[:, :, 32:], pe[:, :, :32]], axis=2)).reshape(1024, 2048))
    wkvb = f32(w_kv_b)[0].reshape(512, 16, 256)
    shared["w_kvb_k"] = _fm(np.ascontiguousarray(wkvb[:, :, :128]).reshape(512, 2048))
    shared["w_kvb_v"] = np.ascontiguousarray(wkvb[:, :, 128:]).reshape(512, 2048)
    shared["w_out"] = np.ascontiguousarray(f32(w_out)[0])
    shared["w_mem_q"] = _fm(f32(w_mem_q)[0])
    wmkv = f32(w_mem_kv)[0].reshape(D, 4, 256)
    shared["w_mem_k"] = _fm(np.ascontiguousarray(wmkv[:, :, :128]).reshape(D, 512))
    shared["w_mem_v"] = np.ascontiguousarray(wmkv[:, :, 128:]).reshape(D, 512)
    shared["w_mem_o"] = np.ascontiguousarray(f32(w_mem_o)[0])
    half = 32
    inv_freq = (1.0 / (np.float32(10000.0) ** (np.arange(half, dtype=np.float32) / np.float32(half)))).astype(np.float32)
    rc = np.zeros((64, 2), np.float32)
    rc[:, 0] = np.tile(inv_freq, 2)
    rc[:32, 1] = -1.0
    rc[32:, 1] = 1.0
    shared["rope_c"] = rc
    rpb = f32(na_rpb)[0]
    biases = [_na_bias(rpb, 0), _na_bias(rpb, 1)]
    perm = np.concatenate([np.arange(0, 256), np.arange(768, 1024), np.arange(256, 768)])
    in_maps = []
    for c in range(8):
        b, h = c // 2, c % 2
        own = slice(h * T, (h + 1) * T)
        oth = slice((1 - h) * T, (2 - h) * T)
        m = dict(shared)
        m["x"] = np.ascontiguousarray(x[b, own])
        m["xo"] = np.ascontiguousarray(x[b, oth][perm])
        m["mem"] = np.ascontiguousarray(mem[b])
        m["pos_all"] = np.ascontiguousarray(np.concatenate([positions[b, own], positions[b, oth][perm]])[None, :])
        m["na_bias"] = biases[h]
        in_maps.append(m)
    if _CACHE.get("prep_only"):
        return in_maps
    if "nc" not in _CACHE:
        _CACHE["nc"] = build_program()
    res = run_bass_kernel_spmd(_CACHE["nc"], in_maps, core_ids=list(range(8)))
    out = np.empty((4, S, D), np.float32)
    for c in range(8):
        b, h = c // 2, c % 2
        out[b, h * T:(h + 1) * T] = np.asarray(res.results[c]["out"], dtype=np.float32)
    return out
```
